# Optimizing a Trainium2 kernel written in Bass

```python
import math
import jax, jax.numpy as jnp
from jax import lax
import numpy as np

D_MODEL = 2048
BATCH = 16
SEQ = 256
DEPTH = 2
DEC_BATCH = 2
DEC_SEQ = 1024
PAST_LEN = 512

GRID_W = 64
HEAD_DIM = 128
MLA_HEADS = D_MODEL // (2 * HEAD_DIM)
MLA_Q_LORA = 3 * D_MODEL // 8
MLA_KV_LORA = D_MODEL // 4
MLA_NOPE_DIM = 128
MLA_ROPE_DIM = 64
MLA_V_DIM = 128
GQA_HEADS = D_MODEL // (2 * HEAD_DIM)
GQA_KV_HEADS = 2
DIFF_HEADS = D_MODEL // (2 * HEAD_DIM)
D_FF = ((8 * D_MODEL // 3 + 255) // 256) * 256
CONV_W = 3
ROPE_BASE = 10000.0
EPS = 1e-6
Q_BLOCK = 128
N_EVEN = (DEPTH + 1) // 2
N_ODD = DEPTH // 2
N_IN_AB = MLA_Q_LORA + MLA_KV_LORA + MLA_ROPE_DIM + (GQA_HEADS + 2 * GQA_KV_HEADS) * HEAD_DIM
N_IN_C = DIFF_HEADS * (2 * HEAD_DIM + 2 * HEAD_DIM + 2 * HEAD_DIM)
MLA_SCALE = (MLA_NOPE_DIM + MLA_ROPE_DIM) ** -0.5
HEAD_SCALE = HEAD_DIM ** -0.5

kernel_name = "hybrid_mla_gqa_diffattn_convffn_prefix_dit_step"


def rmsnorm(x, g):
    xf = x.astype(jnp.float32)
    y = xf * lax.rsqrt(jnp.mean(xf * xf, axis=-1, keepdims=True) + EPS)
    return (y * g.astype(jnp.float32)).astype(x.dtype)


def rope_tables(n_tokens, rot_dim):
    n_rows = n_tokens // GRID_W
    row = jnp.repeat(jnp.arange(n_rows), GRID_W).astype(jnp.float32)
    col = jnp.tile(jnp.arange(GRID_W), n_rows).astype(jnp.float32)
    n_freq = rot_dim // 4
    freqs = ROPE_BASE ** (-jnp.arange(n_freq, dtype=jnp.float32) / n_freq)
    ang = jnp.concatenate([row[:, None] * freqs, col[:, None] * freqs], axis=-1)
    return jnp.cos(ang), jnp.sin(ang)


def apply_rope(x, cos, sin):
    half = x.shape[-1] // 2
    x1, x2 = x[..., :half], x[..., half:]
    c = cos[None, :, None, :].astype(x.dtype)
    s = sin[None, :, None, :].astype(x.dtype)
    return jnp.concatenate([x1 * c - x2 * s, x1 * s + x2 * c], axis=-1)


def _to_blocks(q):
    b, t = q.shape[:2]
    return jnp.moveaxis(q.reshape(b, t // Q_BLOCK, Q_BLOCK, *q.shape[2:]), 1, 0)


def _from_blocks(o):
    o = jnp.moveaxis(o, 0, 1)
    return o.reshape(o.shape[0], -1, *o.shape[3:])


def softmax_attention(q, k, v, scale):
    b, t, h, dk = q.shape
    hkv = k.shape[2]
    qg = q.reshape(b, t, hkv, h // hkv, dk)

    def block(qb):
        s = jnp.einsum("bqhgd,bkhd->bhgqk", qb, k).astype(jnp.float32) * scale
        p = jax.nn.softmax(s, axis=-1).astype(v.dtype)
        return jnp.einsum("bhgqk,bkhd->bqhgd", p, v)

    o = _from_blocks(lax.map(block, _to_blocks(qg)))
    return o.reshape(b, t, h, v.shape[-1])


def diff_attention(q, k, v, lam, scale):
    def block(qb):
        s = jnp.einsum("bqhjd,bkhjd->bhjqk", qb, k).astype(jnp.float32) * scale
        p = jax.nn.softmax(s, axis=-1)
        a = (p[:, :, 0] - lam * p[:, :, 1]).astype(v.dtype)
        return jnp.einsum("bhqk,bkhd->bqhd", a, v)

    return _from_blocks(lax.map(block, _to_blocks(q)))


def mixer_ab(h, rope, ctx, w_in, g_q, w_q_up, g_kv, w_kv_up, g_qn, g_kn, w_out):
    b, t, _ = h.shape
    proj = h @ w_in
    i0 = MLA_Q_LORA
    i1 = i0 + MLA_KV_LORA
    i2 = i1 + MLA_ROPE_DIM
    i3 = i2 + GQA_HEADS * HEAD_DIM
    i4 = i3 + GQA_KV_HEADS * HEAD_DIM
    q_lat, ckv, k_rope = proj[..., :i0], proj[..., i0:i1], proj[..., i1:i2]
    qb = rmsnorm(proj[..., i2:i3].reshape(b, t, GQA_HEADS, HEAD_DIM), g_qn)
    kb = rmsnorm(proj[..., i3:i4].reshape(b, t, GQA_KV_HEADS, HEAD_DIM), g_kn)
    vb = proj[..., i4:].reshape(b, t, GQA_KV_HEADS, HEAD_DIM)
    qa = (rmsnorm(q_lat, g_q) @ w_q_up).reshape(b, t, MLA_HEADS, MLA_NOPE_DIM + MLA_ROPE_DIM)
    qa_nope, qa_rope = qa[..., :MLA_NOPE_DIM], qa[..., MLA_NOPE_DIM:]
    ckv = rmsnorm(ckv, g_kv)
    if rope is not None:
        cos_r, sin_r, cos_h, sin_h = rope
        qa_rope = apply_rope(qa_rope, cos_r, sin_r)
        k_rope = apply_rope(k_rope[:, :, None, :], cos_r, sin_r)[:, :, 0]
        qb = apply_rope(qb, cos_h, sin_h)
        kb = apply_rope(kb, cos_h, sin_h)
    new_ctx = (ckv, k_rope, kb, vb)
    if ctx is None:
        ckv_all, krope_all, kb_all, vb_all = new_ctx
    else:
        ckv_all = jnp.concatenate([ctx[0], ckv], axis=1)
        krope_all = jnp.concatenate([ctx[1], k_rope], axis=1)
        kb_all = jnp.concatenate([ctx[2], kb], axis=1)
        vb_all = jnp.concatenate([ctx[3], vb], axis=1)
    s = ckv_all.shape[1]
    kv = (ckv_all @ w_kv_up).reshape(b, s, MLA_HEADS, MLA_NOPE_DIM + MLA_V_DIM)
    ka = jnp.concatenate(
        [kv[..., :MLA_NOPE_DIM], jnp.broadcast_to(krope_all[:, :, None, :], (b, s, MLA_HEADS, MLA_ROPE_DIM))],
        axis=-1)
    va = kv[..., MLA_NOPE_DIM:]
    out_a = softmax_attention(jnp.concatenate([qa_nope, qa_rope], axis=-1), ka, va, MLA_SCALE)
    out_b = softmax_attention(qb, kb_all, vb_all, HEAD_SCALE)
    out = jnp.concatenate([out_a.reshape(b, t, -1), out_b.reshape(b, t, -1)], axis=-1) @ w_out
    return out, new_ctx


def mixer_c(h, rope, ctx, layer_idx, w_in, lq1, lk1, lq2, lk2, g_sub, w_out):
    b, t, _ = h.shape
    proj = h @ w_in
    nq = DIFF_HEADS * 2 * HEAD_DIM
    q = proj[..., :nq].reshape(b, t, DIFF_HEADS * 2, HEAD_DIM)
    k = proj[..., nq:2 * nq].reshape(b, t, DIFF_HEADS * 2, HEAD_DIM)
    v = proj[..., 2 * nq:].reshape(b, t, DIFF_HEADS, 2 * HEAD_DIM)
    if rope is not None:
        _, _, cos_h, sin_h = rope
        q = apply_rope(q, cos_h, sin_h)
        k = apply_rope(k, cos_h, sin_h)
    q = q.reshape(b, t, DIFF_HEADS, 2, HEAD_DIM)
    k = k.reshape(b, t, DIFF_HEADS, 2, HEAD_DIM)
    new_ctx = (k, v)
    if ctx is None:
        k_all, v_all = k, v
    else:
        k_all = jnp.concatenate([ctx[0], k], axis=1)
        v_all = jnp.concatenate([ctx[1], v], axis=1)
    lambda_init = 0.8 - 0.6 * math.exp(-0.3 * layer_idx)
    lam = (jnp.exp(jnp.sum((lq1 * lk1).astype(jnp.float32)))
           - jnp.exp(jnp.sum((lq2 * lk2).astype(jnp.float32))) + lambda_init)
    o = diff_attention(q, k_all, v_all, lam, HEAD_SCALE)
    o = rmsnorm(o, g_sub) * (1.0 - lambda_init)
    return o.reshape(b, t, -1) @ w_out, new_ctx


def conv_ffn(h, w_up, w_conv, b_conv, w_down):
    u = h @ w_up
    up = jnp.pad(u, ((0, 0), (1, 1), (0, 0)))
    u = up[:, :-2] * w_conv[0] + up[:, 1:-1] * w_conv[1] + up[:, 2:] * w_conv[2] + b_conv
    gate, val = u[..., :D_FF], u[..., D_FF:]
    return (jax.nn.silu(gate) * val) @ w_down


def _trunk(x, cond, rope, caches, w_ada, b_ada, g_norm_mix, g_norm_ffn,
           w_in_ab, g_mla_q, w_mla_q_up, g_mla_kv, w_mla_kv_up, g_gqa_q, g_gqa_k, w_out_ab,
           w_in_c, lambda_q1, lambda_k1, lambda_q2, lambda_k2, g_diff_sub, w_out_c,
           w_ffn_up, w_ffn_conv, b_ffn_conv, w_ffn_down, g_final):
    ab_ctx, c_ctx = [], []
    for l in range(DEPTH):
        mod = jax.nn.silu(cond) @ w_ada[l] + b_ada[l]
        sh1, sc1, gt1, sh2, sc2, gt2 = jnp.split(mod[:, None, :], 6, axis=-1)
        h = rmsnorm(x, g_norm_mix[l]) * (1 + sc1) + sh1
        i = l // 2
        if l % 2 == 0:
            ctx = None if caches is None else tuple(cc[:, i] for cc in caches[:4])
            m, new = mixer_ab(h, rope, ctx, w_in_ab[i], g_mla_q[i], w_mla_q_up[i], g_mla_kv[i],
                              w_mla_kv_up[i], g_gqa_q[i], g_gqa_k[i], w_out_ab[i])
            ab_ctx.append(new)
        else:
            ctx = None if caches is None else tuple(cc[:, i] for cc in caches[4:])
            m, new = mixer_c(h, rope, ctx, l, w_in_c[i], lambda_q1[i], lambda_k1[i], lambda_q2[i],
                             lambda_k2[i], g_diff_sub[i], w_out_c[i])
            c_ctx.append(new)
        x = x + gt1 * m
        h = rmsnorm(x, g_norm_ffn[l]) * (1 + sc2) + sh2
        x = x + gt2 * conv_ffn(h, w_ffn_up[l], w_ffn_conv[l], b_ffn_conv[l], w_ffn_down[l])
    return rmsnorm(x, g_final), ab_ctx, c_ctx


def _normal(key, shape, scale):
    return jax.random.normal(key, shape, jnp.float32) * scale


def setup_inputs(seed: int = 0) -> dict:
    key = jax.random.key(seed)
    ks = iter(jax.random.split(key, 40))
    D = D_MODEL
    inp = {}
    inp["x_prompt"] = _normal(next(ks), (BATCH, SEQ, D), 1.0)
    inp["x_sample"] = _normal(next(ks), (DEC_BATCH, DEC_SEQ, D), 1.0)
    inp["cache_mla_ckv"] = _normal(next(ks), (DEC_BATCH, N_EVEN, PAST_LEN, MLA_KV_LORA), 1.0)
    inp["cache_mla_krope"] = _normal(next(ks), (DEC_BATCH, N_EVEN, PAST_LEN, MLA_ROPE_DIM), 1.0)
    inp["cache_gqa_k"] = _normal(next(ks), (DEC_BATCH, N_EVEN, PAST_LEN, GQA_KV_HEADS, HEAD_DIM), 1.0)
    inp["cache_gqa_v"] = _normal(next(ks), (DEC_BATCH, N_EVEN, PAST_LEN, GQA_KV_HEADS, HEAD_DIM), 1.0)
    inp["cache_diff_k"] = _normal(next(ks), (DEC_BATCH, N_ODD, PAST_LEN, DIFF_HEADS, 2, HEAD_DIM), 1.0)
    inp["cache_diff_v"] = _normal(next(ks), (DEC_BATCH, N_ODD, PAST_LEN, DIFF_HEADS, 2 * HEAD_DIM), 1.0)
    inp["c"] = _normal(next(ks), (DEC_BATCH, D), 1.0)
    inp["c_ctx"] = _normal(next(ks), (D,), 1.0)
    inp["w_ada"] = _normal(next(ks), (DEPTH, D, 6 * D), D ** -0.5)
    inp["b_ada"] = _normal(next(ks), (DEPTH, 6 * D), 0.01)
    inp["g_norm_mix"] = 1.0 + _normal(next(ks), (DEPTH, D), 0.02)
    inp["g_norm_ffn"] = 1.0 + _normal(next(ks), (DEPTH, D), 0.02)
    inp["w_in_ab"] = _normal(next(ks), (N_EVEN, D, N_IN_AB), D ** -0.5)
    inp["g_mla_q"] = 1.0 + _normal(next(ks), (N_EVEN, MLA_Q_LORA), 0.02)
    inp["w_mla_q_up"] = _normal(next(ks), (N_EVEN, MLA_Q_LORA, MLA_HEADS * (MLA_NOPE_DIM + MLA_ROPE_DIM)), MLA_Q_LORA ** -0.5)
    inp["g_mla_kv"] = 1.0 + _normal(next(ks), (N_EVEN, MLA_KV_LORA), 0.02)
    inp["w_mla_kv_up"] = _normal(next(ks), (N_EVEN, MLA_KV_LORA, MLA_HEADS * (MLA_NOPE_DIM + MLA_V_DIM)), MLA_KV_LORA ** -0.5)
    inp["g_gqa_q"] = 1.0 + _normal(next(ks), (N_EVEN, HEAD_DIM), 0.02)
    inp["g_gqa_k"] = 1.0 + _normal(next(ks), (N_EVEN, HEAD_DIM), 0.02)
    n_ab_out = MLA_HEADS * MLA_V_DIM + GQA_HEADS * HEAD_DIM
    inp["w_out_ab"] = _normal(next(ks), (N_EVEN, n_ab_out, D), n_ab_out ** -0.5)
    inp["w_in_c"] = _normal(next(ks), (N_ODD, D, N_IN_C), D ** -0.5)
    inp["lambda_q1"] = _normal(next(ks), (N_ODD, HEAD_DIM), 0.1)
    inp["lambda_k1"] = _normal(next(ks), (N_ODD, HEAD_DIM), 0.1)
    inp["lambda_q2"] = _normal(next(ks), (N_ODD, HEAD_DIM), 0.1)
    inp["lambda_k2"] = _normal(next(ks), (N_ODD, HEAD_DIM), 0.1)
    inp["g_diff_sub"] = 1.0 + _normal(next(ks), (N_ODD, 2 * HEAD_DIM), 0.02)
    n_c_out = DIFF_HEADS * 2 * HEAD_DIM
    inp["w_out_c"] = _normal(next(ks), (N_ODD, n_c_out, D), n_c_out ** -0.5)
    inp["w_ffn_up"] = _normal(next(ks), (DEPTH, D, 2 * D_FF), D ** -0.5)
    inp["w_ffn_conv"] = _normal(next(ks), (DEPTH, CONV_W, 2 * D_FF), CONV_W ** -0.5)
    inp["b_ffn_conv"] = _normal(next(ks), (DEPTH, 2 * D_FF), 0.01)
    inp["w_ffn_down"] = _normal(next(ks), (DEPTH, D_FF, D), D_FF ** -0.5)
    inp["g_final"] = 1.0 + _normal(next(ks), (D,), 0.02)
    return inp


def reference(x_prompt, x_sample, cache_mla_ckv, cache_mla_krope, cache_gqa_k, cache_gqa_v,
              cache_diff_k, cache_diff_v, c, c_ctx, w_ada, b_ada, g_norm_mix, g_norm_ffn,
              w_in_ab, g_mla_q, w_mla_q_up, g_mla_kv, w_mla_kv_up, g_gqa_q, g_gqa_k, w_out_ab,
              w_in_c, lambda_q1, lambda_k1, lambda_q2, lambda_k2, g_diff_sub, w_out_c,
              w_ffn_up, w_ffn_conv, b_ffn_conv, w_ffn_down, g_final):
    y_prompt, ab_p, c_p = _trunk(
        x_prompt, c_ctx[None, :], None, None, w_ada, b_ada, g_norm_mix, g_norm_ffn,
        w_in_ab, g_mla_q, w_mla_q_up, g_mla_kv, w_mla_kv_up, g_gqa_q, g_gqa_k, w_out_ab,
        w_in_c, lambda_q1, lambda_k1, lambda_q2, lambda_k2, g_diff_sub, w_out_c,
        w_ffn_up, w_ffn_conv, b_ffn_conv, w_ffn_down, g_final)
    n_lat = x_sample.shape[1]
    rope = (*rope_tables(n_lat, MLA_ROPE_DIM), *rope_tables(n_lat, HEAD_DIM))
    caches = (cache_mla_ckv, cache_mla_krope, cache_gqa_k, cache_gqa_v, cache_diff_k, cache_diff_v)
    y_sample, _, _ = _trunk(
        x_sample, c, rope, caches, w_ada, b_ada, g_norm_mix, g_norm_ffn,
        w_in_ab, g_mla_q, w_mla_q_up, g_mla_kv, w_mla_kv_up, g_gqa_q, g_gqa_k, w_out_ab,
        w_in_c, lambda_q1, lambda_k1, lambda_q2, lambda_k2, g_diff_sub, w_out_c,
        w_ffn_up, w_ffn_conv, b_ffn_conv, w_ffn_down, g_final)
    new_mla_ckv = jnp.stack([t[0] for t in ab_p], axis=1)
    new_mla_krope = jnp.stack([t[1] for t in ab_p], axis=1)
    new_gqa_k = jnp.stack([t[2] for t in ab_p], axis=1)
    new_gqa_v = jnp.stack([t[3] for t in ab_p], axis=1)
    new_diff_k = jnp.stack([t[0] for t in c_p], axis=1)
    new_diff_v = jnp.stack([t[1] for t in c_p], axis=1)
    return (y_prompt, y_sample, new_mla_ckv, new_mla_krope, new_gqa_k, new_gqa_v, new_diff_k, new_diff_v)
```

```python
import math
import numpy as np
import concourse.bass as bass
import concourse.mybir as mybir
from concourse.bass_utils import run_bass_kernel_spmd

F32 = mybir.dt.float32
BF16 = mybir.dt.bfloat16
AF = mybir.ActivationFunctionType
ALU = mybir.AluOpType
AX = mybir.AxisListType

T = 1024
D = 2048
KC = 16
NKEY = 1536
NKB = 12
QG = 256
NQG = 4
DFF = 5632
NFC = 44
EPS = 1e-6
MLA_SCALE = 192 ** -0.5
HEAD_SCALE = 128 ** -0.5
LAMBDA_INIT = 0.8 - 0.6 * math.exp(-0.3 * 1)
NEG = -30000.0
NSLOT = 4
WSLOT_ELEMS = 4096

SAME_ENGINE_SYNC = True


class Buf:
    __slots__ = ("name", "w", "r", "excl")

    def __init__(self, name="", excl=False):
        self.name = name
        self.w = None
        self.r = []
        self.excl = excl


class Tok:
    __slots__ = ("sem", "val", "eng")

    def __init__(self, sem, val, eng):
        self.sem, self.val, self.eng = sem, val, eng


class Sched:
    def __init__(self, nc, n_dma_sems=8):
        self.nc = nc
        self.dry = False
        self.eng = {"pe": nc.tensor, "act": nc.scalar, "dve": nc.vector,
                    "pool": nc.gpsimd, "sp": nc.sync}
        self.sems = {k: [nc.alloc_semaphore("s_%s%d" % (k, i)) for i in range(3)]
                     for k in ("pe", "act", "dve", "pool")}
        self.sem = {k: v[0] for k, v in self.sems.items()}
        self.dsem = {q: [nc.alloc_semaphore("d_%s%d" % (q, i)) for i in range(n_dma_sems)]
                     for q in ("sp", "pool")}
        self.reset()

    def reset(self):
        self.cnt = {k: 0 for k in self.sems}
        self.epoch = {k: 0 for k in self.sems}
        self.sem = {k: v[0] for k, v in self.sems.items()}
        self.last_tok = {k: None for k in self.sems}
        self.seen = {k: {} for k in self.eng}
        self.dval = {q: [0] * len(v) for q, v in self.dsem.items()}
        self.dpos = {q: 0 for q in self.dsem}
        self.n_inst = 0

    def _wait(self, e, tok):
        if tok is None:
            return
        if tok.eng == e:
            if (not SAME_ENGINE_SYNC) or e == "pe":
                return
        key = id(tok.sem)
        if self.seen[e].get(key, 0) >= tok.val:
            return
        self.eng[e].wait_ge(tok.sem, tok.val)
        self.seen[e][key] = tok.val

    def _deps(self, e, reads, writes):
        for b in reads:
            self._wait(e, b.w)
            if b.excl:
                for t in b.r:
                    if t.eng != e:
                        self._wait(e, t)
        for b in writes:
            self._wait(e, b.w)
            for t in b.r:
                self._wait(e, t)

    def _commit(self, tok, reads, writes):
        for b in reads:
            b.r.append(tok)
            if len(b.r) > 16:
                best = {}
                for t in b.r:
                    k = id(t.sem)
                    if k not in best or best[k].val < t.val:
                        best[k] = t
                b.r = list(best.values())
        for b in writes:
            b.w = tok
            b.r = []

    def op(self, e, reads, writes, fn):
        if self.dry:
            return None
        self._deps(e, reads, writes)
        ins = fn(self.eng[e])
        self.cnt[e] += 1
        ins.then_inc(self.sem[e], 1)
        tok = Tok(self.sem[e], self.cnt[e], e)
        self.last_tok[e] = tok
        self._commit(tok, reads, writes)
        if self.cnt[e] >= 20000:
            self.epoch[e] += 1
            self.sem[e] = self.sems[e][self.epoch[e]]
            self.cnt[e] = 0
        self.n_inst += 1
        return tok

    def dma(self, q, reads, writes, out, in_):
        if self.dry:
            return None
        i = self.dpos[q]
        self.dpos[q] = (i + 1) % len(self.dsem[q])
        sem = self.dsem[q][i]
        if self.dval[q][i] > 0:
            self._wait(q, Tok(sem, self.dval[q][i], "dma_" + q))
        self._deps(q, reads, writes)
        ins = self.eng[q].dma_start(out=out, in_=in_)
        self.dval[q][i] += 16
        ins.then_inc(sem, 16)
        tok = Tok(sem, self.dval[q][i], "dma_" + q)
        self._commit(tok, reads, writes)
        self.n_inst += 1
        return tok

    def _all_toks(self):
        toks = []
        for k in self.sems:
            if self.last_tok[k] is not None:
                toks.append(self.last_tok[k])
        for q in self.dsem:
            for i, s in enumerate(self.dsem[q]):
                if self.dval[q][i] > 0:
                    toks.append(Tok(s, self.dval[q][i], "dma_" + q))
        return toks

    def barrier(self, bufs=()):
        if self.dry:
            return
        toks = self._all_toks()
        for e in ("pe", "act", "dve", "pool", "sp"):
            for t in toks:
                if t.eng == e:
                    continue
                self._wait(e, t)

    def finish(self):
        if self.dry:
            return
        for t in self._all_toks():
            self._wait("sp", t)


def _fm(v):
    v = np.asarray(v, np.float32)
    lead = v.shape[:-1]
    n = v.shape[-1] // 128
    v = v.reshape(*lead, n, 128)
    v = np.moveaxis(v, -1, 0)
    return np.ascontiguousarray(v).reshape(128, -1)


PRM_FIELDS = [("gmix", 32), ("gffn", 32), ("gfin", 16), ("bada", 192), ("gq", 6), ("gkv", 4),
              ("ggq", 1), ("ggqs", 1), ("ggk", 1), ("ggks", 1), ("gsub", 2),
              ("wconv", 2 * 3 * 88), ("bconv", 2 * 88), ("maskb", 48), ("cflag", 1),
              ("lam", 512), ("cond", 16)]
OFF = {}
_o = 0
for _n, _s in PRM_FIELDS:
    OFF[_n] = _o
    _o += _s
NP_ = _o


class Prog:
    def __init__(self, stop=99, dbg=False, sub=99):
        self.sub = sub
        self.stop = stop
        self.dbg = dbg
        nc = bass.Bass("TRN2", target_bir_lowering=False)
        self.nc = nc
        dt = nc.dram_tensor
        I = "ExternalInput"
        O = "ExternalOutput"
        self.d = {}
        for name, shape in [("xT", [D, T]), ("prm", [128, NP_]), ("rope128", [2, 128, T]),
                            ("rope64", [2, 64, T]), ("ckvT", [512, 512]), ("kropeT", [64, 512]),
                            ("gkT", [2, 128, 512]), ("gv", [512, 256]), ("dkT", [16, 128, 512]),
                            ("dv", [512, 2048]), ("w_ada", [2, D, 6 * D]), ("w_in_ab", [D, 2880]),
                            ("w_q_up", [768, 1536]), ("w_kv_up", [512, 2048]), ("w_out_ab", [D, D]),
                            ("w_in_c", [D, 6144]), ("w_out_c", [D, D]), ("w_up", [2, D, 2 * DFF]),
                            ("w_down", [2, DFF, D])]:
            self.d[name] = dt(name, shape, F32, kind=I).ap()
        for name, shape in [("yT", [D, T]), ("o_ckvT", [512, T]), ("o_kropeT", [64, T]),
                            ("o_gkT", [256, T]), ("o_gv", [T, 256]), ("o_dkT", [2048, T]),
                            ("o_dv", [T, 2048])]:
            self.d[name] = dt(name, shape, F32, kind=O).ap()
        if dbg:
            self.d["dbg"] = dt("dbg", [128, 4096], F32, kind=O).ap()
        self.S = Sched(nc)
        self.off = 16512
        self.xT = self.alloc("xT", [128, KC, T], F32)
        self.hT = self.alloc("hT", [128, KC, T], BF16)
        self.W = [self.alloc("w%d" % i, [128, WSLOT_ELEMS], BF16) for i in range(NSLOT)]
        self.rope = self.alloc("rope", [128, 2, T], F32)
        self.prm = self.alloc("prm", [128, NP_], F32)
        self.modT = self.alloc("modT", [128, 2, 96], F32)
        self.der = self.alloc("der", [128, 64], F32)
        self.fix = self.alloc("fix", [128, 2, 2, 88], F32)
        self.lamt = self.alloc("lamt", [128, 8], F32)
        self.ones = self.alloc("ones", [128, 128], BF16)
        self.condT = self.alloc("condT", [128, 16], BF16)
        self.arena0 = self.off
        self.arena_end = 229376
        self.ps = nc.alloc_psum_tensor("ps", [128, 4096], F32)
        self.b_x = [Buf("x%d" % i) for i in range(KC)]
        self.b_h = [Buf("h%d" % i) for i in range(KC)]
        self.b_w = [Buf("w%d" % i) for i in range(NSLOT)]
        self.b_rope = Buf("rope")
        self.b_prm = Buf("prm")
        self.b_mod = [[Buf("mod%d_%d" % (l, s)) for s in range(6)] for l in range(2)]
        self.b_der = Buf("der")
        self.b_fix = Buf("fix")
        self.b_lam = Buf("lam")
        self.b_ones = Buf("ones")
        self.b_cond = Buf("cond")
        self.b_ps = [Buf("ps%d" % i, excl=True) for i in range(8)]
        self.wplan = None

    def alloc(self, name, shape, dtype, arena=False):
        nbytes = int(np.prod(shape[1:])) * (4 if dtype == F32 else 2)
        nbytes = (nbytes + 31) // 32 * 32
        t = self.nc.alloc_sbuf_tensor_at(name, shape, dtype, offset=self.off)
        self.off += nbytes
        assert self.off <= 229376, (name, self.off)
        return t

    def arena_reset(self):
        self.S.barrier()
        self.off = self.arena0
        self.acount = getattr(self, "acount", 0) + 1

    def aalloc(self, name, shape, dtype):
        t = self.alloc("%s_%d_%d" % (name, self.runid, self.acount), shape, dtype)
        return t

    def _psalloc(self, pool, n):
        p = self.pspos[pool]
        p = (p + n - 1) // n * n
        if p + n > 4:
            p = 0
        self.pspos[pool] = (p + n) % 4
        base = pool * 4 + p
        return base * 512, self.b_ps[base:base + n]

    def psA(self, n):
        return self._psalloc(0, n)

    def psB(self, n):
        return self._psalloc(1, n)

    def get_w(self, name, l, row0, kcn, col0, ncols, hold=False):
        spec = (name, l, row0, kcn, col0, ncols)
        i = self.wi
        self.wi += 1
        self.last_w = i
        if self.S.dry:
            self.wplan_new.append(spec)
            slot = 0
        else:
            assert self.wplan[i] == spec, (i, self.wplan[i], spec)
            for j in self.w_auto:
                self._free_w(j)
            self.w_auto = set()
            if not hold:
                self.w_auto.add(i)
            self._pump()
            assert self.wissued > i, ("weight tile not issued (no free slot)", i, spec)
            slot = self.w_slot[i]
        view = self.W[slot][:, 0:kcn * ncols].rearrange("p (k n) -> p k n", k=kcn)
        return view, self.b_w[slot]

    def _free_w(self, j):
        self.w_free.append(self.w_slot[j])

    def rel_w(self, i):
        if not self.S.dry:
            self._free_w(i)
            self._pump()

    def _pump(self):
        while self.wissued < len(self.wplan) and self.wissued < self.wi + NSLOT - 1 and self.w_free:
            self._issue_w(self.wissued)
            self.wissued += 1

    def _issue_w(self, j):
        name, l, row0, kcn, col0, ncols = self.wplan[j]
        src = self.d[name]
        if l is not None:
            src = src[l]
        src = src[row0:row0 + kcn * 128, col0:col0 + ncols].rearrange("(k p) n -> p k n", p=128)
        slot = self.w_free.pop(0)
        self.w_slot[j] = slot
        dst = self.W[slot][:, 0:kcn * ncols].rearrange("p (k n) -> p k n", k=kcn)
        self.S.dma("pool", [], [self.b_w[slot]], dst, src)

    def mm(self, out, lhsT, rhs, start, stop, reads, writes):
        self.S.op("pe", reads + writes if not start else reads, writes,
                  lambda e: e.matmul(out, lhsT, rhs, start=start, stop=stop))

    def P(self, name, n=None, lo=0):
        o = OFF[name] + lo
        if n is None:
            n = 1
        return self.prm[:, o:o + n]

    def ada_to(self, upto):
        while self.ada_done < min(upto, 96):
            self.ada_tile(self.ada_done)
            self.ada_done += 1

    def ada_more(self, k=1):
        self.ada_to(self.ada_done + k)

    def ada_tile(self, a):
        S = self.S
        l, t = divmod(a, 48)
        w, bw = self.get_w("w_ada", l, 0, KC, t * 256, 256)
        col, pb = self.psA(1)
        ps = self.ps
        for j in range(2):
            for kc in range(KC):
                self.mm(ps[:, col + j:col + j + 1], w[:, kc, j * 128:(j + 1) * 128],
                        self.condT[:, kc:kc + 1], kc == 0, kc == KC - 1, [bw, self.b_cond], pb)
        sec = (t * 2) // 16
        c0 = t * 2
        S.op("dve", pb + [self.b_prm], [self.b_mod[l][sec]],
             lambda e: e.tensor_tensor(self.modT[:, l, c0:c0 + 2], ps[:, col:col + 2],
                                       self.P("bada", 2, l * 96 + c0), ALU.add))

    def rstd_from_ss(self, col, pb, n, dst, dst_bufs, width=T):
        S = self.S
        ps = self.ps
        S.op("act", pb, dst_bufs,
             lambda e: e.activation(dst, ps[:, col:col + width], AF.Ln, bias=EPS, scale=1.0 / n))
        S.op("act", dst_bufs, dst_bufs, lambda e: e.activation(dst, dst, AF.Exp, scale=-0.5))

    def sumsq(self, src_fn, src_bufs_fn, nchunks, sqs, sq_bufs, col, pb, first=True, last=True, base=0, total=None, dve_alt=False):
        S = self.S
        ps = self.ps
        total = nchunks if total is None else total
        for c in range(nchunks):
            sq = sqs[c % len(sqs)]
            sb = sq_bufs[c % len(sqs)]
            src = src_fn(c)
            if dve_alt and c % 2 == 1:
                S.op("dve", src_bufs_fn(c), [sb], lambda e: e.tensor_tensor(sq[:, :], src, src, ALU.mult))
            else:
                S.op("act", src_bufs_fn(c), [sb], lambda e: e.activation(sq[:, :], src, AF.Square))
            for tg in range(2):
                self.mm(ps[:, col + tg * 512:col + (tg + 1) * 512], self.ones[:, :],
                        sq[:, tg * 512:(tg + 1) * 512], (base + c) == 0, (base + c) == total - 1,
                        [sb, self.b_ones], [pb[tg]])

    def norm_mod(self, l, which, rstd, b_rstd, tmp, b_tmp, sqs, b_sqs):
        S = self.S
        self.ada_to(l * 48 + (16 if which == 0 else 40))
        sec_sh, sec_sc = 3 * which, 3 * which + 1
        gname = "gmix" if which == 0 else "gffn"
        gs = self.der[:, 0:16]
        S.op("dve", [self.b_mod[l][sec_sc], self.b_prm], [self.b_der],
             lambda e: e.scalar_tensor_tensor(gs, self.modT[:, l, sec_sc * 16:(sec_sc + 1) * 16], 1.0,
                                              self.P(gname, 16, l * 16), ALU.add, ALU.mult))
        col, pb = self.psB(2)
        self.sumsq(lambda c: self.xT[:, c, :], lambda c: [self.b_x[c]], KC, sqs, b_sqs, col, pb, dve_alt=True)
        self.rstd_from_ss(col, pb, D, rstd[:, :], [b_rstd])
        for kc in range(KC):
            t_ = tmp[kc % len(tmp)]
            bt = b_tmp[kc % len(tmp)]
            S.op("dve", [self.b_x[kc], self.b_der, b_rstd], [bt],
                 lambda e: e.scalar_tensor_tensor(t_[:, :], self.xT[:, kc, :], self.der[:, kc:kc + 1],
                                                  rstd[:, :], ALU.mult, ALU.mult))
            S.op("act", [bt, self.b_mod[l][sec_sh]], [self.b_h[kc]],
                 lambda e: e.activation(self.hT[:, kc, :], t_[:, :], AF.Identity,
                                        bias=self.modT[:, l, sec_sh * 16 + kc:sec_sh * 16 + kc + 1], scale=1.0))

    def ps_alt(self, n):
        self.alt_i ^= 1
        return self.psA(n) if self.alt_i else self.psB(n)

    def lin_fm(self, w, bw, kcn, c0, M, rhs_fn, rhs_bufs_fn, alt=False):
        col, pb = self.ps_alt(2) if alt else self.psA(2)
        ps = self.ps
        for k in range(kcn):
            for tg in range(2):
                self.mm(ps[0:M, col + tg * 512:col + (tg + 1) * 512], w[:, k, c0:c0 + M],
                        rhs_fn(k, tg), k == 0, k == kcn - 1, [bw] + rhs_bufs_fn(k), [pb[tg]])
        return col, pb

    def h_rhs(self, k, tg):
        return self.hT[:, k, tg * 512:(tg + 1) * 512]

    def h_bufs(self, k):
        return [self.b_h[k]]

    def lin_tm(self, w, bw, c0, n, blk):
        col, pb = self.psA(1)
        ps = self.ps
        for k in range(KC):
            self.mm(ps[:, col:col + n], self.hT[:, k, blk * 128:(blk + 1) * 128], w[:, k, c0:c0 + n],
                    k == 0, k == KC - 1, [bw, self.b_h[k]], pb)
        return col, pb

    def rope_chunk(self, col, pb, M, g, gsw, t1, b_t1, t2, b_t2, dst=None, dst_bufs=None):
        S = self.S
        ps = self.ps
        h = M // 2
        C = self.rope[0:M, 0, :]
        SN = self.rope[0:M, 1, :]
        rd = pb + [self.b_rope, self.b_prm]
        S.op("dve", rd, [b_t1],
             lambda e: e.scalar_tensor_tensor(t1[0:M, :], ps[0:M, col:col + T], g, C, ALU.mult, ALU.mult))
        gl = gsw[0:h] if not isinstance(gsw, float) else gsw
        gh = gsw[h:M] if not isinstance(gsw, float) else gsw
        S.op("dve", rd, [b_t2],
             lambda e: e.scalar_tensor_tensor(t2[0:h, :], ps[h:M, col:col + T], gl, SN[0:h, :],
                                              ALU.mult, ALU.mult))
        S.op("dve", rd, [b_t2],
             lambda e: e.scalar_tensor_tensor(t2[h:M, :], ps[0:h, col:col + T], gh, SN[h:M, :],
                                              ALU.mult, ALU.mult))
        if dst is None:
            S.op("dve", [b_t1, b_t2], [b_t1],
                 lambda e: e.tensor_tensor(t1[0:M, :], t1[0:M, :], t2[0:M, :], ALU.add))
        else:
            S.op("dve", [b_t1, b_t2], dst_bufs,
                 lambda e: e.tensor_tensor(dst, t1[0:M, :], t2[0:M, :], ALU.add))

    def attn_score_tile(self, parts, scale, qg, pi, kb):
        S = self.S
        ps = self.ps
        col, pb = self.psA(1)
        for i, (lf, rhs, rf) in enumerate(parts):
            self.mm(ps[:, col:col + QG], lf(kb), rhs, i == 0, i == len(parts) - 1, rf(kb), pb)
        mb = self.P("maskb", 1, qg * NKB + kb)
        pt = self.PT[pi][:, kb, :]
        S.op("act", pb + [self.b_prm], [self.b_pt[pi][kb]],
             lambda e: e.activation(pt, ps[:, col:col + QG], AF.Exp, bias=mb, scale=scale))

    def attn_pv_list(self, pi, v_fn, v_reads_fn, ndv):
        col, pb = self.psB(2)
        ps = self.ps
        lst = []
        for c in range(ndv):
            for kb in range(NKB):
                lst.append(lambda c=c, kb=kb: self.mm(
                    ps[:, col + c * QG:col + (c + 1) * QG], v_fn(kb, c), self.PT[pi][:, kb, :],
                    kb == 0, kb == NKB - 1, v_reads_fn(kb) + [self.b_pt[pi][kb]], [pb[0]]))
        for kb in range(NKB):
            lst.append(lambda kb=kb: self.mm(
                ps[:, col + 2 * QG:col + 3 * QG], self.ones[:, :], self.PT[pi][:, kb, :],
                kb == 0, kb == NKB - 1, [self.b_ones, self.b_pt[pi][kb]], [pb[1]]))
        return col, pb, lst

    def run_units(self, units):
        prev = None
        for u in list(units) + [None]:
            lst = []
            if prev is not None:
                pu, ppi = prev
                col, pb, lst = self.attn_pv_list(ppi, pu["v_fn"], pu["v_reads"], pu["ndv"])
            if u is not None:
                pi = self.pt_i
                self.pt_i ^= 1
                per = (len(lst) + NKB - 1) // NKB
                for kb in range(NKB):
                    self.attn_score_tile(u["parts"], u["scale"], u["qg"], pi, kb)
                    for f in lst[kb * per:(kb + 1) * per]:
                        f()
                for f in lst[NKB * per:]:
                    f()
                self.unit_ctr += 1
                if self.unit_ctr % 2 == 0:
                    self.ada_more(1)
            else:
                for f in lst:
                    f()
            if prev is not None:
                pu["fin"](col, pb)
            prev = (u, pi) if u is not None else None

    def softmax_fin(self, col, pb, ndv, outs, out_bufs):
        S = self.S
        ps = self.ps
        r = self.rD[self.rd_i % len(self.rD)]
        br = self.b_rD[self.rd_i % len(self.rD)]
        self.rd_i += 1
        S.op("dve", [pb[1]], [br], lambda e: e.reciprocal(r[:, :], ps[:, col + 2 * QG:col + 3 * QG]))
        for c in range(ndv):
            o = outs[c]
            S.op("dve", [pb[0], br], [out_bufs[c]],
                 lambda e: e.tensor_tensor(o, ps[:, col + c * QG:col + (c + 1) * QG], r[:, :], ALU.mult))

    def resid_partial(self, wl, kcn, rhs_fn, rhs_bufs_fn, l, sec, alt=False):
        S = self.S
        ps = self.ps
        for oc in range(KC):
            col, pb = self.ps_alt(2) if alt else self.psB(2)
            for k in range(kcn):
                w, ki, bw = wl(k)
                for tg in range(2):
                    self.mm(ps[:, col + tg * 512:col + (tg + 1) * 512], w[:, ki, oc * 128:(oc + 1) * 128],
                            rhs_fn(k, tg), k == 0, k == kcn - 1, [bw] + rhs_bufs_fn(k), [pb[tg]])
            gate = self.modT[:, l, sec * 16 + oc:sec * 16 + oc + 1]
            xs = self.xT[:, oc, :]
            S.op("dve", pb + [self.b_mod[l][sec], self.b_x[oc]], [self.b_x[oc]],
                 lambda e: e.scalar_tensor_tensor(xs, ps[:, col:col + T], gate, xs, ALU.mult, ALU.add))

    def emit(self):
        S = self.S
        nc = self.nc
        d = self.d
        ps = self.ps
        self.pspos = [0, 0]
        self.wi = 0
        self.wissued = 0
        self.w_free = list(range(NSLOT))
        self.w_slot = {}
        self.w_auto = set()
        self.ada_done = 0
        self.pt_i = 0
        self.alt_i = 0
        self.unit_ctr = 0
        self.rd_i = 0
        self.acount = 0
        self.wplan_new = []
        for bl in (self.b_x, self.b_h, self.b_w, self.b_ps, [self.b_rope, self.b_prm, self.b_der, self.b_fix,
                                                            self.b_lam, self.b_ones, self.b_cond],
                   self.b_mod[0], self.b_mod[1]):
            for b in bl:
                b.w = None
                b.r = []
        self.off = self.arena0

        S.dma("sp", [], [self.b_prm], self.prm[:, :], d["prm"])
        xsrc = d["xT"].rearrange("(k p) t -> p k t", p=128)
        for g in range(4):
            S.dma("sp", [], self.b_x[4 * g:4 * g + 4], self.xT[:, 4 * g:4 * g + 4, :], xsrc[:, 4 * g:4 * g + 4, :])
        S.op("dve", [], [self.b_ones], lambda e: e.memset(self.ones[:, :], 1.0))
        S.op("act", [self.b_prm], [self.b_cond],
             lambda e: e.activation(self.condT[:, :], self.P("cond", 16), AF.Silu))
        self.lamtmp = self.aalloc("lamtmp", [128, 128], F32)
        for i in range(2):
            S.op("dve", [self.b_prm], [self.b_lam],
                 lambda e: e.tensor_tensor(self.lamtmp[:, :], self.P("lam", 128, 256 * i),
                                           self.P("lam", 128, 256 * i + 128), ALU.mult))
            S.op("dve", [self.b_lam], [self.b_lam],
                 lambda e: e.reduce_sum(self.lamt[:, i:i + 1], self.lamtmp[:, :], AX.X))
        S.op("act", [self.b_lam], [self.b_lam],
             lambda e: e.activation(self.lamt[:, 2:4], self.lamt[:, 0:2], AF.Exp))
        S.op("dve", [self.b_lam], [self.b_lam],
             lambda e: e.tensor_tensor(self.lamt[:, 4:5], self.lamt[:, 3:4], self.lamt[:, 2:3], ALU.subtract))
        S.op("dve", [self.b_lam], [self.b_lam],
             lambda e: e.tensor_scalar(self.lamt[:, 5:6], self.lamt[:, 4:5], -LAMBDA_INIT, None, ALU.add))
        S.op("dve", [self.b_prm], [self.b_lam],
             lambda e: e.tensor_scalar(self.lamt[:, 6:8], self.P("gsub", 2), 1.0 - LAMBDA_INIT, None, ALU.mult))
        for l in range(2):
            for i, tap in enumerate((0, 2)):
                S.op("dve", [self.b_prm], [self.b_fix],
                     lambda e: e.tensor_scalar(self.fix[:, l, i, :], self.P("wconv", 88, (l * 3 + tap) * 88),
                                               self.P("cflag", 1), None, ALU.mult))
        self.ada_to(16)

        self.pre_normed = set()
        self.hoist = (self.stop >= 99)
        phases = [self.layer0_gqa, self.layer0_mla, lambda: self.ffn(0), self.layer1_attn,
                  lambda: self.ffn(1), lambda: (None if self.hoist else self.final_norm())]
        for i, ph in enumerate(phases):
            if i < self.stop:
                ph()
        if self.dbg:
            S.barrier()
            S.dma("sp", [], [], d["dbg"][:, 0:192], self.modT[:, :, :].rearrange("p l c -> p (l c)"))
            S.dma("sp", [], [], d["dbg"][:, 192:200], self.lamt[:, :])
            S.dma("sp", [], [], d["dbg"][:, 256:256 + 352], self.fix[:, :, :, :].rearrange("p a b c -> p (a b c)"))
            for kc in range(4):
                S.dma("sp", [], [], d["dbg"][:, 1024 + kc * 512:1024 + (kc + 1) * 512], self.xT[:, kc, 0:512])
        S.finish()

    def common_attn_arena(self):
        self.PT = [self.aalloc("pt%d" % i, [128, NKB, QG], BF16) for i in range(2)]
        self.b_pt = [[Buf("pt%d_%d" % (i, k)) for k in range(NKB)] for i in range(2)]

    def load_rope(self, which):
        S = self.S
        if which == 128:
            S.dma("sp", [], [self.b_rope], self.rope[:, :, :], self.d["rope128"].rearrange("c p t -> p c t"))
        else:
            S.dma("sp", [], [self.b_rope], self.rope[0:64, :, :], self.d["rope64"].rearrange("c p t -> p c t"))

    def out_fm(self, dram_rows_ap, src, src_bufs):
        self.S.dma("sp", src_bufs, [], dram_rows_ap, src)

    def layer0_gqa(self):
        S = self.S
        ps = self.ps
        d = self.d
        self.arena_reset()
        self.common_attn_arena()
        kbT = self.aalloc("kbT", [128, 2, NKEY], BF16)
        vb = self.aalloc("vb", [128, NKB, 256], BF16)
        F = [self.aalloc("F%d" % i, [128, T], F32) for i in range(5)]
        bF = [Buf("F%d" % i) for i in range(5)]
        qb = [self.aalloc("qb%d" % i, [128, T], BF16) for i in range(2)]
        b_qb = [Buf("qb%d" % i) for i in range(2)]
        sq = [self.aalloc("sq0", [128, T], BF16)]
        b_sq = [Buf("sq0")]
        ao = self.aalloc("ao", [128, 4, T], BF16)
        b_ao = [Buf("ao%d" % i) for i in range(4)]
        self.rD = [self.aalloc("rD%d" % i, [128, QG], F32) for i in range(2)]
        self.b_rD = [Buf("rD%d" % i) for i in range(2)]
        b_kb = [Buf("kb0"), Buf("kb1")]
        b_vb = [Buf("vb%d" % i) for i in range(NKB)]

        import os
        skip = os.environ.get("DBG_SKIP", "")
        if "rope" not in skip:
            self.load_rope(128)
        if "kv" not in skip:
            for g in range(2):
                S.dma("pool", [], [b_kb[g]], kbT[:, g, T:NKEY], d["gkT"][g])
            S.dma("pool", [], b_vb[8:12], vb[:, 8:12, :], d["gv"].rearrange("(b p) n -> p b n", p=128))

        if self.sub == 0:
            return
        self.norm_mod(0, 0, F[0], bF[0], [F[1], F[2]], [bF[1], bF[2]], sq + qb, b_sq + b_qb)
        rstd, b_rstd = F[0], bF[0]
        if self.sub == 1:
            return

        def normrope_head(col, pb, gname, gsname, dst_bf, dst_bufs, out_rows=None):
            c2, pb2 = self.psB(2)
            self.sumsq(lambda c: ps[:, col:col + T], lambda c: pb, 1, sq, b_sq, c2, pb2)
            self.rstd_from_ss(c2, pb2, 128, F[3][:, :], [bF[3]])
            self.rope_chunk(col, pb, 128, self.P(gname), self.P(gsname), F[1], bF[1], F[2], bF[2])
            if out_rows is None:
                S.op("dve", [bF[1], bF[3]], dst_bufs,
                     lambda e: e.tensor_tensor(dst_bf, F[1][:, :], F[3][:, :], ALU.mult))
            else:
                S.op("dve", [bF[1], bF[3]], [bF[4]],
                     lambda e: e.tensor_tensor(F[4][:, :], F[1][:, :], F[3][:, :], ALU.mult))
                self.out_fm(out_rows, F[4][:, :], [bF[4]])
                S.op("act", [bF[4]], dst_bufs, lambda e: e.activation(dst_bf, F[4][:, :], AF.Copy))

        w, bw = self.get_w("w_in_ab", None, 0, KC, 2368, 256)
        for g in range(2):
            col, pb = self.lin_fm(w, bw, KC, g * 128, 128, self.h_rhs, self.h_bufs)
            normrope_head(col, pb, "ggk", "ggks", kbT[:, g, 0:T], [b_kb[g]],
                          out_rows=d["o_gkT"][g * 128:(g + 1) * 128, :])
        self.ada_more(2)
        if self.sub == 2:
            return
        w, bw = self.get_w("w_in_ab", None, 0, KC, 2624, 256)
        for blk in range(8):
            col, pb = self.lin_tm(w, bw, 0, 256, blk)
            f = F[4] if blk % 2 == 0 else F[2]
            bf = bF[4] if blk % 2 == 0 else bF[2]
            S.op("act", pb, [bf], lambda e: e.activation(f[:, 0:256], ps[:, col:col + 256], AF.Copy))
            S.dma("sp", [bf], [], d["o_gv"][blk * 128:(blk + 1) * 128, :], f[:, 0:256])
            S.op("dve", pb, [b_vb[blk]], lambda e: e.tensor_copy(vb[:, blk, :], ps[:, col:col + 256]))
        self.ada_more(2)
        if self.sub == 3:
            return
        st = {}

        def stage_a(hh):
            g, half, j = hh // 4, (hh % 4) // 2, hh % 2
            if j == 0:
                st["w"] = self.get_w("w_in_ab", None, 0, KC, 1344 + g * 512 + half * 256, 256, hold=True)
                st["wi"] = self.last_w
            w, bw = st["w"]
            col, pb = self.lin_fm(w, bw, KC, j * 128, 128, self.h_rhs, self.h_bufs)
            normrope_head(col, pb, "ggq", "ggqs", qb[hh % 2][:, :], [b_qb[hh % 2]])
            if j == 1:
                self.rel_w(st["wi"])
                self.ada_more(2)

        def stage_b(hh):
            g, h4 = hh // 4, hh % 4
            q = qb[hh % 2]
            bq = b_qb[hh % 2]
            units = []
            for qg in range(NQG):
                units.append(dict(
                    parts=[(lambda kb, g=g: kbT[:, g, kb * 128:(kb + 1) * 128],
                            q[:, qg * QG:(qg + 1) * QG],
                            lambda kb, g=g, bq=bq: [b_kb[g], bq])],
                    scale=HEAD_SCALE, qg=qg,
                    v_fn=lambda kb, c, g=g: vb[:, kb, g * 128:(g + 1) * 128],
                    v_reads=lambda kb: [b_vb[kb]], ndv=1,
                    fin=lambda col, pb, qg=qg, h4=h4: self.softmax_fin(
                        col, pb, 1, [ao[:, h4, qg * QG:(qg + 1) * QG]], [b_ao[h4]])))
            self.run_units(units)
            if h4 == 3:
                self.ada_to(24)
                wl = []
                for i in range(2):
                    wl.append(self.get_w("w_out_ab", None, 1024 + g * 512 + i * 256, 2, 0, D, hold=(i == 0)))
                    if i == 0:
                        w0i = self.last_w
                self.resid_partial(lambda k: (wl[k // 2][0], k % 2, wl[k // 2][1]), 4,
                                   lambda k, tg: ao[:, k, tg * 512:(tg + 1) * 512], lambda k: [b_ao[k]], 0, 2)
                self.rel_w(w0i)

        stage_a(0)
        for hh in range(8):
            if hh + 1 < 8:
                stage_a(hh + 1)
            stage_b(hh)
        self.ada_more(2)

    def layer0_mla(self):
        S = self.S
        ps = self.ps
        d = self.d
        self.arena_reset()
        self.common_attn_arena()
        ckvT = self.aalloc("ckvT", [128, 4, NKEY], BF16)
        kropeT = self.aalloc("kropeT", [128, NKEY], BF16)
        qlat = self.aalloc("qlat", [128, 6, T], BF16)
        F = [self.aalloc("F%d" % i, [128, T], F32) for i in range(5)]
        bF = [Buf("F%d" % i) for i in range(5)]
        sq = [self.aalloc("sq0", [128, T], BF16)]
        b_sq = [Buf("sq0")]
        self.rD = [self.aalloc("rD%d" % i, [128, QG], F32) for i in range(2)]
        self.b_rD = [Buf("rD%d" % i) for i in range(2)]
        b_ckv = [Buf("ckv%d" % i) for i in range(4)]
        b_ckvc = Buf("ckvc")
        b_krope = Buf("krope")
        b_kropec = Buf("kropec")
        b_qlat = [Buf("qlat%d" % i) for i in range(6)]

        self.load_rope(64)
        S.dma("pool", [], [b_ckvc], ckvT[:, :, T:NKEY], d["ckvT"].rearrange("(k p) s -> p k s", p=128))
        S.dma("pool", [], [b_kropec], kropeT[0:64, T:NKEY], d["kropeT"])

        c2, pb2 = self.psB(2)
        for i in range(2):
            w, bw = self.get_w("w_in_ab", None, 0, KC, 768 + i * 256, 256)
            for j in range(2):
                c = i * 2 + j
                col, pb = self.lin_fm(w, bw, KC, j * 128, 128, self.h_rhs, self.h_bufs)
                S.op("act", pb + [self.b_prm], [bF[c]],
                     lambda e: e.activation(F[c][:, :], ps[:, col:col + T], AF.Identity, scale=self.P("gkv", 1, c)))
                self.sumsq(lambda _c: ps[:, col:col + T], lambda _c: pb, 1, sq, b_sq, c2, pb2, base=c, total=4)
            self.ada_more(2)
        self.rstd_from_ss(c2, pb2, 512, F[4][:, :], [bF[4]])
        for c in range(4):
            S.op("dve", [bF[c], bF[4]], [bF[c]],
                 lambda e: e.tensor_tensor(F[c][:, :], F[c][:, :], F[4][:, :], ALU.mult))
            self.out_fm(d["o_ckvT"][c * 128:(c + 1) * 128, :], F[c][:, :], [bF[c]])
            S.op("act", [bF[c]], [b_ckv[c]], lambda e: e.activation(ckvT[:, c, 0:T], F[c][:, :], AF.Copy))
        w, bw = self.get_w("w_in_ab", None, 0, KC, 1280, 64)
        col, pb = self.lin_fm(w, bw, KC, 0, 64, self.h_rhs, self.h_bufs)
        self.rope_chunk(col, pb, 64, 1.0, 1.0, F[4], bF[4], F[0], bF[0])
        self.out_fm(d["o_kropeT"][:, :], F[4][0:64, :], [bF[4]])
        S.op("act", [bF[4]], [b_krope], lambda e: e.activation(kropeT[0:64, 0:T], F[4][0:64, :], AF.Copy))
        self.ada_more(2)
        c2, pb2 = self.psB(2)
        for i in range(3):
            w, bw = self.get_w("w_in_ab", None, 0, KC, i * 256, 256)
            for j in range(2):
                c = i * 2 + j
                col, pb = self.lin_fm(w, bw, KC, j * 128, 128, self.h_rhs, self.h_bufs)
                S.op("dve", pb + [self.b_prm], [b_qlat[c]],
                     lambda e: e.tensor_scalar(qlat[:, c, :], ps[:, col:col + T], self.P("gq", 1, c), None, ALU.mult))
                self.sumsq(lambda _c: ps[:, col:col + T], lambda _c: pb, 1, sq, b_sq, c2, pb2, base=c, total=6)
            self.ada_more(2)
        rq, b_rq = F[3], bF[3]
        self.rstd_from_ss(c2, pb2, 768, rq[:, :], [b_rq])
        hT = self.hT
        bh = self.b_h
        qn = [hT[:, 0, :], hT[:, 1, :]]
        b_qn = [bh[0], bh[1]]
        qr = [hT[0:64, 2, :], hT[0:64, 3, :]]
        b_qr = [bh[2], bh[3]]
        hflat = hT[:, :, :].rearrange("p k t -> p (k t)")
        kn = [hflat[:, 4 * T:4 * T + NKEY], hflat[:, 6 * T:6 * T + NKEY]]
        b_kn = [[bh[4], bh[5]], [bh[6], bh[7]]]
        vh = [hflat[:, 8 * T:8 * T + NKEY].rearrange("p (b n) -> p b n", b=NKB),
              hflat[:, 10 * T:10 * T + NKEY].rearrange("p (b n) -> p b n", b=NKB)]
        b_vh = [[bh[8], bh[9]], [bh[10], bh[11]]]
        ao = hT[:, 12:16, :]
        b_ao = bh[12:16]
        def stage_a(h):
            i2 = h % 2
            wq, bwq = self.get_w("w_q_up", None, 0, 6, h * 192, 192, hold=True)
            wqi = self.last_w
            wkv, bwkv = self.get_w("w_kv_up", None, 0, 4, h * 256, 256)
            for kg in range(3):
                col, pb = self.psA(1)
                for k in range(4):
                    self.mm(ps[:, col:col + 512], wkv[:, k, 0:128], ckvT[:, k, kg * 512:(kg + 1) * 512],
                            k == 0, k == 3, [bwkv, b_ckvc] + b_ckv, pb)
                S.op("act", pb, b_kn[i2],
                     lambda e: e.activation(kn[i2][:, kg * 512:(kg + 1) * 512], ps[:, col:col + 512], AF.Copy))
            for kq in range(3):
                col, pb = self.psA(1)
                for b4 in range(4):
                    kb = kq * 4 + b4
                    for k in range(4):
                        self.mm(ps[:, col + b4 * 128:col + (b4 + 1) * 128], ckvT[:, k, kb * 128:(kb + 1) * 128],
                                wkv[:, k, 128:256], k == 0, k == 3,
                                [bwkv, b_ckvc] + b_ckv, pb)
                S.op("dve", pb, b_vh[i2],
                     lambda e: e.tensor_copy(vh[i2][:, kq * 4:(kq + 1) * 4, :],
                                             ps[:, col:col + 512].rearrange("p (b n) -> p b n", b=4)))
            col, pb = self.lin_fm(wq, bwq, 6, 0, 128, lambda k, tg: qlat[:, k, tg * 512:(tg + 1) * 512],
                                  lambda k: [b_qlat[k]])
            S.op("dve", pb + [b_rq], [b_qn[i2]],
                 lambda e: e.tensor_tensor(qn[i2], ps[:, col:col + T], rq[:, :], ALU.mult))
            col, pb = self.lin_fm(wq, bwq, 6, 128, 64, lambda k, tg: qlat[:, k, tg * 512:(tg + 1) * 512],
                                  lambda k: [b_qlat[k]])
            self.rel_w(wqi)
            self.rope_chunk(col, pb, 64, 1.0, 1.0, F[0], bF[0], F[1], bF[1])
            S.op("dve", [bF[0], b_rq], [b_qr[i2]],
                 lambda e: e.tensor_tensor(qr[i2], F[0][0:64, :], rq[0:64, :], ALU.mult))
            self.ada_more(2)

        def stage_b(h):
            i2 = h % 2
            hk = h % 4
            units = []
            for qg in range(NQG):
                units.append(dict(
                    parts=[(lambda kb, i2=i2: kn[i2][:, kb * 128:(kb + 1) * 128],
                            qn[i2][:, qg * QG:(qg + 1) * QG],
                            lambda kb, i2=i2: b_kn[i2] + [b_qn[i2]]),
                           (lambda kb: kropeT[0:64, kb * 128:(kb + 1) * 128],
                            qr[i2][:, qg * QG:(qg + 1) * QG],
                            lambda kb, i2=i2: [b_krope, b_kropec, b_qr[i2]])],
                    scale=MLA_SCALE, qg=qg,
                    v_fn=lambda kb, c, i2=i2: vh[i2][:, kb, :],
                    v_reads=lambda kb, i2=i2: b_vh[i2], ndv=1,
                    fin=lambda col, pb, qg=qg, hk=hk: self.softmax_fin(
                        col, pb, 1, [ao[:, hk, qg * QG:(qg + 1) * QG]], [b_ao[hk]])))
            self.run_units(units)
            if hk == 3:
                g = h // 4
                wl = []
                for i in range(2):
                    wl.append(self.get_w("w_out_ab", None, g * 512 + i * 256, 2, 0, D, hold=(i == 0)))
                    if i == 0:
                        w0i = self.last_w
                self.resid_partial(lambda k: (wl[k // 2][0], k % 2, wl[k // 2][1]), 4,
                                   lambda k, tg: ao[:, k, tg * 512:(tg + 1) * 512], lambda k: [b_ao[k]], 0, 2)
                self.rel_w(w0i)

        stage_a(0)
        for h in range(8):
            if h + 1 < 8:
                stage_a(h + 1)
            stage_b(h)
        self.ada_more(2)
        if self.hoist:
            self.norm_mod(0, 1, F[0], bF[0], [F[1], F[2]], [bF[1], bF[2]],
                          sq + [qlat[:, 0, :], qlat[:, 1, :], qlat[:, 2, :]], b_sq + b_qlat[0:3])
            self.pre_normed.add(("ffn", 0))

    def ffn(self, l):
        S = self.S
        ps = self.ps
        self.arena_reset()
        F = [self.aalloc("F%d" % i, [128, T], F32) for i in range(8)]
        bF = [Buf("F%d" % i) for i in range(8)]
        act = [self.aalloc("act%d" % i, [128, 4, T], BF16) for i in range(2)]
        b_act = [[Buf("act%d_%d" % (i, j)) for j in range(4)] for i in range(2)]
        sq = [self.aalloc("sq%d" % i, [128, T], BF16) for i in range(2)]
        b_sq = [Buf("sq0"), Buf("sq1")]
        if ("ffn", l) not in self.pre_normed:
            self.norm_mod(l, 1, F[0], bF[0], [F[1], F[2]], [bF[1], bF[2]], sq, b_sq)
        self.ada_to(l * 48 + 48)

        def conv_chunk(col, pb, fc, dst, bdst):
            w0 = self.P("wconv", 1, (l * 3 + 0) * 88 + fc)
            w1 = self.P("wconv", 1, (l * 3 + 1) * 88 + fc)
            w2 = self.P("wconv", 1, (l * 3 + 2) * 88 + fc)
            bb = self.P("bconv", 1, l * 88 + fc)
            S.op("act", pb + [self.b_prm], [bdst],
                 lambda e: e.activation(dst[:, :], ps[:, col:col + T], AF.Identity, bias=bb, scale=w1))
            S.op("dve", pb + [self.b_prm, bdst], [bdst],
                 lambda e: e.scalar_tensor_tensor(dst[:, 1:T], ps[:, col:col + T - 1], w0, dst[:, 1:T],
                                                  ALU.mult, ALU.add))
            S.op("dve", pb + [self.b_prm, bdst], [bdst],
                 lambda e: e.scalar_tensor_tensor(dst[:, 0:T - 1], ps[:, col + 1:col + T], w2, dst[:, 0:T - 1],
                                                  ALU.mult, ALU.add))
            f0 = self.fix[:, l, 0, fc:fc + 1]
            f2 = self.fix[:, l, 1, fc:fc + 1]
            S.op("dve", pb + [self.b_fix, bdst], [bdst],
                 lambda e: e.scalar_tensor_tensor(dst[:, 256:T:256], ps[:, col + 255:col + T - 1:256], f0,
                                                  dst[:, 256:T:256], ALU.mult, ALU.add))
            S.op("dve", pb + [self.b_fix, bdst], [bdst],
                 lambda e: e.scalar_tensor_tensor(dst[:, 255:T - 1:256], ps[:, col + 256:col + T:256], f2,
                                                  dst[:, 255:T - 1:256], ALU.mult, ALU.add))

        fic = [0]

        def up(g):
            ai = g % 2
            for half in range(2):
                wg, bwg = self.get_w("w_up", l, 0, KC, g * 512 + half * 256, 256)
                sg = []
                for j in range(2):
                    fc = g * 4 + half * 2 + j
                    col, pb = self.lin_fm(wg, bwg, KC, j * 128, 128, self.h_rhs, self.h_bufs)
                    f, bf = F[fic[0] % 8], bF[fic[0] % 8]
                    fic[0] += 1
                    conv_chunk(col, pb, fc, f, bf)
                    S.op("act", [bf], [bf], lambda e: e.activation(f[:, :], f[:, :], AF.Silu))
                    sg.append((f, bf))
                if l == 0 and half == 0:
                    self.ada_more(1)
                wv, bwv = self.get_w("w_up", l, 0, KC, DFF + g * 512 + half * 256, 256)
                for j in range(2):
                    fc = g * 4 + half * 2 + j
                    col, pb = self.lin_fm(wv, bwv, KC, j * 128, 128, self.h_rhs, self.h_bufs)
                    f, bf = F[fic[0] % 8], bF[fic[0] % 8]
                    fic[0] += 1
                    conv_chunk(col, pb, 44 + fc, f, bf)
                    a = act[ai][:, half * 2 + j, :]
                    S.op("dve", [bf, sg[j][1]], [b_act[ai][half * 2 + j]],
                         lambda e: e.tensor_tensor(a, f[:, :], sg[j][0][:, :], ALU.mult))
                if l == 0 and half == 1:
                    self.ada_more(1)

        def down(g):
            ai = g % 2
            wl = []
            for i in range(2):
                wl.append(self.get_w("w_down", l, g * 512 + i * 256, 2, 0, D, hold=(i == 0)))
                if i == 0:
                    w0i = self.last_w
            self.resid_partial(lambda k: (wl[k // 2][0], k % 2, wl[k // 2][1]), 4,
                               lambda k, tg: act[ai][:, k, tg * 512:(tg + 1) * 512],
                               lambda k: [b_act[ai][k]], l, 5, alt=True)
            self.rel_w(w0i)
            if l == 0:
                self.ada_more(1)

        up(0)
        for g in range(11):
            if g + 1 < 11:
                up(g + 1)
            down(g)
        if l == 0:
            self.ada_to(96)
        if self.hoist:
            if l == 0:
                self.norm_mod(1, 0, F[0], bF[0], [F[1], F[2]], [bF[1], bF[2]], sq, b_sq)
                self.pre_normed.add(("l1", 0))
            else:
                self.final_norm(F[0:5], bF[0:5], sq, b_sq)

    def layer1_attn(self):
        S = self.S
        ps = self.ps
        d = self.d
        self.arena_reset()
        self.common_attn_arena()
        KhT = self.aalloc("KhT", [128, 2, NKEY], BF16)
        Vh = self.aalloc("Vh", [128, NKB, 256], BF16)
        QhT = [self.aalloc("QhT%d" % i, [128, 2, T], BF16) for i in range(2)]
        F = [self.aalloc("F%d" % i, [128, T], F32) for i in range(5)]
        bF = [Buf("F%d" % i) for i in range(5)]
        vt = [self.aalloc("vt%d" % i, [128, 256], F32) for i in range(2)]
        b_vt = [Buf("vt0"), Buf("vt1")]
        sq = [self.aalloc("sq0", [128, T], BF16)]
        b_sq = [Buf("sq0")]
        ao = self.aalloc("ao", [128, 2, T], BF16)
        b_ao = [Buf("ao0"), Buf("ao1")]
        self.rD = [self.aalloc("rD%d" % i, [128, QG], F32) for i in range(3)]
        self.b_rD = [Buf("rD%d" % i) for i in range(3)]
        b_k = [Buf("k0"), Buf("k1")]
        b_kc = [Buf("kc0"), Buf("kc1")]
        b_v = [Buf("v%d" % i) for i in range(NKB)]
        b_q = [[Buf("q0_%d" % i), Buf("q1_%d" % i)] for i in range(2)]
        self.load_rope(128)
        if ("l1", 0) not in self.pre_normed:
            self.norm_mod(1, 0, F[0], bF[0], [F[1], F[2]], [bF[1], bF[2]],
                          sq + [QhT[0][:, 0, :], QhT[0][:, 1, :], QhT[1][:, 0, :]],
                          b_sq + [b_q[0][0], b_q[0][1], b_q[1][0]])
        self.ada_to(48 + 48)
        neglam = self.lamt[:, 5:6]
        oraw = [F[3], F[4]]
        b_oraw = [bF[3], bF[4]]

        def stage_q(h):
            qi = h % 2
            w, bw = self.get_w("w_in_c", None, 0, KC, h * 256, 256)
            for j in range(2):
                col, pb = self.lin_fm(w, bw, KC, j * 128, 128, self.h_rhs, self.h_bufs)
                self.rope_chunk(col, pb, 128, 1.0, 1.0, F[1], bF[1], F[2], bF[2],
                                dst=QhT[qi][:, j, :], dst_bufs=[b_q[qi][j]])

        def stage_k(h):
            for j in range(2):
                S.dma("pool", [], [b_kc[j]], KhT[:, j, T:NKEY], d["dkT"][h * 2 + j])
            w, bw = self.get_w("w_in_c", None, 0, KC, 2048 + h * 256, 256)
            for j in range(2):
                col, pb = self.lin_fm(w, bw, KC, j * 128, 128, self.h_rhs, self.h_bufs)
                self.rope_chunk(col, pb, 128, 1.0, 1.0, F[1], bF[1], F[2], bF[2])
                self.out_fm(d["o_dkT"][(h * 2 + j) * 128:(h * 2 + j + 1) * 128, :], F[1][:, :], [bF[1]])
                S.op("act", [bF[1]], [b_k[j]], lambda e: e.activation(KhT[:, j, 0:T], F[1][:, :], AF.Copy))

        def stage_v(h):
            S.dma("pool", [], b_v[8:12], Vh[:, 8:12, :],
                  d["dv"][:, h * 256:(h + 1) * 256].rearrange("(b p) n -> p b n", p=128))
            w, bw = self.get_w("w_in_c", None, 0, KC, 4096 + h * 256, 256)
            for blk in range(8):
                col, pb = self.lin_tm(w, bw, 0, 256, blk)
                f, bf = vt[blk % 2], b_vt[blk % 2]
                S.op("act", pb, [bf], lambda e: e.activation(f[:, :], ps[:, col:col + 256], AF.Copy))
                S.dma("sp", [bf], [], d["o_dv"][blk * 128:(blk + 1) * 128, h * 256:(h + 1) * 256], f[:, :])
                S.op("dve", pb, [b_v[blk]], lambda e: e.tensor_copy(Vh[:, blk, :], ps[:, col:col + 256]))

        def stage_att(h):
            qi = h % 2
            units = []
            for qg in range(NQG):
                for j in range(2):
                    def fin(col, pb, qg=qg, j=j):
                        r = self.rD[j]
                        br = self.b_rD[j]
                        S.op("dve", [pb[1]], [br], lambda e: e.reciprocal(r[:, :], ps[:, col + 2 * QG:col + 3 * QG]))
                        if j == 1:
                            S.op("dve", [br, self.b_lam], [br],
                                 lambda e: e.tensor_scalar(r[:, :], r[:, :], neglam, None, ALU.mult))
                        for c in range(2):
                            o = oraw[c][:, qg * QG:(qg + 1) * QG]
                            if j == 0:
                                S.op("dve", [pb[0], br], [b_oraw[c]],
                                     lambda e: e.tensor_tensor(o, ps[:, col + c * QG:col + (c + 1) * QG], r[:, :], ALU.mult))
                            else:
                                t2 = self.rD[2]
                                S.op("dve", [pb[0], br], [self.b_rD[2]],
                                     lambda e: e.tensor_tensor(t2[:, :], ps[:, col + c * QG:col + (c + 1) * QG], r[:, :], ALU.mult))
                                S.op("dve", [self.b_rD[2], b_oraw[c]], [b_oraw[c]],
                                     lambda e: e.tensor_tensor(o, o, t2[:, :], ALU.add))
                    units.append(dict(
                        parts=[(lambda kb, j=j: KhT[:, j, kb * 128:(kb + 1) * 128],
                                QhT[qi][:, j, qg * QG:(qg + 1) * QG],
                                lambda kb, j=j: [b_k[j], b_kc[j], b_q[qi][j]])],
                        scale=HEAD_SCALE, qg=qg,
                        v_fn=lambda kb, c: Vh[:, kb, c * 128:(c + 1) * 128],
                        v_reads=lambda kb: [b_v[kb]], ndv=2, fin=fin))
            self.run_units(units)

        def stage_out_a(h):
            c2, pb2 = self.psB(2)
            self.sumsq(lambda c: oraw[c][:, :], lambda c: [b_oraw[c]], 2, sq, b_sq, c2, pb2)
            self.rstd_from_ss(c2, pb2, 256, F[0][:, :], [bF[0]])
            for c in range(2):
                S.op("dve", [b_oraw[c], self.b_lam, bF[0]], [b_ao[c]],
                     lambda e: e.scalar_tensor_tensor(ao[:, c, :], oraw[c][:, :], self.lamt[:, 6 + c:7 + c],
                                                      F[0][:, :], ALU.mult, ALU.mult))

        def stage_out_b(h):
            wo, bwo = self.get_w("w_out_c", None, h * 256, 2, 0, D)
            self.resid_partial(lambda k: (wo, k, bwo), 2,
                               lambda k, tg: ao[:, k, tg * 512:(tg + 1) * 512], lambda k: [b_ao[k]], 1, 2, alt=True)

        stage_q(0)
        stage_k(0)
        stage_v(0)
        for h in range(8):
            stage_att(h)
            if h + 1 < 8:
                stage_q(h + 1)
            stage_out_a(h)
            if h + 1 < 8:
                stage_k(h + 1)
            stage_out_b(h)
            if h + 1 < 8:
                stage_v(h + 1)
        if self.hoist:
            self.norm_mod(1, 1, F[0], bF[0], [F[1], F[2]], [bF[1], bF[2]],
                          sq + [QhT[0][:, 0, :], QhT[0][:, 1, :], QhT[1][:, 0, :]],
                          b_sq + [b_q[0][0], b_q[0][1], b_q[1][0]])
            self.pre_normed.add(("ffn", 1))

    def final_norm(self, F=None, bF=None, sq=None, b_sq=None):
        S = self.S
        d = self.d
        if F is None:
            self.arena_reset()
            F = [self.aalloc("F%d" % i, [128, T], F32) for i in range(5)]
            bF = [Buf("F%d" % i) for i in range(5)]
            sq = [self.aalloc("sq%d" % i, [128, T], BF16) for i in range(2)]
            b_sq = [Buf("sq0"), Buf("sq1")]
        col, pb = self.psB(2)
        self.sumsq(lambda c: self.xT[:, c, :], lambda c: [self.b_x[c]], KC, sq, b_sq, col, pb, dve_alt=True)
        self.rstd_from_ss(col, pb, D, F[0][:, :], [bF[0]])
        for kc in range(KC):
            f, bf = F[1 + kc % 4], bF[1 + kc % 4]
            S.op("dve", [self.b_x[kc], self.b_prm, bF[0]], [bf],
                 lambda e: e.scalar_tensor_tensor(f[:, :], self.xT[:, kc, :], self.P("gfin", 1, kc),
                                                  F[0][:, :], ALU.mult, ALU.mult))
            S.dma("sp", [bf], [], d["yT"][kc * 128:(kc + 1) * 128, :], f[:, :])

    def build(self):
        self.S.dry = True
        self.runid = 0
        self.emit()
        self.runid = 1
        self.wplan = self.wplan_new
        self.S.dry = False
        self.S.reset()
        self.emit()
        return self.nc


def _rope_tables(rot_dim, n_tokens=T, grid_w=64):
    n_rows = n_tokens // grid_w
    row = np.repeat(np.arange(n_rows), grid_w).astype(np.float32)
    col = np.tile(np.arange(grid_w), n_rows).astype(np.float32)
    n_freq = rot_dim // 4
    freqs = (10000.0 ** (-np.arange(n_freq, dtype=np.float32) / n_freq)).astype(np.float32)
    ang = np.concatenate([row[:, None] * freqs, col[:, None] * freqs], axis=-1)
    cos, sin = np.cos(ang).astype(np.float32), np.sin(ang).astype(np.float32)
    C = np.concatenate([cos, cos], axis=1).T
    SN = np.concatenate([-sin, sin], axis=1).T
    return np.ascontiguousarray(np.stack([C, SN]).astype(np.float32))


_PROG = None


def make_in_maps(g):
    f32 = np.float32

    def swap(v):
        v = np.asarray(v, f32)
        h = v.shape[-1] // 2
        return np.concatenate([v[..., h:], v[..., :h]], axis=-1)

    shared = {
        "w_ada": np.ascontiguousarray(g["w_ada"], f32),
        "w_in_ab": np.ascontiguousarray(g["w_in_ab"][0], f32),
        "w_q_up": np.ascontiguousarray(g["w_mla_q_up"][0], f32),
        "w_kv_up": np.ascontiguousarray(g["w_mla_kv_up"][0], f32),
        "w_out_ab": np.ascontiguousarray(g["w_out_ab"][0], f32),
        "w_in_c": np.ascontiguousarray(g["w_in_c"][0], f32),
        "w_out_c": np.ascontiguousarray(g["w_out_c"][0], f32),
        "w_up": np.ascontiguousarray(g["w_ffn_up"], f32),
        "w_down": np.ascontiguousarray(g["w_ffn_down"], f32),
    }
    rope128 = _rope_tables(128)
    rope64 = _rope_tables(64)
    id128 = np.stack([np.ones((128, T), f32), np.zeros((128, T), f32)])
    id64 = np.stack([np.ones((64, T), f32), np.zeros((64, T), f32)])

    def prm_for(cond, maskb, cflag):
        parts = {
            "gmix": _fm(g["g_norm_mix"]), "gffn": _fm(g["g_norm_ffn"]), "gfin": _fm(g["g_final"]),
            "bada": _fm(g["b_ada"]), "gq": _fm(g["g_mla_q"][0]), "gkv": _fm(g["g_mla_kv"][0]),
            "ggq": _fm(g["g_gqa_q"][0]), "ggqs": _fm(swap(g["g_gqa_q"][0])),
            "ggk": _fm(g["g_gqa_k"][0]), "ggks": _fm(swap(g["g_gqa_k"][0])),
            "gsub": _fm(g["g_diff_sub"][0]), "wconv": _fm(g["w_ffn_conv"]), "bconv": _fm(g["b_ffn_conv"]),
            "maskb": np.broadcast_to(maskb.reshape(1, 48), (128, 48)),
            "cflag": np.full((128, 1), cflag, f32),
            "lam": np.broadcast_to(np.concatenate([g["lambda_q1"][0], g["lambda_k1"][0],
                                                   g["lambda_q2"][0], g["lambda_k2"][0]]).reshape(1, 512), (128, 512)),
            "cond": _fm(cond),
        }
        out = np.zeros((128, NP_), f32)
        for n, s in PRM_FIELDS:
            a = np.asarray(parts[n], f32)
            assert a.shape == (128, s), (n, a.shape, s)
            out[:, OFF[n]:OFF[n] + s] = a
        return out

    in_maps = []
    for c in range(8):
        m = dict(shared)
        if c < 2:
            b = c
            m["xT"] = np.ascontiguousarray(g["x_sample"][b].T, f32)
            maskb = np.zeros((NQG, NKB), f32)
            m["prm"] = prm_for(g["c"][b], maskb, 0.0)
            m["rope128"], m["rope64"] = rope128, rope64
            m["ckvT"] = np.ascontiguousarray(g["cache_mla_ckv"][b, 0].T, f32)
            m["kropeT"] = np.ascontiguousarray(g["cache_mla_krope"][b, 0].T, f32)
            m["gkT"] = np.ascontiguousarray(g["cache_gqa_k"][b, 0].transpose(1, 2, 0), f32)
            m["gv"] = np.ascontiguousarray(g["cache_gqa_v"][b, 0].reshape(512, 256), f32)
            m["dkT"] = np.ascontiguousarray(g["cache_diff_k"][b, 0].transpose(1, 2, 3, 0).reshape(16, 128, 512), f32)
            m["dv"] = np.ascontiguousarray(g["cache_diff_v"][b, 0].reshape(512, 2048), f32)
        else:
            pc = c - 2 if c < 6 else 0
            xs = g["x_prompt"][4 * pc:4 * pc + 4].reshape(T, D)
            m["xT"] = np.ascontiguousarray(xs.T, f32)
            maskb = np.full((NQG, NKB), NEG, f32)
            for s in range(4):
                maskb[s, 2 * s] = 0.0
                maskb[s, 2 * s + 1] = 0.0
            m["prm"] = prm_for(g["c_ctx"], maskb, -1.0)
            m["rope128"], m["rope64"] = id128, id64
            m["ckvT"] = np.zeros((512, 512), f32)
            m["kropeT"] = np.zeros((64, 512), f32)
            m["gkT"] = np.zeros((2, 128, 512), f32)
            m["gv"] = np.zeros((512, 256), f32)
            m["dkT"] = np.zeros((16, 128, 512), f32)
            m["dv"] = np.zeros((512, 2048), f32)
        in_maps.append(m)
    return in_maps


def kernel(**inp):
    global _PROG
    f32 = np.float32
    g = {k: np.asarray(v) for k, v in inp.items()}
    if _PROG is None:
        _PROG = Prog().build()
    nc = _PROG
    in_maps = make_in_maps(g)
    res = run_bass_kernel_spmd(nc, in_maps, core_ids=list(range(8)))
    R = res.results
    y_sample = np.stack([R[b]["yT"].T for b in range(2)]).astype(f32)
    y_prompt = np.concatenate([R[2 + pc]["yT"].T.reshape(4, 256, D) for pc in range(4)]).astype(f32)

    def gather(fn):
        return np.concatenate([fn(R[2 + pc]) for pc in range(4)]).astype(f32)

    new_ckv = gather(lambda r: r["o_ckvT"].T.reshape(4, 1, 256, 512))
    new_krope = gather(lambda r: r["o_kropeT"].T.reshape(4, 1, 256, 64))
    new_gk = gather(lambda r: r["o_gkT"].reshape(2, 128, 4, 256).transpose(2, 3, 0, 1).reshape(4, 1, 256, 2, 128))
    new_gv = gather(lambda r: r["o_gv"].reshape(4, 1, 256, 2, 128))
    new_dk = gather(lambda r: r["o_dkT"].reshape(8, 2, 128, 4, 256).transpose(3, 4, 0, 1, 2).reshape(4, 1, 256, 8, 2, 128))
    new_dv = gather(lambda r: r["o_dv"].reshape(4, 1, 256, 8, 256))
    return (np.ascontiguousarray(y_prompt), np.ascontiguousarray(y_sample), np.ascontiguousarray(new_ckv),
            np.ascontiguousarray(new_krope), np.ascontiguousarray(new_gk), np.ascontiguousarray(new_gv),
            np.ascontiguousarray(new_dk), np.ascontiguousarray(new_dv))
```

```python
import math
import numpy as np
import concourse.bass as bass
import concourse.mybir as mybir
from concourse.bass_utils import run_bass_kernel_spmd

F32 = mybir.dt.float32
BF16 = mybir.dt.bfloat16
AF = mybir.ActivationFunctionType
ALU = mybir.AluOpType
AX = mybir.AxisListType

T = 1024
D = 2048
KC = 16
NKEY = 1536
NKB = 12
QG = 256
NQG = 4
DFF = 5632
NFC = 44
EPS = 1e-6
MLA_SCALE = 192 ** -0.5
HEAD_SCALE = 128 ** -0.5
LAMBDA_INIT = 0.8 - 0.6 * math.exp(-0.3 * 1)
NEG = -30000.0
NSLOT = 4
WSLOT_ELEMS = 4096

SAME_ENGINE_SYNC = True


_FENCE = []


class Buf:
    __slots__ = ("name", "w", "r", "excl")

    def __init__(self, name="", excl=False):
        self.name = name
        self.w = None
        self.r = list(_FENCE)
        self.excl = excl


class Tok:
    __slots__ = ("sem", "val", "eng")

    def __init__(self, sem, val, eng):
        self.sem, self.val, self.eng = sem, val, eng


class Sched:
    def __init__(self, nc, n_dma_sems=8):
        self.nc = nc
        self.dry = False
        self.eng = {"pe": nc.tensor, "act": nc.scalar, "dve": nc.vector,
                    "pool": nc.gpsimd, "sp": nc.sync}
        self.sems = {k: [nc.alloc_semaphore("s_%s%d" % (k, i)) for i in range(3)]
                     for k in ("pe", "act", "dve", "pool")}
        self.sem = {k: v[0] for k, v in self.sems.items()}
        self.dsem = {q: [nc.alloc_semaphore("d_%s%d" % (q, i)) for i in range(n_dma_sems)]
                     for q in ("sp", "pool")}
        self.reset()

    def reset(self):
        self.cnt = {k: 0 for k in self.sems}
        self.epoch = {k: 0 for k in self.sems}
        self.sem = {k: v[0] for k, v in self.sems.items()}
        self.last_tok = {k: None for k in self.sems}
        self.seen = {k: {} for k in self.eng}
        self.dval = {q: [0] * len(v) for q, v in self.dsem.items()}
        self.dpos = {q: 0 for q in self.dsem}
        self.n_inst = 0

    def _wait(self, e, tok):
        if tok is None:
            return
        if tok.eng == e:
            if (not SAME_ENGINE_SYNC) or e == "pe":
                return
        key = id(tok.sem)
        if self.seen[e].get(key, 0) >= tok.val:
            return
        self.eng[e].wait_ge(tok.sem, tok.val)
        self.seen[e][key] = tok.val

    def _deps(self, e, reads, writes):
        for b in reads:
            self._wait(e, b.w)
            if b.excl:
                for t in b.r:
                    if t.eng != e:
                        self._wait(e, t)
        for b in writes:
            self._wait(e, b.w)
            for t in b.r:
                self._wait(e, t)

    def _commit(self, tok, reads, writes):
        for b in reads:
            b.r.append(tok)
            if len(b.r) > 16:
                best = {}
                for t in b.r:
                    k = id(t.sem)
                    if k not in best or best[k].val < t.val:
                        best[k] = t
                b.r = list(best.values())
        for b in writes:
            b.w = tok
            b.r = []

    def op(self, e, reads, writes, fn):
        if self.dry:
            return None
        self._deps(e, reads, writes)
        ins = fn(self.eng[e])
        self.cnt[e] += 1
        ins.then_inc(self.sem[e], 1)
        tok = Tok(self.sem[e], self.cnt[e], e)
        self.last_tok[e] = tok
        self._commit(tok, reads, writes)
        if self.cnt[e] >= 20000:
            self.epoch[e] += 1
            self.sem[e] = self.sems[e][self.epoch[e]]
            self.cnt[e] = 0
        self.n_inst += 1
        return tok

    def dma(self, q, reads, writes, out, in_):
        if self.dry:
            return None
        i = self.dpos[q]
        self.dpos[q] = (i + 1) % len(self.dsem[q])
        sem = self.dsem[q][i]
        if self.dval[q][i] > 0:
            self._wait(q, Tok(sem, self.dval[q][i], "dma_" + q))
        self._deps(q, reads, writes)
        ins = self.eng[q].dma_start(out=out, in_=in_)
        self.dval[q][i] += 16
        ins.then_inc(sem, 16)
        tok = Tok(sem, self.dval[q][i], "dma_" + q)
        self._commit(tok, reads, writes)
        self.n_inst += 1
        return tok

    def _all_toks(self):
        toks = []
        for k in self.sems:
            if self.last_tok[k] is not None:
                toks.append(self.last_tok[k])
        for q in self.dsem:
            for i, s in enumerate(self.dsem[q]):
                if self.dval[q][i] > 0:
                    toks.append(Tok(s, self.dval[q][i], "dma_" + q))
        return toks

    def barrier(self, bufs=()):
        if self.dry:
            return
        toks = self._all_toks()
        for e in ("pe", "act", "dve", "pool", "sp"):
            for t in toks:
                if t.eng == e:
                    continue
                self._wait(e, t)

    def finish(self):
        if self.dry:
            return
        for t in self._all_toks():
            self._wait("sp", t)


def _fm(v):
    v = np.asarray(v, np.float32)
    lead = v.shape[:-1]
    n = v.shape[-1] // 128
    v = v.reshape(*lead, n, 128)
    v = np.moveaxis(v, -1, 0)
    return np.ascontiguousarray(v).reshape(128, -1)


PRM_FIELDS = [("gmix", 32), ("gffn", 32), ("gfin", 16), ("bada", 192), ("gq", 6), ("gkv", 4),
              ("ggq", 1), ("ggqs", 1), ("ggk", 1), ("ggks", 1), ("gsub", 2),
              ("wconv", 2 * 3 * 88), ("bconv", 2 * 88), ("maskb", 48), ("cflag", 1),
              ("lam", 512), ("cond", 16)]
OFF = {}
_o = 0
for _n, _s in PRM_FIELDS:
    OFF[_n] = _o
    _o += _s
NP_ = _o


class Prog:
    def __init__(self, stop=99, dbg=False, sub=99):
        self.sub = sub
        self.stop = stop
        self.dbg = dbg
        nc = bass.Bass("TRN2", target_bir_lowering=False)
        self.nc = nc
        dt = nc.dram_tensor
        I = "ExternalInput"
        O = "ExternalOutput"
        self.d = {}
        for name, shape in [("xT", [D, T]), ("prm", [128, NP_]), ("rope128", [2, 128, T]),
                            ("rope64", [2, 64, T]), ("ckvT", [512, 512]), ("kropeT", [64, 512]),
                            ("gkT", [2, 128, 512]), ("gv", [512, 256]), ("dkT", [16, 128, 512]),
                            ("dv", [512, 2048]), ("w_ada", [2, D, 6 * D]), ("w_in_ab", [D, 2880]),
                            ("w_q_up", [768, 1536]), ("w_kv_up", [512, 2048]), ("w_out_ab", [D, D]),
                            ("w_in_c", [D, 6144]), ("w_out_c", [D, D]), ("w_up", [2, D, 2 * DFF]),
                            ("w_down", [2, DFF, D])]:
            self.d[name] = dt(name, shape, F32, kind=I).ap()
        for name, shape in [("yT", [D, T]), ("o_ckvT", [512, T]), ("o_kropeT", [64, T]),
                            ("o_gkT", [256, T]), ("o_gv", [T, 256]), ("o_dkT", [2048, T]),
                            ("o_dv", [T, 2048])]:
            self.d[name] = dt(name, shape, F32, kind=O).ap()
        if dbg:
            self.d["dbg"] = dt("dbg", [128, 4096], F32, kind=O).ap()
        self.S = Sched(nc)
        self.off = 16512
        self.xT = self.alloc("xT", [128, KC, T], F32)
        self.hT = self.alloc("hT", [128, KC, T], BF16)
        self.W = [self.alloc("w%d" % i, [128, WSLOT_ELEMS], BF16) for i in range(NSLOT)]
        self.rope = self.alloc("rope", [128, 2, T], F32)
        self.prm = self.alloc("prm", [128, NP_], F32)
        self.modT = self.alloc("modT", [128, 2, 96], F32)
        self.der = self.alloc("der", [128, 64], F32)
        self.fix = self.alloc("fix", [128, 2, 2, 88], F32)
        self.lamt = self.alloc("lamt", [128, 8], F32)
        self.ones = self.alloc("ones", [128, 128], BF16)
        self.condT = self.alloc("condT", [128, 16], BF16)
        self.arena0 = self.off
        self.arena_end = 229376
        self.ps = nc.alloc_psum_tensor("ps", [128, 4096], F32)
        self.b_x = [Buf("x%d" % i) for i in range(KC)]
        self.b_h = [Buf("h%d" % i) for i in range(KC)]
        self.b_w = [Buf("w%d" % i) for i in range(NSLOT)]
        self.b_rope = Buf("rope")
        self.b_prm = Buf("prm")
        self.b_mod = [[Buf("mod%d_%d" % (l, s)) for s in range(6)] for l in range(2)]
        self.b_der = Buf("der")
        self.b_fix = Buf("fix")
        self.b_lam = Buf("lam")
        self.b_ones = Buf("ones")
        self.b_cond = Buf("cond")
        self.b_ps = [Buf("ps%d" % i, excl=True) for i in range(8)]
        self.wplan = None

    def alloc(self, name, shape, dtype, arena=False):
        nbytes = int(np.prod(shape[1:])) * (4 if dtype == F32 else 2)
        nbytes = (nbytes + 31) // 32 * 32
        t = self.nc.alloc_sbuf_tensor_at(name, shape, dtype, offset=self.off)
        self.off += nbytes
        assert self.off <= 229376, (name, self.off)
        return t

    def arena_reset(self):
        _FENCE[:] = [] if self.S.dry else self.S._all_toks()
        self.off = self.arena0
        self.acount = getattr(self, "acount", 0) + 1

    def aalloc(self, name, shape, dtype):
        t = self.alloc("%s_%d_%d" % (name, self.runid, self.acount), shape, dtype)
        return t

    def _psalloc(self, pool, n):
        p = self.pspos[pool]
        p = (p + n - 1) // n * n
        if p + n > 4:
            p = 0
        self.pspos[pool] = (p + n) % 4
        base = pool * 4 + p
        return base * 512, self.b_ps[base:base + n]

    def psA(self, n):
        return self._psalloc(0, n)

    def psB(self, n):
        return self._psalloc(1, n)

    def get_w(self, name, l, row0, kcn, col0, ncols, hold=False):
        spec = (name, l, row0, kcn, col0, ncols)
        i = self.wi
        self.wi += 1
        self.last_w = i
        if self.S.dry:
            self.wplan_new.append(spec)
            slot = 0
        else:
            assert self.wplan[i] == spec, (i, self.wplan[i], spec)
            for j in self.w_auto:
                self._free_w(j)
            self.w_auto = set()
            if not hold:
                self.w_auto.add(i)
            self._pump()
            assert self.wissued > i, ("weight tile not issued (no free slot)", i, spec)
            slot = self.w_slot[i]
        view = self.W[slot][:, 0:kcn * ncols].rearrange("p (k n) -> p k n", k=kcn)
        return view, self.b_w[slot]

    def _free_w(self, j):
        self.w_free.append(self.w_slot[j])

    def rel_w(self, i):
        if not self.S.dry:
            self._free_w(i)
            self._pump()

    def _pump(self):
        while self.wissued < len(self.wplan) and self.wissued < self.wi + NSLOT - 1 and self.w_free:
            self._issue_w(self.wissued)
            self.wissued += 1

    def _issue_w(self, j):
        name, l, row0, kcn, col0, ncols = self.wplan[j]
        src = self.d[name]
        if l is not None:
            src = src[l]
        src = src[row0:row0 + kcn * 128, col0:col0 + ncols].rearrange("(k p) n -> p k n", p=128)
        slot = self.w_free.pop(0)
        self.w_slot[j] = slot
        dst = self.W[slot][:, 0:kcn * ncols].rearrange("p (k n) -> p k n", k=kcn)
        self.S.dma("pool", [], [self.b_w[slot]], dst, src)

    def mm(self, out, lhsT, rhs, start, stop, reads, writes):
        self.S.op("pe", reads + writes if not start else reads, writes,
                  lambda e: e.matmul(out, lhsT, rhs, start=start, stop=stop))

    def P(self, name, n=None, lo=0):
        o = OFF[name] + lo
        if n is None:
            n = 1
        return self.prm[:, o:o + n]

    def ada_to(self, upto):
        while self.ada_done < min(upto, 96):
            self.ada_tile(self.ada_done)
            self.ada_done += 1

    def ada_more(self, k=1):
        self.ada_to(self.ada_done + k)

    def ada_tile(self, a):
        S = self.S
        l, t = divmod(a, 48)
        w, bw = self.get_w("w_ada", l, 0, KC, t * 256, 256)
        col, pb = self.psA(1)
        ps = self.ps
        for j in range(2):
            for kc in range(KC):
                self.mm(ps[:, col + j:col + j + 1], w[:, kc, j * 128:(j + 1) * 128],
                        self.condT[:, kc:kc + 1], kc == 0, kc == KC - 1, [bw, self.b_cond], pb)
        sec = (t * 2) // 16
        c0 = t * 2
        S.op("dve", pb + [self.b_prm], [self.b_mod[l][sec]],
             lambda e: e.tensor_tensor(self.modT[:, l, c0:c0 + 2], ps[:, col:col + 2],
                                       self.P("bada", 2, l * 96 + c0), ALU.add))

    def rstd_from_ss(self, col, pb, n, dst, dst_bufs, width=T):
        S = self.S
        ps = self.ps
        S.op("act", pb, dst_bufs,
             lambda e: e.activation(dst, ps[:, col:col + width], AF.Ln, bias=EPS, scale=1.0 / n))
        S.op("act", dst_bufs, dst_bufs, lambda e: e.activation(dst, dst, AF.Exp, scale=-0.5))

    def sumsq(self, src_fn, src_bufs_fn, nchunks, sqs, sq_bufs, col, pb, first=True, last=True, base=0, total=None, dve_alt=False):
        S = self.S
        ps = self.ps
        total = nchunks if total is None else total
        for c in range(nchunks):
            sq = sqs[c % len(sqs)]
            sb = sq_bufs[c % len(sqs)]
            src = src_fn(c)
            if dve_alt and c % 2 == 1:
                S.op("dve", src_bufs_fn(c), [sb], lambda e: e.tensor_tensor(sq[:, :], src, src, ALU.mult))
            else:
                S.op("act", src_bufs_fn(c), [sb], lambda e: e.activation(sq[:, :], src, AF.Square))
            for tg in range(2):
                self.mm(ps[:, col + tg * 512:col + (tg + 1) * 512], self.ones[:, :],
                        sq[:, tg * 512:(tg + 1) * 512], (base + c) == 0, (base + c) == total - 1,
                        [sb, self.b_ones], [pb[tg]])

    def norm_mod(self, l, which, rstd, b_rstd, tmp, b_tmp, sqs, b_sqs):
        S = self.S
        self.ada_to(l * 48 + (16 if which == 0 else 40))
        sec_sh, sec_sc = 3 * which, 3 * which + 1
        gname = "gmix" if which == 0 else "gffn"
        gs = self.der[:, 0:16]
        S.op("dve", [self.b_mod[l][sec_sc], self.b_prm], [self.b_der],
             lambda e: e.scalar_tensor_tensor(gs, self.modT[:, l, sec_sc * 16:(sec_sc + 1) * 16], 1.0,
                                              self.P(gname, 16, l * 16), ALU.add, ALU.mult))
        col, pb = self.psB(2)
        self.sumsq(lambda c: self.xT[:, c, :], lambda c: [self.b_x[c]], KC, sqs, b_sqs, col, pb, dve_alt=True)
        self.rstd_from_ss(col, pb, D, rstd[:, :], [b_rstd])
        for kc in range(KC):
            t_ = tmp[kc % len(tmp)]
            bt = b_tmp[kc % len(tmp)]
            S.op("dve", [self.b_x[kc], self.b_der, b_rstd], [bt],
                 lambda e: e.scalar_tensor_tensor(t_[:, :], self.xT[:, kc, :], self.der[:, kc:kc + 1],
                                                  rstd[:, :], ALU.mult, ALU.mult))
            S.op("act", [bt, self.b_mod[l][sec_sh]], [self.b_h[kc]],
                 lambda e: e.activation(self.hT[:, kc, :], t_[:, :], AF.Identity,
                                        bias=self.modT[:, l, sec_sh * 16 + kc:sec_sh * 16 + kc + 1], scale=1.0))

    def ps_alt(self, n):
        self.alt_i ^= 1
        return self.psA(n) if self.alt_i else self.psB(n)

    def lin_fm(self, w, bw, kcn, c0, M, rhs_fn, rhs_bufs_fn, alt=False):
        col, pb = self.ps_alt(2) if alt else self.psA(2)
        ps = self.ps
        for k in range(kcn):
            for tg in range(2):
                self.mm(ps[0:M, col + tg * 512:col + (tg + 1) * 512], w[:, k, c0:c0 + M],
                        rhs_fn(k, tg), k == 0, k == kcn - 1, [bw] + rhs_bufs_fn(k), [pb[tg]])
        return col, pb

    def h_rhs(self, k, tg):
        return self.hT[:, k, tg * 512:(tg + 1) * 512]

    def h_bufs(self, k):
        return [self.b_h[k]]

    def lin_tm(self, w, bw, c0, n, blk):
        col, pb = self.psA(1)
        ps = self.ps
        for k in range(KC):
            self.mm(ps[:, col:col + n], self.hT[:, k, blk * 128:(blk + 1) * 128], w[:, k, c0:c0 + n],
                    k == 0, k == KC - 1, [bw, self.b_h[k]], pb)
        return col, pb

    def rope_chunk(self, col, pb, M, g, gsw, t1, b_t1, t2, b_t2, dst=None, dst_bufs=None):
        S = self.S
        ps = self.ps
        h = M // 2
        C = self.rope[0:M, 0, :]
        SN = self.rope[0:M, 1, :]
        rd = pb + [self.b_rope, self.b_prm]
        S.op("dve", rd, [b_t1],
             lambda e: e.scalar_tensor_tensor(t1[0:M, :], ps[0:M, col:col + T], g, C, ALU.mult, ALU.mult))
        gl = gsw[0:h] if not isinstance(gsw, float) else gsw
        gh = gsw[h:M] if not isinstance(gsw, float) else gsw
        S.op("dve", rd, [b_t2],
             lambda e: e.scalar_tensor_tensor(t2[0:h, :], ps[h:M, col:col + T], gl, SN[0:h, :],
                                              ALU.mult, ALU.mult))
        S.op("dve", rd, [b_t2],
             lambda e: e.scalar_tensor_tensor(t2[h:M, :], ps[0:h, col:col + T], gh, SN[h:M, :],
                                              ALU.mult, ALU.mult))
        if dst is None:
            S.op("dve", [b_t1, b_t2], [b_t1],
                 lambda e: e.tensor_tensor(t1[0:M, :], t1[0:M, :], t2[0:M, :], ALU.add))
        else:
            S.op("dve", [b_t1, b_t2], dst_bufs,
                 lambda e: e.tensor_tensor(dst, t1[0:M, :], t2[0:M, :], ALU.add))

    def attn_score_tile(self, parts, scale, qg, pi, kb):
        S = self.S
        ps = self.ps
        col, pb = self.psA(1)
        for i, (lf, rhs, rf) in enumerate(parts):
            self.mm(ps[:, col:col + QG], lf(kb), rhs, i == 0, i == len(parts) - 1, rf(kb), pb)
        mb = self.P("maskb", 1, qg * NKB + kb)
        pt = self.PT[pi][:, kb, :]
        S.op("act", pb + [self.b_prm], [self.b_pt[pi][kb]],
             lambda e: e.activation(pt, ps[:, col:col + QG], AF.Exp, bias=mb, scale=scale))

    def attn_pv_list(self, pi, v_fn, v_reads_fn, ndv):
        col, pb = self.psB(2)
        ps = self.ps
        lst = []
        for c in range(ndv):
            for kb in range(NKB):
                lst.append(lambda c=c, kb=kb: self.mm(
                    ps[:, col + c * QG:col + (c + 1) * QG], v_fn(kb, c), self.PT[pi][:, kb, :],
                    kb == 0, kb == NKB - 1, v_reads_fn(kb) + [self.b_pt[pi][kb]], [pb[0]]))
        for kb in range(NKB):
            lst.append(lambda kb=kb: self.mm(
                ps[:, col + 2 * QG:col + 3 * QG], self.ones[:, :], self.PT[pi][:, kb, :],
                kb == 0, kb == NKB - 1, [self.b_ones, self.b_pt[pi][kb]], [pb[1]]))
        return col, pb, lst

    def run_units(self, units):
        prev = None
        for u in list(units) + [None]:
            lst = []
            if prev is not None:
                pu, ppi = prev
                col, pb, lst = self.attn_pv_list(ppi, pu["v_fn"], pu["v_reads"], pu["ndv"])
            if u is not None:
                pi = self.pt_i
                self.pt_i ^= 1
                per = (len(lst) + NKB - 1) // NKB
                for kb in range(NKB):
                    self.attn_score_tile(u["parts"], u["scale"], u["qg"], pi, kb)
                    for f in lst[kb * per:(kb + 1) * per]:
                        f()
                for f in lst[NKB * per:]:
                    f()
                self.unit_ctr += 1
                if self.unit_ctr % 2 == 0:
                    self.ada_more(1)
            else:
                for f in lst:
                    f()
            if prev is not None:
                pu["fin"](col, pb)
            prev = (u, pi) if u is not None else None

    def softmax_fin(self, col, pb, ndv, outs, out_bufs):
        S = self.S
        ps = self.ps
        r = self.rD[self.rd_i % len(self.rD)]
        br = self.b_rD[self.rd_i % len(self.rD)]
        self.rd_i += 1
        S.op("dve", [pb[1]], [br], lambda e: e.reciprocal(r[:, :], ps[:, col + 2 * QG:col + 3 * QG]))
        for c in range(ndv):
            o = outs[c]
            S.op("dve", [pb[0], br], [out_bufs[c]],
                 lambda e: e.tensor_tensor(o, ps[:, col + c * QG:col + (c + 1) * QG], r[:, :], ALU.mult))

    def resid_partial(self, wl, kcn, rhs_fn, rhs_bufs_fn, l, sec, alt=False):
        S = self.S
        ps = self.ps
        for oc in range(KC):
            col, pb = self.ps_alt(2) if alt else self.psB(2)
            for k in range(kcn):
                w, ki, bw = wl(k)
                for tg in range(2):
                    self.mm(ps[:, col + tg * 512:col + (tg + 1) * 512], w[:, ki, oc * 128:(oc + 1) * 128],
                            rhs_fn(k, tg), k == 0, k == kcn - 1, [bw] + rhs_bufs_fn(k), [pb[tg]])
            gate = self.modT[:, l, sec * 16 + oc:sec * 16 + oc + 1]
            xs = self.xT[:, oc, :]
            S.op("dve", pb + [self.b_mod[l][sec], self.b_x[oc]], [self.b_x[oc]],
                 lambda e: e.scalar_tensor_tensor(xs, ps[:, col:col + T], gate, xs, ALU.mult, ALU.add))

    def emit(self):
        S = self.S
        nc = self.nc
        d = self.d
        ps = self.ps
        _FENCE[:] = []
        self.pspos = [0, 0]
        self.wi = 0
        self.wissued = 0
        self.w_free = list(range(NSLOT))
        self.w_slot = {}
        self.w_auto = set()
        self.ada_done = 0
        self.pt_i = 0
        self.alt_i = 0
        self.unit_ctr = 0
        self.rd_i = 0
        self.acount = 0
        self.wplan_new = []
        for bl in (self.b_x, self.b_h, self.b_w, self.b_ps, [self.b_rope, self.b_prm, self.b_der, self.b_fix,
                                                            self.b_lam, self.b_ones, self.b_cond],
                   self.b_mod[0], self.b_mod[1]):
            for b in bl:
                b.w = None
                b.r = []
        self.off = self.arena0

        S.dma("sp", [], [self.b_prm], self.prm[:, :], d["prm"])
        xsrc = d["xT"].rearrange("(k p) t -> p k t", p=128)
        for g in range(4):
            S.dma("sp", [], self.b_x[4 * g:4 * g + 4], self.xT[:, 4 * g:4 * g + 4, :], xsrc[:, 4 * g:4 * g + 4, :])
        S.op("dve", [], [self.b_ones], lambda e: e.memset(self.ones[:, :], 1.0))
        S.op("act", [self.b_prm], [self.b_cond],
             lambda e: e.activation(self.condT[:, :], self.P("cond", 16), AF.Silu))
        self.lamtmp = self.aalloc("lamtmp", [128, 128], F32)
        for i in range(2):
            S.op("dve", [self.b_prm], [self.b_lam],
                 lambda e: e.tensor_tensor(self.lamtmp[:, :], self.P("lam", 128, 256 * i),
                                           self.P("lam", 128, 256 * i + 128), ALU.mult))
            S.op("dve", [self.b_lam], [self.b_lam],
                 lambda e: e.reduce_sum(self.lamt[:, i:i + 1], self.lamtmp[:, :], AX.X))
        S.op("act", [self.b_lam], [self.b_lam],
             lambda e: e.activation(self.lamt[:, 2:4], self.lamt[:, 0:2], AF.Exp))
        S.op("dve", [self.b_lam], [self.b_lam],
             lambda e: e.tensor_tensor(self.lamt[:, 4:5], self.lamt[:, 3:4], self.lamt[:, 2:3], ALU.subtract))
        S.op("dve", [self.b_lam], [self.b_lam],
             lambda e: e.tensor_scalar(self.lamt[:, 5:6], self.lamt[:, 4:5], -LAMBDA_INIT, None, ALU.add))
        S.op("dve", [self.b_prm], [self.b_lam],
             lambda e: e.tensor_scalar(self.lamt[:, 6:8], self.P("gsub", 2), 1.0 - LAMBDA_INIT, None, ALU.mult))
        for l in range(2):
            for i, tap in enumerate((0, 2)):
                S.op("dve", [self.b_prm], [self.b_fix],
                     lambda e: e.tensor_scalar(self.fix[:, l, i, :], self.P("wconv", 88, (l * 3 + tap) * 88),
                                               self.P("cflag", 1), None, ALU.mult))
        self.ada_to(16)

        self.pre_normed = set()
        self.hoist = (self.stop >= 99)
        phases = [self.layer0_gqa, self.layer0_mla, lambda: self.ffn(0), self.layer1_attn,
                  lambda: self.ffn(1), lambda: (None if self.hoist else self.final_norm())]
        for i, ph in enumerate(phases):
            if i < self.stop:
                ph()
        if self.dbg:
            S.barrier()
            S.dma("sp", [], [], d["dbg"][:, 0:192], self.modT[:, :, :].rearrange("p l c -> p (l c)"))
            S.dma("sp", [], [], d["dbg"][:, 192:200], self.lamt[:, :])
            S.dma("sp", [], [], d["dbg"][:, 256:256 + 352], self.fix[:, :, :, :].rearrange("p a b c -> p (a b c)"))
            for kc in range(4):
                S.dma("sp", [], [], d["dbg"][:, 1024 + kc * 512:1024 + (kc + 1) * 512], self.xT[:, kc, 0:512])
        S.finish()

    def common_attn_arena(self):
        self.PT = [self.aalloc("pt%d" % i, [128, NKB, QG], BF16) for i in range(2)]
        self.b_pt = [[Buf("pt%d_%d" % (i, k)) for k in range(NKB)] for i in range(2)]

    def load_rope(self, which):
        S = self.S
        if which == 128:
            S.dma("sp", [], [self.b_rope], self.rope[:, :, :], self.d["rope128"].rearrange("c p t -> p c t"))
        else:
            S.dma("sp", [], [self.b_rope], self.rope[0:64, :, :], self.d["rope64"].rearrange("c p t -> p c t"))

    def out_fm(self, dram_rows_ap, src, src_bufs):
        self.S.dma("sp", src_bufs, [], dram_rows_ap, src)

    def layer0_gqa(self):
        S = self.S
        ps = self.ps
        d = self.d
        self.arena_reset()
        self.common_attn_arena()
        kbT = self.aalloc("kbT", [128, 2, NKEY], BF16)
        vb = self.aalloc("vb", [128, NKB, 256], BF16)
        F = [self.aalloc("F%d" % i, [128, T], F32) for i in range(5)]
        bF = [Buf("F%d" % i) for i in range(5)]
        qb = [self.aalloc("qb%d" % i, [128, T], BF16) for i in range(2)]
        b_qb = [Buf("qb%d" % i) for i in range(2)]
        sq = [self.aalloc("sq0", [128, T], BF16)]
        b_sq = [Buf("sq0")]
        ao = self.aalloc("ao", [128, 4, T], BF16)
        b_ao = [Buf("ao%d" % i) for i in range(4)]
        self.rD = [self.aalloc("rD%d" % i, [128, QG], F32) for i in range(2)]
        self.b_rD = [Buf("rD%d" % i) for i in range(2)]
        b_kb = [Buf("kb0"), Buf("kb1")]
        b_vb = [Buf("vb%d" % i) for i in range(NKB)]

        import os
        skip = os.environ.get("DBG_SKIP", "")
        if "rope" not in skip:
            self.load_rope(128)
        if "kv" not in skip:
            for g in range(2):
                S.dma("pool", [], [b_kb[g]], kbT[:, g, T:NKEY], d["gkT"][g])
            S.dma("pool", [], b_vb[8:12], vb[:, 8:12, :], d["gv"].rearrange("(b p) n -> p b n", p=128))

        if self.sub == 0:
            return
        self.norm_mod(0, 0, F[0], bF[0], [F[1], F[2]], [bF[1], bF[2]], sq + qb, b_sq + b_qb)
        rstd, b_rstd = F[0], bF[0]
        if self.sub == 1:
            return

        def normrope_head(col, pb, gname, gsname, dst_bf, dst_bufs, out_rows=None):
            c2, pb2 = self.psB(2)
            self.sumsq(lambda c: ps[:, col:col + T], lambda c: pb, 1, sq, b_sq, c2, pb2)
            self.rstd_from_ss(c2, pb2, 128, F[3][:, :], [bF[3]])
            self.rope_chunk(col, pb, 128, self.P(gname), self.P(gsname), F[1], bF[1], F[2], bF[2])
            if out_rows is None:
                S.op("dve", [bF[1], bF[3]], dst_bufs,
                     lambda e: e.tensor_tensor(dst_bf, F[1][:, :], F[3][:, :], ALU.mult))
            else:
                S.op("dve", [bF[1], bF[3]], [bF[4]],
                     lambda e: e.tensor_tensor(F[4][:, :], F[1][:, :], F[3][:, :], ALU.mult))
                self.out_fm(out_rows, F[4][:, :], [bF[4]])
                S.op("act", [bF[4]], dst_bufs, lambda e: e.activation(dst_bf, F[4][:, :], AF.Copy))

        w, bw = self.get_w("w_in_ab", None, 0, KC, 2368, 256)
        for g in range(2):
            col, pb = self.lin_fm(w, bw, KC, g * 128, 128, self.h_rhs, self.h_bufs)
            normrope_head(col, pb, "ggk", "ggks", kbT[:, g, 0:T], [b_kb[g]],
                          out_rows=d["o_gkT"][g * 128:(g + 1) * 128, :])
        self.ada_more(2)
        if self.sub == 2:
            return
        w, bw = self.get_w("w_in_ab", None, 0, KC, 2624, 256)
        for blk in range(8):
            col, pb = self.lin_tm(w, bw, 0, 256, blk)
            f = F[4] if blk % 2 == 0 else F[2]
            bf = bF[4] if blk % 2 == 0 else bF[2]
            S.op("act", pb, [bf], lambda e: e.activation(f[:, 0:256], ps[:, col:col + 256], AF.Copy))
            S.dma("sp", [bf], [], d["o_gv"][blk * 128:(blk + 1) * 128, :], f[:, 0:256])
            S.op("dve", pb, [b_vb[blk]], lambda e: e.tensor_copy(vb[:, blk, :], ps[:, col:col + 256]))
        self.ada_more(2)
        if self.sub == 3:
            return
        st = {}

        def stage_a(hh):
            g, half, j = hh // 4, (hh % 4) // 2, hh % 2
            if j == 0:
                st["w"] = self.get_w("w_in_ab", None, 0, KC, 1344 + g * 512 + half * 256, 256, hold=True)
                st["wi"] = self.last_w
            w, bw = st["w"]
            col, pb = self.lin_fm(w, bw, KC, j * 128, 128, self.h_rhs, self.h_bufs)
            normrope_head(col, pb, "ggq", "ggqs", qb[hh % 2][:, :], [b_qb[hh % 2]])
            if j == 1:
                self.rel_w(st["wi"])
                self.ada_more(2)

        def stage_b(hh):
            g, h4 = hh // 4, hh % 4
            q = qb[hh % 2]
            bq = b_qb[hh % 2]
            units = []
            for qg in range(NQG):
                units.append(dict(
                    parts=[(lambda kb, g=g: kbT[:, g, kb * 128:(kb + 1) * 128],
                            q[:, qg * QG:(qg + 1) * QG],
                            lambda kb, g=g, bq=bq: [b_kb[g], bq])],
                    scale=HEAD_SCALE, qg=qg,
                    v_fn=lambda kb, c, g=g: vb[:, kb, g * 128:(g + 1) * 128],
                    v_reads=lambda kb: [b_vb[kb]], ndv=1,
                    fin=lambda col, pb, qg=qg, h4=h4: self.softmax_fin(
                        col, pb, 1, [ao[:, h4, qg * QG:(qg + 1) * QG]], [b_ao[h4]])))
            self.run_units(units)
            if h4 == 3:
                self.ada_to(24)
                wl = []
                for i in range(2):
                    wl.append(self.get_w("w_out_ab", None, 1024 + g * 512 + i * 256, 2, 0, D, hold=(i == 0)))
                    if i == 0:
                        w0i = self.last_w
                self.resid_partial(lambda k: (wl[k // 2][0], k % 2, wl[k // 2][1]), 4,
                                   lambda k, tg: ao[:, k, tg * 512:(tg + 1) * 512], lambda k: [b_ao[k]], 0, 2)
                self.rel_w(w0i)

        stage_a(0)
        for hh in range(8):
            if hh + 1 < 8:
                stage_a(hh + 1)
            stage_b(hh)
        self.ada_more(2)

    def layer0_mla(self):
        S = self.S
        ps = self.ps
        d = self.d
        self.arena_reset()
        self.common_attn_arena()
        ckvT = self.aalloc("ckvT", [128, 4, NKEY], BF16)
        kropeT = self.aalloc("kropeT", [128, NKEY], BF16)
        qlat = self.aalloc("qlat", [128, 6, T], BF16)
        F = [self.aalloc("F%d" % i, [128, T], F32) for i in range(5)]
        bF = [Buf("F%d" % i) for i in range(5)]
        sq = [self.aalloc("sq0", [128, T], BF16)]
        b_sq = [Buf("sq0")]
        self.rD = [self.aalloc("rD%d" % i, [128, QG], F32) for i in range(2)]
        self.b_rD = [Buf("rD%d" % i) for i in range(2)]
        b_ckv = [Buf("ckv%d" % i) for i in range(4)]
        b_ckvc = Buf("ckvc")
        b_krope = Buf("krope")
        b_kropec = Buf("kropec")
        b_qlat = [Buf("qlat%d" % i) for i in range(6)]

        self.load_rope(64)
        S.dma("pool", [], [b_ckvc], ckvT[:, :, T:NKEY], d["ckvT"].rearrange("(k p) s -> p k s", p=128))
        S.dma("pool", [], [b_kropec], kropeT[0:64, T:NKEY], d["kropeT"])

        c2, pb2 = self.psB(2)
        for i in range(2):
            w, bw = self.get_w("w_in_ab", None, 0, KC, 768 + i * 256, 256)
            for j in range(2):
                c = i * 2 + j
                col, pb = self.lin_fm(w, bw, KC, j * 128, 128, self.h_rhs, self.h_bufs)
                S.op("act", pb + [self.b_prm], [bF[c]],
                     lambda e: e.activation(F[c][:, :], ps[:, col:col + T], AF.Identity, scale=self.P("gkv", 1, c)))
                self.sumsq(lambda _c: ps[:, col:col + T], lambda _c: pb, 1, sq, b_sq, c2, pb2, base=c, total=4)
            self.ada_more(2)
        self.rstd_from_ss(c2, pb2, 512, F[4][:, :], [bF[4]])
        for c in range(4):
            S.op("dve", [bF[c], bF[4]], [bF[c]],
                 lambda e: e.tensor_tensor(F[c][:, :], F[c][:, :], F[4][:, :], ALU.mult))
            self.out_fm(d["o_ckvT"][c * 128:(c + 1) * 128, :], F[c][:, :], [bF[c]])
            S.op("act", [bF[c]], [b_ckv[c]], lambda e: e.activation(ckvT[:, c, 0:T], F[c][:, :], AF.Copy))
        w, bw = self.get_w("w_in_ab", None, 0, KC, 1280, 64)
        col, pb = self.lin_fm(w, bw, KC, 0, 64, self.h_rhs, self.h_bufs)
        self.rope_chunk(col, pb, 64, 1.0, 1.0, F[4], bF[4], F[0], bF[0])
        self.out_fm(d["o_kropeT"][:, :], F[4][0:64, :], [bF[4]])
        S.op("act", [bF[4]], [b_krope], lambda e: e.activation(kropeT[0:64, 0:T], F[4][0:64, :], AF.Copy))
        self.ada_more(2)
        c2, pb2 = self.psB(2)
        for i in range(3):
            w, bw = self.get_w("w_in_ab", None, 0, KC, i * 256, 256)
            for j in range(2):
                c = i * 2 + j
                col, pb = self.lin_fm(w, bw, KC, j * 128, 128, self.h_rhs, self.h_bufs)
                S.op("dve", pb + [self.b_prm], [b_qlat[c]],
                     lambda e: e.tensor_scalar(qlat[:, c, :], ps[:, col:col + T], self.P("gq", 1, c), None, ALU.mult))
                self.sumsq(lambda _c: ps[:, col:col + T], lambda _c: pb, 1, sq, b_sq, c2, pb2, base=c, total=6)
            self.ada_more(2)
        rq, b_rq = F[3], bF[3]
        self.rstd_from_ss(c2, pb2, 768, rq[:, :], [b_rq])
        hT = self.hT
        bh = self.b_h
        qn = [hT[:, 0, :], hT[:, 1, :]]
        b_qn = [bh[0], bh[1]]
        qr = [hT[0:64, 2, :], hT[0:64, 3, :]]
        b_qr = [bh[2], bh[3]]
        hflat = hT[:, :, :].rearrange("p k t -> p (k t)")
        kn = [hflat[:, 4 * T:4 * T + NKEY], hflat[:, 6 * T:6 * T + NKEY]]
        b_kn = [[bh[4], bh[5]], [bh[6], bh[7]]]
        vh = [hflat[:, 8 * T:8 * T + NKEY].rearrange("p (b n) -> p b n", b=NKB),
              hflat[:, 10 * T:10 * T + NKEY].rearrange("p (b n) -> p b n", b=NKB)]
        b_vh = [[bh[8], bh[9]], [bh[10], bh[11]]]
        ao = hT[:, 12:16, :]
        b_ao = bh[12:16]
        def stage_a(h):
            i2 = h % 2
            wq, bwq = self.get_w("w_q_up", None, 0, 6, h * 192, 192, hold=True)
            wqi = self.last_w
            wkv, bwkv = self.get_w("w_kv_up", None, 0, 4, h * 256, 256)
            for kg in range(3):
                col, pb = self.psA(1)
                for k in range(4):
                    self.mm(ps[:, col:col + 512], wkv[:, k, 0:128], ckvT[:, k, kg * 512:(kg + 1) * 512],
                            k == 0, k == 3, [bwkv, b_ckvc] + b_ckv, pb)
                S.op("act", pb, b_kn[i2],
                     lambda e: e.activation(kn[i2][:, kg * 512:(kg + 1) * 512], ps[:, col:col + 512], AF.Copy))
            for kq in range(3):
                col, pb = self.psA(1)
                for b4 in range(4):
                    kb = kq * 4 + b4
                    for k in range(4):
                        self.mm(ps[:, col + b4 * 128:col + (b4 + 1) * 128], ckvT[:, k, kb * 128:(kb + 1) * 128],
                                wkv[:, k, 128:256], k == 0, k == 3,
                                [bwkv, b_ckvc] + b_ckv, pb)
                S.op("dve", pb, b_vh[i2],
                     lambda e: e.tensor_copy(vh[i2][:, kq * 4:(kq + 1) * 4, :],
                                             ps[:, col:col + 512].rearrange("p (b n) -> p b n", b=4)))
            col, pb = self.lin_fm(wq, bwq, 6, 0, 128, lambda k, tg: qlat[:, k, tg * 512:(tg + 1) * 512],
                                  lambda k: [b_qlat[k]])
            S.op("dve", pb + [b_rq], [b_qn[i2]],
                 lambda e: e.tensor_tensor(qn[i2], ps[:, col:col + T], rq[:, :], ALU.mult))
            col, pb = self.lin_fm(wq, bwq, 6, 128, 64, lambda k, tg: qlat[:, k, tg * 512:(tg + 1) * 512],
                                  lambda k: [b_qlat[k]])
            self.rel_w(wqi)
            self.rope_chunk(col, pb, 64, 1.0, 1.0, F[0], bF[0], F[1], bF[1])
            S.op("dve", [bF[0], b_rq], [b_qr[i2]],
                 lambda e: e.tensor_tensor(qr[i2], F[0][0:64, :], rq[0:64, :], ALU.mult))
            self.ada_more(2)

        def stage_b(h):
            i2 = h % 2
            hk = h % 4
            units = []
            for qg in range(NQG):
                units.append(dict(
                    parts=[(lambda kb, i2=i2: kn[i2][:, kb * 128:(kb + 1) * 128],
                            qn[i2][:, qg * QG:(qg + 1) * QG],
                            lambda kb, i2=i2: b_kn[i2] + [b_qn[i2]]),
                           (lambda kb: kropeT[0:64, kb * 128:(kb + 1) * 128],
                            qr[i2][:, qg * QG:(qg + 1) * QG],
                            lambda kb, i2=i2: [b_krope, b_kropec, b_qr[i2]])],
                    scale=MLA_SCALE, qg=qg,
                    v_fn=lambda kb, c, i2=i2: vh[i2][:, kb, :],
                    v_reads=lambda kb, i2=i2: b_vh[i2], ndv=1,
                    fin=lambda col, pb, qg=qg, hk=hk: self.softmax_fin(
                        col, pb, 1, [ao[:, hk, qg * QG:(qg + 1) * QG]], [b_ao[hk]])))
            self.run_units(units)
            if hk == 3:
                g = h // 4
                wl = []
                for i in range(2):
                    wl.append(self.get_w("w_out_ab", None, g * 512 + i * 256, 2, 0, D, hold=(i == 0)))
                    if i == 0:
                        w0i = self.last_w
                self.resid_partial(lambda k: (wl[k // 2][0], k % 2, wl[k // 2][1]), 4,
                                   lambda k, tg: ao[:, k, tg * 512:(tg + 1) * 512], lambda k: [b_ao[k]], 0, 2)
                self.rel_w(w0i)

        stage_a(0)
        for h in range(8):
            if h + 1 < 8:
                stage_a(h + 1)
            stage_b(h)
        self.ada_more(2)
        if self.hoist:
            self.norm_mod(0, 1, F[0], bF[0], [F[1], F[2]], [bF[1], bF[2]],
                          sq + [qlat[:, 0, :], qlat[:, 1, :], qlat[:, 2, :]], b_sq + b_qlat[0:3])
            self.pre_normed.add(("ffn", 0))

    def ffn(self, l):
        S = self.S
        ps = self.ps
        self.arena_reset()
        F = [self.aalloc("F%d" % i, [128, T], F32) for i in range(8)]
        bF = [Buf("F%d" % i) for i in range(8)]
        act = [self.aalloc("act%d" % i, [128, 4, T], BF16) for i in range(2)]
        b_act = [[Buf("act%d_%d" % (i, j)) for j in range(4)] for i in range(2)]
        sq = [self.aalloc("sq%d" % i, [128, T], BF16) for i in range(2)]
        b_sq = [Buf("sq0"), Buf("sq1")]
        if ("ffn", l) not in self.pre_normed:
            self.norm_mod(l, 1, F[0], bF[0], [F[1], F[2]], [bF[1], bF[2]], sq, b_sq)
        self.ada_to(l * 48 + 48)

        def conv_chunk(col, pb, fc, dst, bdst):
            w0 = self.P("wconv", 1, (l * 3 + 0) * 88 + fc)
            w1 = self.P("wconv", 1, (l * 3 + 1) * 88 + fc)
            w2 = self.P("wconv", 1, (l * 3 + 2) * 88 + fc)
            bb = self.P("bconv", 1, l * 88 + fc)
            S.op("act", pb + [self.b_prm], [bdst],
                 lambda e: e.activation(dst[:, :], ps[:, col:col + T], AF.Identity, bias=bb, scale=w1))
            S.op("dve", pb + [self.b_prm, bdst], [bdst],
                 lambda e: e.scalar_tensor_tensor(dst[:, 1:T], ps[:, col:col + T - 1], w0, dst[:, 1:T],
                                                  ALU.mult, ALU.add))
            S.op("dve", pb + [self.b_prm, bdst], [bdst],
                 lambda e: e.scalar_tensor_tensor(dst[:, 0:T - 1], ps[:, col + 1:col + T], w2, dst[:, 0:T - 1],
                                                  ALU.mult, ALU.add))
            f0 = self.fix[:, l, 0, fc:fc + 1]
            f2 = self.fix[:, l, 1, fc:fc + 1]
            S.op("dve", pb + [self.b_fix, bdst], [bdst],
                 lambda e: e.scalar_tensor_tensor(dst[:, 256:T:256], ps[:, col + 255:col + T - 1:256], f0,
                                                  dst[:, 256:T:256], ALU.mult, ALU.add))
            S.op("dve", pb + [self.b_fix, bdst], [bdst],
                 lambda e: e.scalar_tensor_tensor(dst[:, 255:T - 1:256], ps[:, col + 256:col + T:256], f2,
                                                  dst[:, 255:T - 1:256], ALU.mult, ALU.add))

        fic = [0]

        def up(g):
            ai = g % 2
            for half in range(2):
                wg, bwg = self.get_w("w_up", l, 0, KC, g * 512 + half * 256, 256)
                sg = []
                for j in range(2):
                    fc = g * 4 + half * 2 + j
                    col, pb = self.lin_fm(wg, bwg, KC, j * 128, 128, self.h_rhs, self.h_bufs)
                    f, bf = F[fic[0] % 8], bF[fic[0] % 8]
                    fic[0] += 1
                    conv_chunk(col, pb, fc, f, bf)
                    S.op("act", [bf], [bf], lambda e: e.activation(f[:, :], f[:, :], AF.Silu))
                    sg.append((f, bf))
                if l == 0 and half == 0:
                    self.ada_more(1)
                wv, bwv = self.get_w("w_up", l, 0, KC, DFF + g * 512 + half * 256, 256)
                for j in range(2):
                    fc = g * 4 + half * 2 + j
                    col, pb = self.lin_fm(wv, bwv, KC, j * 128, 128, self.h_rhs, self.h_bufs)
                    f, bf = F[fic[0] % 8], bF[fic[0] % 8]
                    fic[0] += 1
                    conv_chunk(col, pb, 44 + fc, f, bf)
                    a = act[ai][:, half * 2 + j, :]
                    S.op("dve", [bf, sg[j][1]], [b_act[ai][half * 2 + j]],
                         lambda e: e.tensor_tensor(a, f[:, :], sg[j][0][:, :], ALU.mult))
                if l == 0 and half == 1:
                    self.ada_more(1)

        def down(g):
            ai = g % 2
            wl = []
            for i in range(2):
                wl.append(self.get_w("w_down", l, g * 512 + i * 256, 2, 0, D, hold=(i == 0)))
                if i == 0:
                    w0i = self.last_w
            self.resid_partial(lambda k: (wl[k // 2][0], k % 2, wl[k // 2][1]), 4,
                               lambda k, tg: act[ai][:, k, tg * 512:(tg + 1) * 512],
                               lambda k: [b_act[ai][k]], l, 5, alt=True)
            self.rel_w(w0i)
            if l == 0:
                self.ada_more(1)

        up(0)
        for g in range(11):
            if g + 1 < 11:
                up(g + 1)
            down(g)
        if l == 0:
            self.ada_to(96)
        if self.hoist:
            if l == 0:
                self.norm_mod(1, 0, F[0], bF[0], [F[1], F[2]], [bF[1], bF[2]], sq, b_sq)
                self.pre_normed.add(("l1", 0))
            else:
                self.final_norm(F[0:5], bF[0:5], sq, b_sq)

    def layer1_attn(self):
        S = self.S
        ps = self.ps
        d = self.d
        self.arena_reset()
        self.common_attn_arena()
        KhT = self.aalloc("KhT", [128, 2, NKEY], BF16)
        Vh = self.aalloc("Vh", [128, NKB, 256], BF16)
        QhT = [self.aalloc("QhT%d" % i, [128, 2, T], BF16) for i in range(2)]
        F = [self.aalloc("F%d" % i, [128, T], F32) for i in range(5)]
        bF = [Buf("F%d" % i) for i in range(5)]
        vt = [self.aalloc("vt%d" % i, [128, 256], F32) for i in range(2)]
        b_vt = [Buf("vt0"), Buf("vt1")]
        sq = [self.aalloc("sq0", [128, T], BF16)]
        b_sq = [Buf("sq0")]
        ao = self.aalloc("ao", [128, 2, T], BF16)
        b_ao = [Buf("ao0"), Buf("ao1")]
        self.rD = [self.aalloc("rD%d" % i, [128, QG], F32) for i in range(3)]
        self.b_rD = [Buf("rD%d" % i) for i in range(3)]
        b_k = [Buf("k0"), Buf("k1")]
        b_kc = [Buf("kc0"), Buf("kc1")]
        b_v = [Buf("v%d" % i) for i in range(NKB)]
        b_q = [[Buf("q0_%d" % i), Buf("q1_%d" % i)] for i in range(2)]
        self.load_rope(128)
        if ("l1", 0) not in self.pre_normed:
            self.norm_mod(1, 0, F[0], bF[0], [F[1], F[2]], [bF[1], bF[2]],
                          sq + [QhT[0][:, 0, :], QhT[0][:, 1, :], QhT[1][:, 0, :]],
                          b_sq + [b_q[0][0], b_q[0][1], b_q[1][0]])
        self.ada_to(48 + 48)
        neglam = self.lamt[:, 5:6]
        oraw = [F[3], F[4]]
        b_oraw = [bF[3], bF[4]]

        def stage_q(h):
            qi = h % 2
            w, bw = self.get_w("w_in_c", None, 0, KC, h * 256, 256)
            for j in range(2):
                col, pb = self.lin_fm(w, bw, KC, j * 128, 128, self.h_rhs, self.h_bufs)
                self.rope_chunk(col, pb, 128, 1.0, 1.0, F[1], bF[1], F[2], bF[2],
                                dst=QhT[qi][:, j, :], dst_bufs=[b_q[qi][j]])

        def stage_k(h):
            for j in range(2):
                S.dma("pool", [], [b_kc[j]], KhT[:, j, T:NKEY], d["dkT"][h * 2 + j])
            w, bw = self.get_w("w_in_c", None, 0, KC, 2048 + h * 256, 256)
            for j in range(2):
                col, pb = self.lin_fm(w, bw, KC, j * 128, 128, self.h_rhs, self.h_bufs)
                self.rope_chunk(col, pb, 128, 1.0, 1.0, F[1], bF[1], F[2], bF[2])
                self.out_fm(d["o_dkT"][(h * 2 + j) * 128:(h * 2 + j + 1) * 128, :], F[1][:, :], [bF[1]])
                S.op("act", [bF[1]], [b_k[j]], lambda e: e.activation(KhT[:, j, 0:T], F[1][:, :], AF.Copy))

        def stage_v(h):
            S.dma("pool", [], b_v[8:12], Vh[:, 8:12, :],
                  d["dv"][:, h * 256:(h + 1) * 256].rearrange("(b p) n -> p b n", p=128))
            w, bw = self.get_w("w_in_c", None, 0, KC, 4096 + h * 256, 256)
            for blk in range(8):
                col, pb = self.lin_tm(w, bw, 0, 256, blk)
                f, bf = vt[blk % 2], b_vt[blk % 2]
                S.op("act", pb, [bf], lambda e: e.activation(f[:, :], ps[:, col:col + 256], AF.Copy))
                S.dma("sp", [bf], [], d["o_dv"][blk * 128:(blk + 1) * 128, h * 256:(h + 1) * 256], f[:, :])
                S.op("dve", pb, [b_v[blk]], lambda e: e.tensor_copy(Vh[:, blk, :], ps[:, col:col + 256]))

        def stage_att(h):
            qi = h % 2
            units = []
            for qg in range(NQG):
                for j in range(2):
                    def fin(col, pb, qg=qg, j=j):
                        r = self.rD[j]
                        br = self.b_rD[j]
                        S.op("dve", [pb[1]], [br], lambda e: e.reciprocal(r[:, :], ps[:, col + 2 * QG:col + 3 * QG]))
                        if j == 1:
                            S.op("dve", [br, self.b_lam], [br],
                                 lambda e: e.tensor_scalar(r[:, :], r[:, :], neglam, None, ALU.mult))
                        for c in range(2):
                            o = oraw[c][:, qg * QG:(qg + 1) * QG]
                            if j == 0:
                                S.op("dve", [pb[0], br], [b_oraw[c]],
                                     lambda e: e.tensor_tensor(o, ps[:, col + c * QG:col + (c + 1) * QG], r[:, :], ALU.mult))
                            else:
                                t2 = self.rD[2]
                                S.op("dve", [pb[0], br], [self.b_rD[2]],
                                     lambda e: e.tensor_tensor(t2[:, :], ps[:, col + c * QG:col + (c + 1) * QG], r[:, :], ALU.mult))
                                S.op("dve", [self.b_rD[2], b_oraw[c]], [b_oraw[c]],
                                     lambda e: e.tensor_tensor(o, o, t2[:, :], ALU.add))
                    units.append(dict(
                        parts=[(lambda kb, j=j: KhT[:, j, kb * 128:(kb + 1) * 128],
                                QhT[qi][:, j, qg * QG:(qg + 1) * QG],
                                lambda kb, j=j: [b_k[j], b_kc[j], b_q[qi][j]])],
                        scale=HEAD_SCALE, qg=qg,
                        v_fn=lambda kb, c: Vh[:, kb, c * 128:(c + 1) * 128],
                        v_reads=lambda kb: [b_v[kb]], ndv=2, fin=fin))
            self.run_units(units)

        def stage_out_a(h):
            c2, pb2 = self.psB(2)
            self.sumsq(lambda c: oraw[c][:, :], lambda c: [b_oraw[c]], 2, sq, b_sq, c2, pb2)
            self.rstd_from_ss(c2, pb2, 256, F[0][:, :], [bF[0]])
            for c in range(2):
                S.op("dve", [b_oraw[c], self.b_lam, bF[0]], [b_ao[c]],
                     lambda e: e.scalar_tensor_tensor(ao[:, c, :], oraw[c][:, :], self.lamt[:, 6 + c:7 + c],
                                                      F[0][:, :], ALU.mult, ALU.mult))

        def stage_out_b(h):
            wo, bwo = self.get_w("w_out_c", None, h * 256, 2, 0, D)
            self.resid_partial(lambda k: (wo, k, bwo), 2,
                               lambda k, tg: ao[:, k, tg * 512:(tg + 1) * 512], lambda k: [b_ao[k]], 1, 2, alt=True)

        stage_q(0)
        stage_k(0)
        stage_v(0)
        for h in range(8):
            stage_att(h)
            if h + 1 < 8:
                stage_q(h + 1)
            stage_out_a(h)
            if h + 1 < 8:
                stage_k(h + 1)
            stage_out_b(h)
            if h + 1 < 8:
                stage_v(h + 1)
        if self.hoist:
            self.norm_mod(1, 1, F[0], bF[0], [F[1], F[2]], [bF[1], bF[2]],
                          sq + [QhT[0][:, 0, :], QhT[0][:, 1, :], QhT[1][:, 0, :]],
                          b_sq + [b_q[0][0], b_q[0][1], b_q[1][0]])
            self.pre_normed.add(("ffn", 1))

    def final_norm(self, F=None, bF=None, sq=None, b_sq=None):
        S = self.S
        d = self.d
        if F is None:
            self.arena_reset()
            F = [self.aalloc("F%d" % i, [128, T], F32) for i in range(5)]
            bF = [Buf("F%d" % i) for i in range(5)]
            sq = [self.aalloc("sq%d" % i, [128, T], BF16) for i in range(2)]
            b_sq = [Buf("sq0"), Buf("sq1")]
        col, pb = self.psB(2)
        self.sumsq(lambda c: self.xT[:, c, :], lambda c: [self.b_x[c]], KC, sq, b_sq, col, pb, dve_alt=True)
        self.rstd_from_ss(col, pb, D, F[0][:, :], [bF[0]])
        for kc in range(KC):
            f, bf = F[1 + kc % 4], bF[1 + kc % 4]
            S.op("dve", [self.b_x[kc], self.b_prm, bF[0]], [bf],
                 lambda e: e.scalar_tensor_tensor(f[:, :], self.xT[:, kc, :], self.P("gfin", 1, kc),
                                                  F[0][:, :], ALU.mult, ALU.mult))
            S.dma("sp", [bf], [], d["yT"][kc * 128:(kc + 1) * 128, :], f[:, :])

    def build(self):
        self.S.dry = True
        self.runid = 0
        self.emit()
        self.runid = 1
        self.wplan = self.wplan_new
        self.S.dry = False
        self.S.reset()
        self.emit()
        return self.nc


def _rope_tables(rot_dim, n_tokens=T, grid_w=64):
    n_rows = n_tokens // grid_w
    row = np.repeat(np.arange(n_rows), grid_w).astype(np.float32)
    col = np.tile(np.arange(grid_w), n_rows).astype(np.float32)
    n_freq = rot_dim // 4
    freqs = (10000.0 ** (-np.arange(n_freq, dtype=np.float32) / n_freq)).astype(np.float32)
    ang = np.concatenate([row[:, None] * freqs, col[:, None] * freqs], axis=-1)
    cos, sin = np.cos(ang).astype(np.float32), np.sin(ang).astype(np.float32)
    C = np.concatenate([cos, cos], axis=1).T
    SN = np.concatenate([-sin, sin], axis=1).T
    return np.ascontiguousarray(np.stack([C, SN]).astype(np.float32))


_PROG = None


def make_in_maps(g):
    f32 = np.float32

    def swap(v):
        v = np.asarray(v, f32)
        h = v.shape[-1] // 2
        return np.concatenate([v[..., h:], v[..., :h]], axis=-1)

    shared = {
        "w_ada": np.ascontiguousarray(g["w_ada"], f32),
        "w_in_ab": np.ascontiguousarray(g["w_in_ab"][0], f32),
        "w_q_up": np.ascontiguousarray(g["w_mla_q_up"][0], f32),
        "w_kv_up": np.ascontiguousarray(g["w_mla_kv_up"][0], f32),
        "w_out_ab": np.ascontiguousarray(g["w_out_ab"][0], f32),
        "w_in_c": np.ascontiguousarray(g["w_in_c"][0], f32),
        "w_out_c": np.ascontiguousarray(g["w_out_c"][0], f32),
        "w_up": np.ascontiguousarray(g["w_ffn_up"], f32),
        "w_down": np.ascontiguousarray(g["w_ffn_down"], f32),
    }
    rope128 = _rope_tables(128)
    rope64 = _rope_tables(64)
    id128 = np.stack([np.ones((128, T), f32), np.zeros((128, T), f32)])
    id64 = np.stack([np.ones((64, T), f32), np.zeros((64, T), f32)])

    def prm_for(cond, maskb, cflag):
        parts = {
            "gmix": _fm(g["g_norm_mix"]), "gffn": _fm(g["g_norm_ffn"]), "gfin": _fm(g["g_final"]),
            "bada": _fm(g["b_ada"]), "gq": _fm(g["g_mla_q"][0]), "gkv": _fm(g["g_mla_kv"][0]),
            "ggq": _fm(g["g_gqa_q"][0]), "ggqs": _fm(swap(g["g_gqa_q"][0])),
            "ggk": _fm(g["g_gqa_k"][0]), "ggks": _fm(swap(g["g_gqa_k"][0])),
            "gsub": _fm(g["g_diff_sub"][0]), "wconv": _fm(g["w_ffn_conv"]), "bconv": _fm(g["b_ffn_conv"]),
            "maskb": np.broadcast_to(maskb.reshape(1, 48), (128, 48)),
            "cflag": np.full((128, 1), cflag, f32),
            "lam": np.broadcast_to(np.concatenate([g["lambda_q1"][0], g["lambda_k1"][0],
                                                   g["lambda_q2"][0], g["lambda_k2"][0]]).reshape(1, 512), (128, 512)),
            "cond": _fm(cond),
        }
        out = np.zeros((128, NP_), f32)
        for n, s in PRM_FIELDS:
            a = np.asarray(parts[n], f32)
            assert a.shape == (128, s), (n, a.shape, s)
            out[:, OFF[n]:OFF[n] + s] = a
        return out

    in_maps = []
    for c in range(8):
        m = dict(shared)
        if c < 2:
            b = c
            m["xT"] = np.ascontiguousarray(g["x_sample"][b].T, f32)
            maskb = np.zeros((NQG, NKB), f32)
            m["prm"] = prm_for(g["c"][b], maskb, 0.0)
            m["rope128"], m["rope64"] = rope128, rope64
            m["ckvT"] = np.ascontiguousarray(g["cache_mla_ckv"][b, 0].T, f32)
            m["kropeT"] = np.ascontiguousarray(g["cache_mla_krope"][b, 0].T, f32)
            m["gkT"] = np.ascontiguousarray(g["cache_gqa_k"][b, 0].transpose(1, 2, 0), f32)
            m["gv"] = np.ascontiguousarray(g["cache_gqa_v"][b, 0].reshape(512, 256), f32)
            m["dkT"] = np.ascontiguousarray(g["cache_diff_k"][b, 0].transpose(1, 2, 3, 0).reshape(16, 128, 512), f32)
            m["dv"] = np.ascontiguousarray(g["cache_diff_v"][b, 0].reshape(512, 2048), f32)
        else:
            pc = c - 2 if c < 6 else 0
            xs = g["x_prompt"][4 * pc:4 * pc + 4].reshape(T, D)
            m["xT"] = np.ascontiguousarray(xs.T, f32)
            maskb = np.full((NQG, NKB), NEG, f32)
            for s in range(4):
                maskb[s, 2 * s] = 0.0
                maskb[s, 2 * s + 1] = 0.0
            m["prm"] = prm_for(g["c_ctx"], maskb, -1.0)
            m["rope128"], m["rope64"] = id128, id64
            m["ckvT"] = np.zeros((512, 512), f32)
            m["kropeT"] = np.zeros((64, 512), f32)
            m["gkT"] = np.zeros((2, 128, 512), f32)
            m["gv"] = np.zeros((512, 256), f32)
            m["dkT"] = np.zeros((16, 128, 512), f32)
            m["dv"] = np.zeros((512, 2048), f32)
        in_maps.append(m)
    return in_maps


def kernel(**inp):
    global _PROG
    f32 = np.float32
    g = {k: np.asarray(v) for k, v in inp.items()}
    if _PROG is None:
        _PROG = Prog().build()
    nc = _PROG
    in_maps = make_in_maps(g)
    res = run_bass_kernel_spmd(nc, in_maps, core_ids=list(range(8)))
    R = res.results
    y_sample = np.stack([R[b]["yT"].T for b in range(2)]).astype(f32)
    y_prompt = np.concatenate([R[2 + pc]["yT"].T.reshape(4, 256, D) for pc in range(4)]).astype(f32)

    def gather(fn):
        return np.concatenate([fn(R[2 + pc]) for pc in range(4)]).astype(f32)

    new_ckv = gather(lambda r: r["o_ckvT"].T.reshape(4, 1, 256, 512))
    new_krope = gather(lambda r: r["o_kropeT"].T.reshape(4, 1, 256, 64))
    new_gk = gather(lambda r: r["o_gkT"].reshape(2, 128, 4, 256).transpose(2, 3, 0, 1).reshape(4, 1, 256, 2, 128))
    new_gv = gather(lambda r: r["o_gv"].reshape(4, 1, 256, 2, 128))
    new_dk = gather(lambda r: r["o_dkT"].reshape(8, 2, 128, 4, 256).transpose(3, 4, 0, 1, 2).reshape(4, 1, 256, 8, 2, 128))
    new_dv = gather(lambda r: r["o_dv"].reshape(4, 1, 256, 8, 256))
    return (np.ascontiguousarray(y_prompt), np.ascontiguousarray(y_sample), np.ascontiguousarray(new_ckv),
            np.ascontiguousarray(new_krope), np.ascontiguousarray(new_gk), np.ascontiguousarray(new_gv),
            np.ascontiguousarray(new_dk), np.ascontiguousarray(new_dv))
```

```python
import math
import numpy as np
import concourse.bass as bass
import concourse.mybir as mybir
from concourse.bass_utils import run_bass_kernel_spmd

F32 = mybir.dt.float32
BF16 = mybir.dt.bfloat16
AF = mybir.ActivationFunctionType
ALU = mybir.AluOpType
AX = mybir.AxisListType

T = 1024
D = 2048
KC = 16
NKEY = 1536
NKB = 12
QG = 256
NQG = 4
DFF = 5632
NFC = 44
EPS = 1e-6
MLA_SCALE = 192 ** -0.5
HEAD_SCALE = 128 ** -0.5
LAMBDA_INIT = 0.8 - 0.6 * math.exp(-0.3 * 1)
NEG = -30000.0
NSLOT = 4
WSLOT_ELEMS = 4096

SAME_ENGINE_SYNC = True


_FENCE = []


class Buf:
    __slots__ = ("name", "w", "r", "excl")

    def __init__(self, name="", excl=False):
        self.name = name
        self.w = None
        self.r = list(_FENCE)
        self.excl = excl


class Tok:
    __slots__ = ("sem", "val", "eng")

    def __init__(self, sem, val, eng):
        self.sem, self.val, self.eng = sem, val, eng


class Sched:
    def __init__(self, nc, n_dma_sems=8):
        self.nc = nc
        self.dry = False
        self.eng = {"pe": nc.tensor, "act": nc.scalar, "dve": nc.vector,
                    "pool": nc.gpsimd, "sp": nc.sync}
        self.sems = {k: [nc.alloc_semaphore("s_%s%d" % (k, i)) for i in range(3)]
                     for k in ("pe", "act", "dve", "pool")}
        self.sem = {k: v[0] for k, v in self.sems.items()}
        self.dsem = {q: [nc.alloc_semaphore("d_%s%d" % (q, i)) for i in range(n_dma_sems)]
                     for q in ("sp", "pool")}
        self.reset()

    def reset(self):
        self.cnt = {k: 0 for k in self.sems}
        self.epoch = {k: 0 for k in self.sems}
        self.sem = {k: v[0] for k, v in self.sems.items()}
        self.last_tok = {k: None for k in self.sems}
        self.seen = {k: {} for k in self.eng}
        self.dval = {q: [0] * len(v) for q, v in self.dsem.items()}
        self.dpos = {q: 0 for q in self.dsem}
        self.n_inst = 0

    def _wait(self, e, tok):
        if tok is None:
            return
        if tok.eng == e:
            if (not SAME_ENGINE_SYNC) or e == "pe":
                return
        key = id(tok.sem)
        if self.seen[e].get(key, 0) >= tok.val:
            return
        self.eng[e].wait_ge(tok.sem, tok.val)
        self.seen[e][key] = tok.val

    def _deps(self, e, reads, writes):
        for b in reads:
            self._wait(e, b.w)
            if b.excl:
                for t in b.r:
                    if t.eng != e:
                        self._wait(e, t)
        for b in writes:
            self._wait(e, b.w)
            for t in b.r:
                self._wait(e, t)

    def _commit(self, tok, reads, writes):
        for b in reads:
            b.r.append(tok)
            if len(b.r) > 16:
                best = {}
                for t in b.r:
                    k = id(t.sem)
                    if k not in best or best[k].val < t.val:
                        best[k] = t
                b.r = list(best.values())
        for b in writes:
            b.w = tok
            b.r = []

    def op(self, e, reads, writes, fn):
        if self.dry:
            return None
        self._deps(e, reads, writes)
        ins = fn(self.eng[e])
        self.cnt[e] += 1
        ins.then_inc(self.sem[e], 1)
        tok = Tok(self.sem[e], self.cnt[e], e)
        self.last_tok[e] = tok
        self._commit(tok, reads, writes)
        if self.cnt[e] >= 20000:
            self.epoch[e] += 1
            self.sem[e] = self.sems[e][self.epoch[e]]
            self.cnt[e] = 0
        self.n_inst += 1
        return tok

    def dma(self, q, reads, writes, out, in_):
        if self.dry:
            return None
        i = self.dpos[q]
        self.dpos[q] = (i + 1) % len(self.dsem[q])
        sem = self.dsem[q][i]
        if self.dval[q][i] > 0:
            self._wait(q, Tok(sem, self.dval[q][i], "dma_" + q))
        self._deps(q, reads, writes)
        ins = self.eng[q].dma_start(out=out, in_=in_)
        self.dval[q][i] += 16
        ins.then_inc(sem, 16)
        tok = Tok(sem, self.dval[q][i], "dma_" + q)
        self._commit(tok, reads, writes)
        self.n_inst += 1
        return tok

    def _all_toks(self):
        toks = []
        for k in self.sems:
            if self.last_tok[k] is not None:
                toks.append(self.last_tok[k])
        for q in self.dsem:
            for i, s in enumerate(self.dsem[q]):
                if self.dval[q][i] > 0:
                    toks.append(Tok(s, self.dval[q][i], "dma_" + q))
        return toks

    def barrier(self, bufs=()):
        if self.dry:
            return
        toks = self._all_toks()
        for e in ("pe", "act", "dve", "pool", "sp"):
            for t in toks:
                if t.eng == e:
                    continue
                self._wait(e, t)

    def finish(self):
        if self.dry:
            return
        for t in self._all_toks():
            self._wait("sp", t)


def _fm(v):
    v = np.asarray(v, np.float32)
    lead = v.shape[:-1]
    n = v.shape[-1] // 128
    v = v.reshape(*lead, n, 128)
    v = np.moveaxis(v, -1, 0)
    return np.ascontiguousarray(v).reshape(128, -1)


PRM_FIELDS = [("gmix", 32), ("gffn", 32), ("gfin", 16), ("bada", 192), ("gq", 6), ("gkv", 4),
              ("ggq", 1), ("ggqs", 1), ("ggk", 1), ("ggks", 1), ("gsub", 2),
              ("wconv", 2 * 3 * 88), ("bconv", 2 * 88), ("maskb", 48), ("cflag", 1),
              ("lam", 512), ("cond", 16)]
OFF = {}
_o = 0
for _n, _s in PRM_FIELDS:
    OFF[_n] = _o
    _o += _s
NP_ = _o


class Prog:
    def __init__(self, stop=99, dbg=False, sub=99):
        self.sub = sub
        self.stop = stop
        self.dbg = dbg
        nc = bass.Bass("TRN2", target_bir_lowering=False)
        self.nc = nc
        dt = nc.dram_tensor
        I = "ExternalInput"
        O = "ExternalOutput"
        self.d = {}
        for name, shape in [("xT", [D, T]), ("prm", [128, NP_]), ("rope128", [2, 128, T]),
                            ("rope64", [2, 64, T]), ("ckvT", [512, 512]), ("kropeT", [64, 512]),
                            ("gkT", [2, 128, 512]), ("gv", [512, 256]), ("dkT", [16, 128, 512]),
                            ("dv", [512, 2048]), ("w_ada", [2, D, 6 * D]), ("w_in_ab", [D, 2880]),
                            ("w_q_up", [768, 1536]), ("w_kv_up", [512, 2048]), ("w_out_ab", [D, D]),
                            ("w_in_c", [D, 6144]), ("w_out_c", [D, D]), ("w_up", [2, D, 2 * DFF]),
                            ("w_down", [2, DFF, D])]:
            self.d[name] = dt(name, shape, F32, kind=I).ap()
        for name, shape in [("yT", [D, T]), ("o_ckvT", [512, T]), ("o_kropeT", [64, T]),
                            ("o_gkT", [256, T]), ("o_gv", [T, 256]), ("o_dkT", [2048, T]),
                            ("o_dv", [T, 2048])]:
            self.d[name] = dt(name, shape, F32, kind=O).ap()
        if dbg:
            self.d["dbg"] = dt("dbg", [128, 4096], F32, kind=O).ap()
        self.S = Sched(nc)
        self.off = 16512
        self.xT = self.alloc("xT", [128, KC, T], F32)
        self.hT = self.alloc("hT", [128, KC, T], BF16)
        self.W = [self.alloc("w%d" % i, [128, WSLOT_ELEMS], BF16) for i in range(NSLOT)]
        self.rope = self.alloc("rope", [128, 2, T], F32)
        self.prm = self.alloc("prm", [128, NP_], F32)
        self.modT = self.alloc("modT", [128, 2, 96], F32)
        self.der = self.alloc("der", [128, 64], F32)
        self.fix = self.alloc("fix", [128, 2, 2, 88], F32)
        self.lamt = self.alloc("lamt", [128, 8], F32)
        self.ones = self.alloc("ones", [128, 128], BF16)
        self.condT = self.alloc("condT", [128, 16], BF16)
        self.arena0 = self.off
        self.arena_end = 229376
        self.ps = nc.alloc_psum_tensor("ps", [128, 4096], F32)
        self.b_x = [Buf("x%d" % i) for i in range(KC)]
        self.b_h = [Buf("h%d" % i) for i in range(KC)]
        self.b_w = [Buf("w%d" % i) for i in range(NSLOT)]
        self.b_rope = Buf("rope")
        self.b_prm = Buf("prm")
        self.b_mod = [[Buf("mod%d_%d" % (l, s)) for s in range(6)] for l in range(2)]
        self.b_der = Buf("der")
        self.b_fix = Buf("fix")
        self.b_lam = Buf("lam")
        self.b_ones = Buf("ones")
        self.b_cond = Buf("cond")
        self.b_ps = [Buf("ps%d" % i, excl=True) for i in range(8)]
        self.wplan = None

    def alloc(self, name, shape, dtype, arena=False):
        nbytes = int(np.prod(shape[1:])) * (4 if dtype == F32 else 2)
        nbytes = (nbytes + 31) // 32 * 32
        t = self.nc.alloc_sbuf_tensor_at(name, shape, dtype, offset=self.off)
        self.off += nbytes
        assert self.off <= 229376, (name, self.off)
        return t

    def arena_reset(self):
        _FENCE[:] = [] if self.S.dry else self.S._all_toks()
        self.off = self.arena0
        self.acount = getattr(self, "acount", 0) + 1

    def aalloc(self, name, shape, dtype):
        t = self.alloc("%s_%d_%d" % (name, self.runid, self.acount), shape, dtype)
        return t

    def _psalloc(self, pool, n):
        p = self.pspos[pool]
        p = (p + n - 1) // n * n
        if p + n > 4:
            p = 0
        self.pspos[pool] = (p + n) % 4
        base = pool * 4 + p
        return base * 512, self.b_ps[base:base + n]

    def psA(self, n):
        return self._psalloc(0, n)

    def psB(self, n):
        return self._psalloc(1, n)

    def get_w(self, name, l, row0, kcn, col0, ncols, hold=False):
        spec = (name, l, row0, kcn, col0, ncols)
        i = self.wi
        self.wi += 1
        self.last_w = i
        if self.S.dry:
            self.wplan_new.append(spec)
            slot = 0
        else:
            assert self.wplan[i] == spec, (i, self.wplan[i], spec)
            for j in self.w_auto:
                self._free_w(j)
            self.w_auto = set()
            if not hold:
                self.w_auto.add(i)
            self._pump()
            assert self.wissued > i, ("weight tile not issued (no free slot)", i, spec)
            slot = self.w_slot[i]
        view = self.W[slot][:, 0:kcn * ncols].rearrange("p (k n) -> p k n", k=kcn)
        return view, self.b_w[slot]

    def _free_w(self, j):
        self.w_free.append(self.w_slot[j])

    def rel_w(self, i):
        if not self.S.dry:
            self._free_w(i)
            self._pump()

    def _pump(self):
        while self.wissued < len(self.wplan) and self.wissued < self.wi + NSLOT - 1 and self.w_free:
            self._issue_w(self.wissued)
            self.wissued += 1

    def _issue_w(self, j):
        name, l, row0, kcn, col0, ncols = self.wplan[j]
        src = self.d[name]
        if l is not None:
            src = src[l]
        src = src[row0:row0 + kcn * 128, col0:col0 + ncols].rearrange("(k p) n -> p k n", p=128)
        slot = self.w_free.pop(0)
        self.w_slot[j] = slot
        dst = self.W[slot][:, 0:kcn * ncols].rearrange("p (k n) -> p k n", k=kcn)
        self.S.dma("pool", [], [self.b_w[slot]], dst, src)

    def mm(self, out, lhsT, rhs, start, stop, reads, writes):
        self.S.op("pe", reads + writes if not start else reads, writes,
                  lambda e: e.matmul(out, lhsT, rhs, start=start, stop=stop))

    def P(self, name, n=None, lo=0):
        o = OFF[name] + lo
        if n is None:
            n = 1
        return self.prm[:, o:o + n]

    def ada_to(self, upto):
        while self.ada_done < min(upto, 96):
            self.ada_tile(self.ada_done)
            self.ada_done += 1

    def ada_more(self, k=1):
        self.ada_to(self.ada_done + k)

    def ada_tile(self, a):
        S = self.S
        l, t = divmod(a, 48)
        w, bw = self.get_w("w_ada", l, 0, KC, t * 256, 256)
        col, pb = self.psA(1)
        ps = self.ps
        for j in range(2):
            for kc in range(KC):
                self.mm(ps[:, col + j:col + j + 1], w[:, kc, j * 128:(j + 1) * 128],
                        self.condT[:, kc:kc + 1], kc == 0, kc == KC - 1, [bw, self.b_cond], pb)
        sec = (t * 2) // 16
        c0 = t * 2
        S.op("dve", pb + [self.b_prm], [self.b_mod[l][sec]],
             lambda e: e.tensor_tensor(self.modT[:, l, c0:c0 + 2], ps[:, col:col + 2],
                                       self.P("bada", 2, l * 96 + c0), ALU.add))

    def rstd_from_ss(self, col, pb, n, dst, dst_bufs, width=T):
        S = self.S
        ps = self.ps
        S.op("act", pb, dst_bufs,
             lambda e: e.activation(dst, ps[:, col:col + width], AF.Ln, bias=EPS, scale=1.0 / n))
        S.op("act", dst_bufs, dst_bufs, lambda e: e.activation(dst, dst, AF.Exp, scale=-0.5))

    def sumsq(self, src_fn, src_bufs_fn, nchunks, sqs, sq_bufs, col, pb, first=True, last=True, base=0, total=None, dve_alt=False):
        S = self.S
        ps = self.ps
        total = nchunks if total is None else total
        for c in range(nchunks):
            sq = sqs[(base + c) % len(sqs)]
            sb = sq_bufs[(base + c) % len(sqs)]
            sbl = sb if isinstance(sb, list) else [sb]
            src = src_fn(c)
            if dve_alt and c % 2 == 1:
                S.op("dve", src_bufs_fn(c), sbl, lambda e: e.tensor_tensor(sq[:, :], src, src, ALU.mult))
            else:
                S.op("act", src_bufs_fn(c), sbl, lambda e: e.activation(sq[:, :], src, AF.Square))
            for tg in range(2):
                self.mm(ps[:, col + tg * 512:col + (tg + 1) * 512], self.ones[:, :],
                        sq[:, tg * 512:(tg + 1) * 512], (base + c) == 0, (base + c) == total - 1,
                        sbl + [self.b_ones], [pb[tg]])

    def norm_mod(self, l, which, rstd, b_rstd, tmp, b_tmp, sqs, b_sqs):
        S = self.S
        self.ada_to(l * 48 + (16 if which == 0 else 40))
        sec_sh, sec_sc = 3 * which, 3 * which + 1
        gname = "gmix" if which == 0 else "gffn"
        gs = self.der[:, 0:16]
        S.op("dve", [self.b_mod[l][sec_sc], self.b_prm], [self.b_der],
             lambda e: e.scalar_tensor_tensor(gs, self.modT[:, l, sec_sc * 16:(sec_sc + 1) * 16], 1.0,
                                              self.P(gname, 16, l * 16), ALU.add, ALU.mult))
        col, pb = self.psB(2)
        self.sumsq(lambda c: self.xT[:, c, :], lambda c: [self.b_x[c]], KC, sqs, b_sqs, col, pb, dve_alt=True)
        self.rstd_from_ss(col, pb, D, rstd[:, :], [b_rstd])
        for kc in range(KC):
            t_ = tmp[kc % len(tmp)]
            bt = b_tmp[kc % len(tmp)]
            S.op("dve", [self.b_x[kc], self.b_der, b_rstd], [bt],
                 lambda e: e.scalar_tensor_tensor(t_[:, :], self.xT[:, kc, :], self.der[:, kc:kc + 1],
                                                  rstd[:, :], ALU.mult, ALU.mult))
            S.op("act", [bt, self.b_mod[l][sec_sh]], [self.b_h[kc]],
                 lambda e: e.activation(self.hT[:, kc, :], t_[:, :], AF.Identity,
                                        bias=self.modT[:, l, sec_sh * 16 + kc:sec_sh * 16 + kc + 1], scale=1.0))

    def ps_alt(self, n):
        self.alt_i ^= 1
        return self.psA(n) if self.alt_i else self.psB(n)

    def lin_fm(self, w, bw, kcn, c0, M, rhs_fn, rhs_bufs_fn, alt=False):
        col, pb = self.ps_alt(2) if alt else self.psA(2)
        ps = self.ps
        for k in range(kcn):
            for tg in range(2):
                self.mm(ps[0:M, col + tg * 512:col + (tg + 1) * 512], w[:, k, c0:c0 + M],
                        rhs_fn(k, tg), k == 0, k == kcn - 1, [bw] + rhs_bufs_fn(k), [pb[tg]])
        return col, pb

    def h_rhs(self, k, tg):
        return self.hT[:, k, tg * 512:(tg + 1) * 512]

    def h_bufs(self, k):
        return [self.b_h[k]]

    def lin_tm(self, w, bw, c0, n, blk):
        col, pb = self.psA(1)
        ps = self.ps
        for k in range(KC):
            self.mm(ps[:, col:col + n], self.hT[:, k, blk * 128:(blk + 1) * 128], w[:, k, c0:c0 + n],
                    k == 0, k == KC - 1, [bw, self.b_h[k]], pb)
        return col, pb

    def rope_chunk(self, col, pb, M, g, gsw, t1, b_t1, t2, b_t2, dst=None, dst_bufs=None):
        S = self.S
        ps = self.ps
        h = M // 2
        C = self.rope[0:M, 0, :]
        SN = self.rope[0:M, 1, :]
        rd = pb + [self.b_rope, self.b_prm]
        S.op("dve", rd, [b_t1],
             lambda e: e.scalar_tensor_tensor(t1[0:M, :], ps[0:M, col:col + T], g, C, ALU.mult, ALU.mult))
        gl = gsw[0:h] if not isinstance(gsw, float) else gsw
        gh = gsw[h:M] if not isinstance(gsw, float) else gsw
        S.op("dve", rd, [b_t2],
             lambda e: e.scalar_tensor_tensor(t2[0:h, :], ps[h:M, col:col + T], gl, SN[0:h, :],
                                              ALU.mult, ALU.mult))
        S.op("dve", rd, [b_t2],
             lambda e: e.scalar_tensor_tensor(t2[h:M, :], ps[0:h, col:col + T], gh, SN[h:M, :],
                                              ALU.mult, ALU.mult))
        if dst is None:
            S.op("dve", [b_t1, b_t2], [b_t1],
                 lambda e: e.tensor_tensor(t1[0:M, :], t1[0:M, :], t2[0:M, :], ALU.add))
        else:
            S.op("dve", [b_t1, b_t2], dst_bufs,
                 lambda e: e.tensor_tensor(dst, t1[0:M, :], t2[0:M, :], ALU.add))

    def attn_score_tile(self, parts, scale, qg, pi, kb):
        S = self.S
        ps = self.ps
        col, pb = self.psA(1)
        for i, (lf, rhs, rf) in enumerate(parts):
            self.mm(ps[:, col:col + QG], lf(kb), rhs, i == 0, i == len(parts) - 1, rf(kb), pb)
        mb = self.P("maskb", 1, qg * NKB + kb)
        pt = self.PT[pi][:, kb, :]
        S.op("act", pb + [self.b_prm], [self.b_pt[pi][kb]],
             lambda e: e.activation(pt, ps[:, col:col + QG], AF.Exp, bias=mb, scale=scale))

    def attn_pv_list(self, pi, v_fn, v_reads_fn, ndv):
        col, pb = self.psB(2)
        ps = self.ps
        lst = []
        for c in range(ndv):
            for kb in range(NKB):
                lst.append(lambda c=c, kb=kb: self.mm(
                    ps[:, col + c * QG:col + (c + 1) * QG], v_fn(kb, c), self.PT[pi][:, kb, :],
                    kb == 0, kb == NKB - 1, v_reads_fn(kb) + [self.b_pt[pi][kb]], [pb[0]]))
        for kb in range(NKB):
            lst.append(lambda kb=kb: self.mm(
                ps[:, col + 2 * QG:col + 3 * QG], self.ones[:, :], self.PT[pi][:, kb, :],
                kb == 0, kb == NKB - 1, [self.b_ones, self.b_pt[pi][kb]], [pb[1]]))
        return col, pb, lst

    def run_units(self, units):
        prev = None
        for u in list(units) + [None]:
            lst = []
            if prev is not None:
                pu, ppi = prev
                col, pb, lst = self.attn_pv_list(ppi, pu["v_fn"], pu["v_reads"], pu["ndv"])
            if u is not None:
                pi = self.pt_i
                self.pt_i ^= 1
                per = (len(lst) + NKB - 1) // NKB
                for kb in range(NKB):
                    self.attn_score_tile(u["parts"], u["scale"], u["qg"], pi, kb)
                    for f in lst[kb * per:(kb + 1) * per]:
                        f()
                for f in lst[NKB * per:]:
                    f()
                self.unit_ctr += 1
                if self.unit_ctr % 2 == 0:
                    self.ada_more(1)
            else:
                for f in lst:
                    f()
            if prev is not None:
                pu["fin"](col, pb)
            prev = (u, pi) if u is not None else None

    def softmax_fin(self, col, pb, ndv, outs, out_bufs):
        S = self.S
        ps = self.ps
        r = self.rD[self.rd_i % len(self.rD)]
        br = self.b_rD[self.rd_i % len(self.rD)]
        self.rd_i += 1
        S.op("dve", [pb[1]], [br], lambda e: e.reciprocal(r[:, :], ps[:, col + 2 * QG:col + 3 * QG]))
        for c in range(ndv):
            o = outs[c]
            S.op("dve", [pb[0], br], [out_bufs[c]],
                 lambda e: e.tensor_tensor(o, ps[:, col + c * QG:col + (c + 1) * QG], r[:, :], ALU.mult))

    def resid_partial(self, wl, kcn, rhs_fn, rhs_bufs_fn, l, sec, alt=False):
        S = self.S
        ps = self.ps
        for oc in range(KC):
            col, pb = self.ps_alt(2) if alt else self.psB(2)
            for k in range(kcn):
                w, ki, bw = wl(k)
                for tg in range(2):
                    self.mm(ps[:, col + tg * 512:col + (tg + 1) * 512], w[:, ki, oc * 128:(oc + 1) * 128],
                            rhs_fn(k, tg), k == 0, k == kcn - 1, [bw] + rhs_bufs_fn(k), [pb[tg]])
            gate = self.modT[:, l, sec * 16 + oc:sec * 16 + oc + 1]
            xs = self.xT[:, oc, :]
            S.op("dve", pb + [self.b_mod[l][sec], self.b_x[oc]], [self.b_x[oc]],
                 lambda e: e.scalar_tensor_tensor(xs, ps[:, col:col + T], gate, xs, ALU.mult, ALU.add))

    def emit(self):
        S = self.S
        nc = self.nc
        d = self.d
        ps = self.ps
        _FENCE[:] = []
        self.pspos = [0, 0]
        self.wi = 0
        self.wissued = 0
        self.w_free = list(range(NSLOT))
        self.w_slot = {}
        self.w_auto = set()
        self.ada_done = 0
        self.pt_i = 0
        self.alt_i = 0
        self.unit_ctr = 0
        self.rd_i = 0
        self.acount = 0
        self.wplan_new = []
        for bl in (self.b_x, self.b_h, self.b_w, self.b_ps, [self.b_rope, self.b_prm, self.b_der, self.b_fix,
                                                            self.b_lam, self.b_ones, self.b_cond],
                   self.b_mod[0], self.b_mod[1]):
            for b in bl:
                b.w = None
                b.r = []
        self.off = self.arena0

        S.dma("sp", [], [self.b_prm], self.prm[:, :], d["prm"])
        xsrc = d["xT"].rearrange("(k p) t -> p k t", p=128)
        for g in range(4):
            S.dma("sp", [], self.b_x[4 * g:4 * g + 4], self.xT[:, 4 * g:4 * g + 4, :], xsrc[:, 4 * g:4 * g + 4, :])
        S.op("dve", [], [self.b_ones], lambda e: e.memset(self.ones[:, :], 1.0))
        S.op("act", [self.b_prm], [self.b_cond],
             lambda e: e.activation(self.condT[:, :], self.P("cond", 16), AF.Silu))
        self.lamtmp = self.aalloc("lamtmp", [128, 128], F32)
        for i in range(2):
            S.op("dve", [self.b_prm], [self.b_lam],
                 lambda e: e.tensor_tensor(self.lamtmp[:, :], self.P("lam", 128, 256 * i),
                                           self.P("lam", 128, 256 * i + 128), ALU.mult))
            S.op("dve", [self.b_lam], [self.b_lam],
                 lambda e: e.reduce_sum(self.lamt[:, i:i + 1], self.lamtmp[:, :], AX.X))
        S.op("act", [self.b_lam], [self.b_lam],
             lambda e: e.activation(self.lamt[:, 2:4], self.lamt[:, 0:2], AF.Exp))
        S.op("dve", [self.b_lam], [self.b_lam],
             lambda e: e.tensor_tensor(self.lamt[:, 4:5], self.lamt[:, 3:4], self.lamt[:, 2:3], ALU.subtract))
        S.op("dve", [self.b_lam], [self.b_lam],
             lambda e: e.tensor_scalar(self.lamt[:, 5:6], self.lamt[:, 4:5], -LAMBDA_INIT, None, ALU.add))
        S.op("dve", [self.b_prm], [self.b_lam],
             lambda e: e.tensor_scalar(self.lamt[:, 6:8], self.P("gsub", 2), 1.0 - LAMBDA_INIT, None, ALU.mult))
        for l in range(2):
            for i, tap in enumerate((0, 2)):
                S.op("dve", [self.b_prm], [self.b_fix],
                     lambda e: e.tensor_scalar(self.fix[:, l, i, :], self.P("wconv", 88, (l * 3 + tap) * 88),
                                               self.P("cflag", 1), None, ALU.mult))
        self.ada_to(16)

        self.pre_normed = set()
        self.hoist = (self.stop >= 99)
        phases = [self.layer0_gqa, self.layer0_mla, lambda: self.ffn(0), self.layer1_attn,
                  lambda: self.ffn(1), lambda: (None if self.hoist else self.final_norm())]
        for i, ph in enumerate(phases):
            if i < self.stop:
                ph()
        if self.dbg:
            S.barrier()
            S.dma("sp", [], [], d["dbg"][:, 0:192], self.modT[:, :, :].rearrange("p l c -> p (l c)"))
            S.dma("sp", [], [], d["dbg"][:, 192:200], self.lamt[:, :])
            S.dma("sp", [], [], d["dbg"][:, 256:256 + 352], self.fix[:, :, :, :].rearrange("p a b c -> p (a b c)"))
            for kc in range(4):
                S.dma("sp", [], [], d["dbg"][:, 1024 + kc * 512:1024 + (kc + 1) * 512], self.xT[:, kc, 0:512])
        S.finish()

    def common_attn_arena(self):
        self.PT = [self.aalloc("pt%d" % i, [128, NKB, QG], BF16) for i in range(2)]
        self.b_pt = [[Buf("pt%d_%d" % (i, k)) for k in range(NKB)] for i in range(2)]

    def load_rope(self, which):
        S = self.S
        if which == 128:
            S.dma("sp", [], [self.b_rope], self.rope[:, :, :], self.d["rope128"].rearrange("c p t -> p c t"))
        else:
            S.dma("sp", [], [self.b_rope], self.rope[0:64, :, :], self.d["rope64"].rearrange("c p t -> p c t"))

    def out_fm(self, dram_rows_ap, src, src_bufs):
        self.S.dma("sp", src_bufs, [], dram_rows_ap, src)

    def layer0_gqa(self):
        S = self.S
        ps = self.ps
        d = self.d
        self.arena_reset()
        self.common_attn_arena()
        kbT = self.aalloc("kbT", [128, 2, NKEY], BF16)
        vb = self.aalloc("vb", [128, NKB, 256], BF16)
        F = [self.aalloc("F%d" % i, [128, T], F32) for i in range(5)]
        bF = [Buf("F%d" % i) for i in range(5)]
        qb = [self.aalloc("qb%d" % i, [128, T], BF16) for i in range(2)]
        b_qb = [Buf("qb%d" % i) for i in range(2)]
        sq = [self.aalloc("sq0", [128, T], BF16)]
        b_sq = [Buf("sq0")]
        ao = self.aalloc("ao", [128, 4, T], BF16)
        b_ao = [Buf("ao%d" % i) for i in range(4)]
        self.rD = [self.aalloc("rD%d" % i, [128, QG], F32) for i in range(2)]
        self.b_rD = [Buf("rD%d" % i) for i in range(2)]
        b_kb = [Buf("kb0"), Buf("kb1")]
        b_vb = [Buf("vb%d" % i) for i in range(NKB)]

        import os
        skip = os.environ.get("DBG_SKIP", "")
        if "rope" not in skip:
            self.load_rope(128)
        if "kv" not in skip:
            for g in range(2):
                S.dma("pool", [], [b_kb[g]], kbT[:, g, T:NKEY], d["gkT"][g])
            S.dma("pool", [], b_vb[8:12], vb[:, 8:12, :], d["gv"].rearrange("(b p) n -> p b n", p=128))

        if self.sub == 0:
            return
        self.norm_mod(0, 0, F[0], bF[0], [F[1], F[2]], [bF[1], bF[2]], sq + qb, b_sq + b_qb)
        rstd, b_rstd = F[0], bF[0]
        if self.sub == 1:
            return

        def normrope_head(col, pb, gname, gsname, dst_bf, dst_bufs, out_rows=None):
            c2, pb2 = self.psB(2)
            self.sumsq(lambda c: ps[:, col:col + T], lambda c: pb, 1, sq, b_sq, c2, pb2)
            self.rstd_from_ss(c2, pb2, 128, F[3][:, :], [bF[3]])
            self.rope_chunk(col, pb, 128, self.P(gname), self.P(gsname), F[1], bF[1], F[2], bF[2])
            if out_rows is None:
                S.op("dve", [bF[1], bF[3]], dst_bufs,
                     lambda e: e.tensor_tensor(dst_bf, F[1][:, :], F[3][:, :], ALU.mult))
            else:
                S.op("dve", [bF[1], bF[3]], [bF[4]],
                     lambda e: e.tensor_tensor(F[4][:, :], F[1][:, :], F[3][:, :], ALU.mult))
                self.out_fm(out_rows, F[4][:, :], [bF[4]])
                S.op("act", [bF[4]], dst_bufs, lambda e: e.activation(dst_bf, F[4][:, :], AF.Copy))

        w, bw = self.get_w("w_in_ab", None, 0, KC, 2368, 256)
        for g in range(2):
            col, pb = self.lin_fm(w, bw, KC, g * 128, 128, self.h_rhs, self.h_bufs)
            normrope_head(col, pb, "ggk", "ggks", kbT[:, g, 0:T], [b_kb[g]],
                          out_rows=d["o_gkT"][g * 128:(g + 1) * 128, :])
        self.ada_more(2)
        if self.sub == 2:
            return
        w, bw = self.get_w("w_in_ab", None, 0, KC, 2624, 256)
        for blk in range(8):
            col, pb = self.lin_tm(w, bw, 0, 256, blk)
            f = F[4] if blk % 2 == 0 else F[2]
            bf = bF[4] if blk % 2 == 0 else bF[2]
            S.op("act", pb, [bf], lambda e: e.activation(f[:, 0:256], ps[:, col:col + 256], AF.Copy))
            S.dma("sp", [bf], [], d["o_gv"][blk * 128:(blk + 1) * 128, :], f[:, 0:256])
            S.op("dve", pb, [b_vb[blk]], lambda e: e.tensor_copy(vb[:, blk, :], ps[:, col:col + 256]))
        self.ada_more(2)
        if self.sub == 3:
            return
        st = {}

        def stage_a(hh):
            g, half, j = hh // 4, (hh % 4) // 2, hh % 2
            if j == 0:
                st["w"] = self.get_w("w_in_ab", None, 0, KC, 1344 + g * 512 + half * 256, 256, hold=True)
                st["wi"] = self.last_w
            w, bw = st["w"]
            col, pb = self.lin_fm(w, bw, KC, j * 128, 128, self.h_rhs, self.h_bufs)
            normrope_head(col, pb, "ggq", "ggqs", qb[hh % 2][:, :], [b_qb[hh % 2]])
            if j == 1:
                self.rel_w(st["wi"])
                self.ada_more(2)

        def stage_b(hh):
            g, h4 = hh // 4, hh % 4
            q = qb[hh % 2]
            bq = b_qb[hh % 2]
            units = []
            for qg in range(NQG):
                units.append(dict(
                    parts=[(lambda kb, g=g: kbT[:, g, kb * 128:(kb + 1) * 128],
                            q[:, qg * QG:(qg + 1) * QG],
                            lambda kb, g=g, bq=bq: [b_kb[g], bq])],
                    scale=HEAD_SCALE, qg=qg,
                    v_fn=lambda kb, c, g=g: vb[:, kb, g * 128:(g + 1) * 128],
                    v_reads=lambda kb: [b_vb[kb]], ndv=1,
                    fin=lambda col, pb, qg=qg, h4=h4: self.softmax_fin(
                        col, pb, 1, [ao[:, h4, qg * QG:(qg + 1) * QG]], [b_ao[h4]])))
            self.run_units(units)
            if h4 == 3:
                self.ada_to(24)
                wl = []
                for i in range(2):
                    wl.append(self.get_w("w_out_ab", None, 1024 + g * 512 + i * 256, 2, 0, D, hold=(i == 0)))
                    if i == 0:
                        w0i = self.last_w
                self.resid_partial(lambda k: (wl[k // 2][0], k % 2, wl[k // 2][1]), 4,
                                   lambda k, tg: ao[:, k, tg * 512:(tg + 1) * 512], lambda k: [b_ao[k]], 0, 2)
                self.rel_w(w0i)

        stage_a(0)
        for hh in range(8):
            if hh + 1 < 8:
                stage_a(hh + 1)
            stage_b(hh)
        self.ada_more(2)

    def layer0_mla(self):
        S = self.S
        ps = self.ps
        d = self.d
        self.arena_reset()
        self.common_attn_arena()
        ckvT = self.aalloc("ckvT", [128, 4, NKEY], BF16)
        kropeT = self.aalloc("kropeT", [128, NKEY], BF16)
        qlat = self.aalloc("qlat", [128, 6, T], BF16)
        F = [self.aalloc("F%d" % i, [128, T], F32) for i in range(5)]
        bF = [Buf("F%d" % i) for i in range(5)]
        sq = [self.aalloc("sq0", [128, T], BF16)]
        b_sq = [Buf("sq0")]
        self.rD = [self.aalloc("rD%d" % i, [128, QG], F32) for i in range(2)]
        self.b_rD = [Buf("rD%d" % i) for i in range(2)]
        b_ckv = [Buf("ckv%d" % i) for i in range(4)]
        b_ckvc = Buf("ckvc")
        b_krope = Buf("krope")
        b_kropec = Buf("kropec")
        b_qlat = [Buf("qlat%d" % i) for i in range(6)]

        for i, (a, b) in enumerate(((0, 0), (0, 4), (0, 8), (1, 0))):
            sq.append(self.PT[a][:, b:b + 4, :].rearrange("p a b -> p (a b)"))
            b_sq.append(self.b_pt[a][b:b + 4])
        self.load_rope(64)
        S.dma("pool", [], [b_ckvc], ckvT[:, :, T:NKEY], d["ckvT"].rearrange("(k p) s -> p k s", p=128))
        S.dma("pool", [], [b_kropec], kropeT[0:64, T:NKEY], d["kropeT"])

        c2, pb2 = self.psB(2)
        for i in range(2):
            w, bw = self.get_w("w_in_ab", None, 0, KC, 768 + i * 256, 256)
            for j in range(2):
                c = i * 2 + j
                col, pb = self.lin_fm(w, bw, KC, j * 128, 128, self.h_rhs, self.h_bufs)
                S.op("act", pb + [self.b_prm], [bF[c]],
                     lambda e: e.activation(F[c][:, :], ps[:, col:col + T], AF.Identity, scale=self.P("gkv", 1, c)))
                self.sumsq(lambda _c: ps[:, col:col + T], lambda _c: pb, 1, sq, b_sq, c2, pb2, base=c, total=4)
            self.ada_more(2)
        self.rstd_from_ss(c2, pb2, 512, F[4][:, :], [bF[4]])
        for c in range(4):
            S.op("dve", [bF[c], bF[4]], [bF[c]],
                 lambda e: e.tensor_tensor(F[c][:, :], F[c][:, :], F[4][:, :], ALU.mult))
            self.out_fm(d["o_ckvT"][c * 128:(c + 1) * 128, :], F[c][:, :], [bF[c]])
            S.op("act", [bF[c]], [b_ckv[c]], lambda e: e.activation(ckvT[:, c, 0:T], F[c][:, :], AF.Copy))
        w, bw = self.get_w("w_in_ab", None, 0, KC, 1280, 64)
        col, pb = self.lin_fm(w, bw, KC, 0, 64, self.h_rhs, self.h_bufs)
        self.rope_chunk(col, pb, 64, 1.0, 1.0, F[4], bF[4], F[0], bF[0])
        self.out_fm(d["o_kropeT"][:, :], F[4][0:64, :], [bF[4]])
        S.op("act", [bF[4]], [b_krope], lambda e: e.activation(kropeT[0:64, 0:T], F[4][0:64, :], AF.Copy))
        self.ada_more(2)
        c2, pb2 = self.psB(2)
        for i in range(3):
            w, bw = self.get_w("w_in_ab", None, 0, KC, i * 256, 256)
            for j in range(2):
                c = i * 2 + j
                col, pb = self.lin_fm(w, bw, KC, j * 128, 128, self.h_rhs, self.h_bufs)
                S.op("dve", pb + [self.b_prm], [b_qlat[c]],
                     lambda e: e.tensor_scalar(qlat[:, c, :], ps[:, col:col + T], self.P("gq", 1, c), None, ALU.mult))
                self.sumsq(lambda _c: ps[:, col:col + T], lambda _c: pb, 1, sq, b_sq, c2, pb2, base=c, total=6)
            self.ada_more(2)
        rq, b_rq = F[3], bF[3]
        self.rstd_from_ss(c2, pb2, 768, rq[:, :], [b_rq])
        hT = self.hT
        bh = self.b_h
        qn = [hT[:, 0, :], hT[:, 1, :]]
        b_qn = [bh[0], bh[1]]
        qr = [hT[0:64, 2, :], hT[0:64, 3, :]]
        b_qr = [bh[2], bh[3]]
        hflat = hT[:, :, :].rearrange("p k t -> p (k t)")
        kn = [hflat[:, 4 * T:4 * T + NKEY], hflat[:, 6 * T:6 * T + NKEY]]
        b_kn = [[bh[4], bh[5]], [bh[6], bh[7]]]
        vh = [hflat[:, 8 * T:8 * T + NKEY].rearrange("p (b n) -> p b n", b=NKB),
              hflat[:, 10 * T:10 * T + NKEY].rearrange("p (b n) -> p b n", b=NKB)]
        b_vh = [[bh[8], bh[9]], [bh[10], bh[11]]]
        ao = hT[:, 12:16, :]
        b_ao = bh[12:16]
        def stage_a(h):
            i2 = h % 2
            wq, bwq = self.get_w("w_q_up", None, 0, 6, h * 192, 192, hold=True)
            wqi = self.last_w
            wkv, bwkv = self.get_w("w_kv_up", None, 0, 4, h * 256, 256)
            for kg in range(3):
                col, pb = self.psA(1)
                for k in range(4):
                    self.mm(ps[:, col:col + 512], wkv[:, k, 0:128], ckvT[:, k, kg * 512:(kg + 1) * 512],
                            k == 0, k == 3, [bwkv, b_ckvc] + b_ckv, pb)
                S.op("act", pb, b_kn[i2],
                     lambda e: e.activation(kn[i2][:, kg * 512:(kg + 1) * 512], ps[:, col:col + 512], AF.Copy))
            for kq in range(3):
                col, pb = self.psA(1)
                for b4 in range(4):
                    kb = kq * 4 + b4
                    for k in range(4):
                        self.mm(ps[:, col + b4 * 128:col + (b4 + 1) * 128], ckvT[:, k, kb * 128:(kb + 1) * 128],
                                wkv[:, k, 128:256], k == 0, k == 3,
                                [bwkv, b_ckvc] + b_ckv, pb)
                S.op("dve", pb, b_vh[i2],
                     lambda e: e.tensor_copy(vh[i2][:, kq * 4:(kq + 1) * 4, :],
                                             ps[:, col:col + 512].rearrange("p (b n) -> p b n", b=4)))
            col, pb = self.lin_fm(wq, bwq, 6, 0, 128, lambda k, tg: qlat[:, k, tg * 512:(tg + 1) * 512],
                                  lambda k: [b_qlat[k]])
            S.op("dve", pb + [b_rq], [b_qn[i2]],
                 lambda e: e.tensor_tensor(qn[i2], ps[:, col:col + T], rq[:, :], ALU.mult))
            col, pb = self.lin_fm(wq, bwq, 6, 128, 64, lambda k, tg: qlat[:, k, tg * 512:(tg + 1) * 512],
                                  lambda k: [b_qlat[k]])
            self.rel_w(wqi)
            self.rope_chunk(col, pb, 64, 1.0, 1.0, F[0], bF[0], F[1], bF[1])
            S.op("dve", [bF[0], b_rq], [b_qr[i2]],
                 lambda e: e.tensor_tensor(qr[i2], F[0][0:64, :], rq[0:64, :], ALU.mult))
            self.ada_more(2)

        def stage_b(h):
            i2 = h % 2
            hk = h % 4
            units = []
            for qg in range(NQG):
                units.append(dict(
                    parts=[(lambda kb, i2=i2: kn[i2][:, kb * 128:(kb + 1) * 128],
                            qn[i2][:, qg * QG:(qg + 1) * QG],
                            lambda kb, i2=i2: b_kn[i2] + [b_qn[i2]]),
                           (lambda kb: kropeT[0:64, kb * 128:(kb + 1) * 128],
                            qr[i2][:, qg * QG:(qg + 1) * QG],
                            lambda kb, i2=i2: [b_krope, b_kropec, b_qr[i2]])],
                    scale=MLA_SCALE, qg=qg,
                    v_fn=lambda kb, c, i2=i2: vh[i2][:, kb, :],
                    v_reads=lambda kb, i2=i2: b_vh[i2], ndv=1,
                    fin=lambda col, pb, qg=qg, hk=hk: self.softmax_fin(
                        col, pb, 1, [ao[:, hk, qg * QG:(qg + 1) * QG]], [b_ao[hk]])))
            self.run_units(units)
            if hk == 3:
                g = h // 4
                wl = []
                for i in range(2):
                    wl.append(self.get_w("w_out_ab", None, g * 512 + i * 256, 2, 0, D, hold=(i == 0)))
                    if i == 0:
                        w0i = self.last_w
                self.resid_partial(lambda k: (wl[k // 2][0], k % 2, wl[k // 2][1]), 4,
                                   lambda k, tg: ao[:, k, tg * 512:(tg + 1) * 512], lambda k: [b_ao[k]], 0, 2)
                self.rel_w(w0i)

        stage_a(0)
        for h in range(8):
            if h + 1 < 8:
                stage_a(h + 1)
            stage_b(h)
        self.ada_more(2)
        if self.hoist:
            self.norm_mod(0, 1, F[0], bF[0], [F[1], F[2]], [bF[1], bF[2]],
                          sq + [qlat[:, 0, :], qlat[:, 1, :], qlat[:, 2, :]], b_sq + b_qlat[0:3])
            self.pre_normed.add(("ffn", 0))

    def ffn(self, l):
        S = self.S
        ps = self.ps
        self.arena_reset()
        F = [self.aalloc("F%d" % i, [128, T], F32) for i in range(8)]
        bF = [Buf("F%d" % i) for i in range(8)]
        act = [self.aalloc("act%d" % i, [128, 4, T], BF16) for i in range(2)]
        b_act = [[Buf("act%d_%d" % (i, j)) for j in range(4)] for i in range(2)]
        sq = [self.aalloc("sq%d" % i, [128, T], BF16) for i in range(2)]
        b_sq = [Buf("sq0"), Buf("sq1")]
        if ("ffn", l) not in self.pre_normed:
            self.norm_mod(l, 1, F[0], bF[0], [F[1], F[2]], [bF[1], bF[2]], sq, b_sq)
        self.ada_to(l * 48 + 48)

        def conv_chunk(col, pb, fc, dst, bdst):
            w0 = self.P("wconv", 1, (l * 3 + 0) * 88 + fc)
            w1 = self.P("wconv", 1, (l * 3 + 1) * 88 + fc)
            w2 = self.P("wconv", 1, (l * 3 + 2) * 88 + fc)
            bb = self.P("bconv", 1, l * 88 + fc)
            S.op("act", pb + [self.b_prm], [bdst],
                 lambda e: e.activation(dst[:, :], ps[:, col:col + T], AF.Identity, bias=bb, scale=w1))
            S.op("dve", pb + [self.b_prm, bdst], [bdst],
                 lambda e: e.scalar_tensor_tensor(dst[:, 1:T], ps[:, col:col + T - 1], w0, dst[:, 1:T],
                                                  ALU.mult, ALU.add))
            S.op("dve", pb + [self.b_prm, bdst], [bdst],
                 lambda e: e.scalar_tensor_tensor(dst[:, 0:T - 1], ps[:, col + 1:col + T], w2, dst[:, 0:T - 1],
                                                  ALU.mult, ALU.add))
            f0 = self.fix[:, l, 0, fc:fc + 1]
            f2 = self.fix[:, l, 1, fc:fc + 1]
            S.op("dve", pb + [self.b_fix, bdst], [bdst],
                 lambda e: e.scalar_tensor_tensor(dst[:, 256:T:256], ps[:, col + 255:col + T - 1:256], f0,
                                                  dst[:, 256:T:256], ALU.mult, ALU.add))
            S.op("dve", pb + [self.b_fix, bdst], [bdst],
                 lambda e: e.scalar_tensor_tensor(dst[:, 255:T - 1:256], ps[:, col + 256:col + T:256], f2,
                                                  dst[:, 255:T - 1:256], ALU.mult, ALU.add))

        fic = [0]

        def up(g):
            ai = g % 2
            for half in range(2):
                wg, bwg = self.get_w("w_up", l, 0, KC, g * 512 + half * 256, 256)
                sg = []
                for j in range(2):
                    fc = g * 4 + half * 2 + j
                    col, pb = self.lin_fm(wg, bwg, KC, j * 128, 128, self.h_rhs, self.h_bufs)
                    f, bf = F[fic[0] % 8], bF[fic[0] % 8]
                    fic[0] += 1
                    conv_chunk(col, pb, fc, f, bf)
                    S.op("act", [bf], [bf], lambda e: e.activation(f[:, :], f[:, :], AF.Silu))
                    sg.append((f, bf))
                if l == 0 and half == 0:
                    self.ada_more(1)
                wv, bwv = self.get_w("w_up", l, 0, KC, DFF + g * 512 + half * 256, 256)
                for j in range(2):
                    fc = g * 4 + half * 2 + j
                    col, pb = self.lin_fm(wv, bwv, KC, j * 128, 128, self.h_rhs, self.h_bufs)
                    f, bf = F[fic[0] % 8], bF[fic[0] % 8]
                    fic[0] += 1
                    conv_chunk(col, pb, 44 + fc, f, bf)
                    a = act[ai][:, half * 2 + j, :]
                    S.op("dve", [bf, sg[j][1]], [b_act[ai][half * 2 + j]],
                         lambda e: e.tensor_tensor(a, f[:, :], sg[j][0][:, :], ALU.mult))
                if l == 0 and half == 1:
                    self.ada_more(1)

        def down(g):
            ai = g % 2
            wl = []
            for i in range(2):
                wl.append(self.get_w("w_down", l, g * 512 + i * 256, 2, 0, D, hold=(i == 0)))
                if i == 0:
                    w0i = self.last_w
            self.resid_partial(lambda k: (wl[k // 2][0], k % 2, wl[k // 2][1]), 4,
                               lambda k, tg: act[ai][:, k, tg * 512:(tg + 1) * 512],
                               lambda k: [b_act[ai][k]], l, 5, alt=True)
            self.rel_w(w0i)
            if l == 0:
                self.ada_more(1)

        up(0)
        for g in range(11):
            if g + 1 < 11:
                up(g + 1)
            down(g)
        if l == 0:
            self.ada_to(96)
        if self.hoist:
            if l == 0:
                self.norm_mod(1, 0, F[0], bF[0], [F[1], F[2]], [bF[1], bF[2]], sq, b_sq)
                self.pre_normed.add(("l1", 0))
            else:
                self.final_norm(F[0:5], bF[0:5], sq, b_sq)

    def layer1_attn(self):
        S = self.S
        ps = self.ps
        d = self.d
        self.arena_reset()
        self.common_attn_arena()
        KhT = self.aalloc("KhT", [128, 2, NKEY], BF16)
        Vh = self.aalloc("Vh", [128, NKB, 256], BF16)
        QhT = [self.aalloc("QhT%d" % i, [128, 2, T], BF16) for i in range(2)]
        F = [self.aalloc("F%d" % i, [128, T], F32) for i in range(5)]
        bF = [Buf("F%d" % i) for i in range(5)]
        vt = [self.aalloc("vt%d" % i, [128, 256], F32) for i in range(2)]
        b_vt = [Buf("vt0"), Buf("vt1")]
        sq = [self.aalloc("sq0", [128, T], BF16)]
        b_sq = [Buf("sq0")]
        ao = self.aalloc("ao", [128, 2, T], BF16)
        b_ao = [Buf("ao0"), Buf("ao1")]
        self.rD = [self.aalloc("rD%d" % i, [128, QG], F32) for i in range(3)]
        self.b_rD = [Buf("rD%d" % i) for i in range(3)]
        b_k = [Buf("k0"), Buf("k1")]
        b_kc = [Buf("kc0"), Buf("kc1")]
        b_v = [Buf("v%d" % i) for i in range(NKB)]
        b_q = [[Buf("q0_%d" % i), Buf("q1_%d" % i)] for i in range(2)]
        self.load_rope(128)
        if ("l1", 0) not in self.pre_normed:
            self.norm_mod(1, 0, F[0], bF[0], [F[1], F[2]], [bF[1], bF[2]],
                          sq + [QhT[0][:, 0, :], QhT[0][:, 1, :], QhT[1][:, 0, :]],
                          b_sq + [b_q[0][0], b_q[0][1], b_q[1][0]])
        self.ada_to(48 + 48)
        neglam = self.lamt[:, 5:6]
        oraw = [F[3], F[4]]
        b_oraw = [bF[3], bF[4]]

        def stage_q(h):
            qi = h % 2
            w, bw = self.get_w("w_in_c", None, 0, KC, h * 256, 256)
            for j in range(2):
                col, pb = self.lin_fm(w, bw, KC, j * 128, 128, self.h_rhs, self.h_bufs)
                self.rope_chunk(col, pb, 128, 1.0, 1.0, F[1], bF[1], F[2], bF[2],
                                dst=QhT[qi][:, j, :], dst_bufs=[b_q[qi][j]])

        def stage_k(h):
            for j in range(2):
                S.dma("pool", [], [b_kc[j]], KhT[:, j, T:NKEY], d["dkT"][h * 2 + j])
            w, bw = self.get_w("w_in_c", None, 0, KC, 2048 + h * 256, 256)
            for j in range(2):
                col, pb = self.lin_fm(w, bw, KC, j * 128, 128, self.h_rhs, self.h_bufs)
                self.rope_chunk(col, pb, 128, 1.0, 1.0, F[1], bF[1], F[2], bF[2])
                self.out_fm(d["o_dkT"][(h * 2 + j) * 128:(h * 2 + j + 1) * 128, :], F[1][:, :], [bF[1]])
                S.op("act", [bF[1]], [b_k[j]], lambda e: e.activation(KhT[:, j, 0:T], F[1][:, :], AF.Copy))

        def stage_v(h):
            S.dma("pool", [], b_v[8:12], Vh[:, 8:12, :],
                  d["dv"][:, h * 256:(h + 1) * 256].rearrange("(b p) n -> p b n", p=128))
            w, bw = self.get_w("w_in_c", None, 0, KC, 4096 + h * 256, 256)
            for blk in range(8):
                col, pb = self.lin_tm(w, bw, 0, 256, blk)
                f, bf = vt[blk % 2], b_vt[blk % 2]
                S.op("act", pb, [bf], lambda e: e.activation(f[:, :], ps[:, col:col + 256], AF.Copy))
                S.dma("sp", [bf], [], d["o_dv"][blk * 128:(blk + 1) * 128, h * 256:(h + 1) * 256], f[:, :])
                S.op("dve", pb, [b_v[blk]], lambda e: e.tensor_copy(Vh[:, blk, :], ps[:, col:col + 256]))

        def stage_att(h):
            qi = h % 2
            units = []
            for qg in range(NQG):
                for j in range(2):
                    def fin(col, pb, qg=qg, j=j):
                        r = self.rD[j]
                        br = self.b_rD[j]
                        S.op("dve", [pb[1]], [br], lambda e: e.reciprocal(r[:, :], ps[:, col + 2 * QG:col + 3 * QG]))
                        if j == 1:
                            S.op("dve", [br, self.b_lam], [br],
                                 lambda e: e.tensor_scalar(r[:, :], r[:, :], neglam, None, ALU.mult))
                        for c in range(2):
                            o = oraw[c][:, qg * QG:(qg + 1) * QG]
                            if j == 0:
                                S.op("dve", [pb[0], br], [b_oraw[c]],
                                     lambda e: e.tensor_tensor(o, ps[:, col + c * QG:col + (c + 1) * QG], r[:, :], ALU.mult))
                            else:
                                t2 = self.rD[2]
                                S.op("dve", [pb[0], br], [self.b_rD[2]],
                                     lambda e: e.tensor_tensor(t2[:, :], ps[:, col + c * QG:col + (c + 1) * QG], r[:, :], ALU.mult))
                                S.op("dve", [self.b_rD[2], b_oraw[c]], [b_oraw[c]],
                                     lambda e: e.tensor_tensor(o, o, t2[:, :], ALU.add))
                    units.append(dict(
                        parts=[(lambda kb, j=j: KhT[:, j, kb * 128:(kb + 1) * 128],
                                QhT[qi][:, j, qg * QG:(qg + 1) * QG],
                                lambda kb, j=j: [b_k[j], b_kc[j], b_q[qi][j]])],
                        scale=HEAD_SCALE, qg=qg,
                        v_fn=lambda kb, c: Vh[:, kb, c * 128:(c + 1) * 128],
                        v_reads=lambda kb: [b_v[kb]], ndv=2, fin=fin))
            self.run_units(units)

        def stage_out_a(h):
            c2, pb2 = self.psB(2)
            self.sumsq(lambda c: oraw[c][:, :], lambda c: [b_oraw[c]], 2, sq, b_sq, c2, pb2)
            self.rstd_from_ss(c2, pb2, 256, F[0][:, :], [bF[0]])
            for c in range(2):
                S.op("dve", [b_oraw[c], self.b_lam, bF[0]], [b_ao[c]],
                     lambda e: e.scalar_tensor_tensor(ao[:, c, :], oraw[c][:, :], self.lamt[:, 6 + c:7 + c],
                                                      F[0][:, :], ALU.mult, ALU.mult))

        def stage_out_b(h):
            wo, bwo = self.get_w("w_out_c", None, h * 256, 2, 0, D)
            self.resid_partial(lambda k: (wo, k, bwo), 2,
                               lambda k, tg: ao[:, k, tg * 512:(tg + 1) * 512], lambda k: [b_ao[k]], 1, 2, alt=True)

        stage_q(0)
        stage_k(0)
        stage_v(0)
        for h in range(8):
            stage_att(h)
            if h + 1 < 8:
                stage_q(h + 1)
            stage_out_a(h)
            if h + 1 < 8:
                stage_k(h + 1)
            stage_out_b(h)
            if h + 1 < 8:
                stage_v(h + 1)
        if self.hoist:
            self.norm_mod(1, 1, F[0], bF[0], [F[1], F[2]], [bF[1], bF[2]],
                          sq + [QhT[0][:, 0, :], QhT[0][:, 1, :], QhT[1][:, 0, :]],
                          b_sq + [b_q[0][0], b_q[0][1], b_q[1][0]])
            self.pre_normed.add(("ffn", 1))

    def final_norm(self, F=None, bF=None, sq=None, b_sq=None):
        S = self.S
        d = self.d
        if F is None:
            self.arena_reset()
            F = [self.aalloc("F%d" % i, [128, T], F32) for i in range(5)]
            bF = [Buf("F%d" % i) for i in range(5)]
            sq = [self.aalloc("sq%d" % i, [128, T], BF16) for i in range(2)]
            b_sq = [Buf("sq0"), Buf("sq1")]
        col, pb = self.psB(2)
        self.sumsq(lambda c: self.xT[:, c, :], lambda c: [self.b_x[c]], KC, sq, b_sq, col, pb, dve_alt=True)
        self.rstd_from_ss(col, pb, D, F[0][:, :], [bF[0]])
        for kc in range(KC):
            f, bf = F[1 + kc % 4], bF[1 + kc % 4]
            S.op("dve", [self.b_x[kc], self.b_prm, bF[0]], [bf],
                 lambda e: e.scalar_tensor_tensor(f[:, :], self.xT[:, kc, :], self.P("gfin", 1, kc),
                                                  F[0][:, :], ALU.mult, ALU.mult))
            S.dma("sp", [bf], [], d["yT"][kc * 128:(kc + 1) * 128, :], f[:, :])

    def build(self):
        self.S.dry = True
        self.runid = 0
        self.emit()
        self.runid = 1
        self.wplan = self.wplan_new
        self.S.dry = False
        self.S.reset()
        self.emit()
        return self.nc


def _rope_tables(rot_dim, n_tokens=T, grid_w=64):
    n_rows = n_tokens // grid_w
    row = np.repeat(np.arange(n_rows), grid_w).astype(np.float32)
    col = np.tile(np.arange(grid_w), n_rows).astype(np.float32)
    n_freq = rot_dim // 4
    freqs = (10000.0 ** (-np.arange(n_freq, dtype=np.float32) / n_freq)).astype(np.float32)
    ang = np.concatenate([row[:, None] * freqs, col[:, None] * freqs], axis=-1)
    cos, sin = np.cos(ang).astype(np.float32), np.sin(ang).astype(np.float32)
    C = np.concatenate([cos, cos], axis=1).T
    SN = np.concatenate([-sin, sin], axis=1).T
    return np.ascontiguousarray(np.stack([C, SN]).astype(np.float32))


_PROG = None


def make_in_maps(g):
    f32 = np.float32

    def swap(v):
        v = np.asarray(v, f32)
        h = v.shape[-1] // 2
        return np.concatenate([v[..., h:], v[..., :h]], axis=-1)

    shared = {
        "w_ada": np.ascontiguousarray(g["w_ada"], f32),
        "w_in_ab": np.ascontiguousarray(g["w_in_ab"][0], f32),
        "w_q_up": np.ascontiguousarray(g["w_mla_q_up"][0], f32),
        "w_kv_up": np.ascontiguousarray(g["w_mla_kv_up"][0], f32),
        "w_out_ab": np.ascontiguousarray(g["w_out_ab"][0], f32),
        "w_in_c": np.ascontiguousarray(g["w_in_c"][0], f32),
        "w_out_c": np.ascontiguousarray(g["w_out_c"][0], f32),
        "w_up": np.ascontiguousarray(g["w_ffn_up"], f32),
        "w_down": np.ascontiguousarray(g["w_ffn_down"], f32),
    }
    rope128 = _rope_tables(128)
    rope64 = _rope_tables(64)
    id128 = np.stack([np.ones((128, T), f32), np.zeros((128, T), f32)])
    id64 = np.stack([np.ones((64, T), f32), np.zeros((64, T), f32)])

    def prm_for(cond, maskb, cflag):
        parts = {
            "gmix": _fm(g["g_norm_mix"]), "gffn": _fm(g["g_norm_ffn"]), "gfin": _fm(g["g_final"]),
            "bada": _fm(g["b_ada"]), "gq": _fm(g["g_mla_q"][0]), "gkv": _fm(g["g_mla_kv"][0]),
            "ggq": _fm(g["g_gqa_q"][0]), "ggqs": _fm(swap(g["g_gqa_q"][0])),
            "ggk": _fm(g["g_gqa_k"][0]), "ggks": _fm(swap(g["g_gqa_k"][0])),
            "gsub": _fm(g["g_diff_sub"][0]), "wconv": _fm(g["w_ffn_conv"]), "bconv": _fm(g["b_ffn_conv"]),
            "maskb": np.broadcast_to(maskb.reshape(1, 48), (128, 48)),
            "cflag": np.full((128, 1), cflag, f32),
            "lam": np.broadcast_to(np.concatenate([g["lambda_q1"][0], g["lambda_k1"][0],
                                                   g["lambda_q2"][0], g["lambda_k2"][0]]).reshape(1, 512), (128, 512)),
            "cond": _fm(cond),
        }
        out = np.zeros((128, NP_), f32)
        for n, s in PRM_FIELDS:
            a = np.asarray(parts[n], f32)
            assert a.shape == (128, s), (n, a.shape, s)
            out[:, OFF[n]:OFF[n] + s] = a
        return out

    in_maps = []
    for c in range(8):
        m = dict(shared)
        if c < 2:
            b = c
            m["xT"] = np.ascontiguousarray(g["x_sample"][b].T, f32)
            maskb = np.zeros((NQG, NKB), f32)
            m["prm"] = prm_for(g["c"][b], maskb, 0.0)
            m["rope128"], m["rope64"] = rope128, rope64
            m["ckvT"] = np.ascontiguousarray(g["cache_mla_ckv"][b, 0].T, f32)
            m["kropeT"] = np.ascontiguousarray(g["cache_mla_krope"][b, 0].T, f32)
            m["gkT"] = np.ascontiguousarray(g["cache_gqa_k"][b, 0].transpose(1, 2, 0), f32)
            m["gv"] = np.ascontiguousarray(g["cache_gqa_v"][b, 0].reshape(512, 256), f32)
            m["dkT"] = np.ascontiguousarray(g["cache_diff_k"][b, 0].transpose(1, 2, 3, 0).reshape(16, 128, 512), f32)
            m["dv"] = np.ascontiguousarray(g["cache_diff_v"][b, 0].reshape(512, 2048), f32)
        else:
            pc = c - 2 if c < 6 else 0
            xs = g["x_prompt"][4 * pc:4 * pc + 4].reshape(T, D)
            m["xT"] = np.ascontiguousarray(xs.T, f32)
            maskb = np.full((NQG, NKB), NEG, f32)
            for s in range(4):
                maskb[s, 2 * s] = 0.0
                maskb[s, 2 * s + 1] = 0.0
            m["prm"] = prm_for(g["c_ctx"], maskb, -1.0)
            m["rope128"], m["rope64"] = id128, id64
            m["ckvT"] = np.zeros((512, 512), f32)
            m["kropeT"] = np.zeros((64, 512), f32)
            m["gkT"] = np.zeros((2, 128, 512), f32)
            m["gv"] = np.zeros((512, 256), f32)
            m["dkT"] = np.zeros((16, 128, 512), f32)
            m["dv"] = np.zeros((512, 2048), f32)
        in_maps.append(m)
    return in_maps


def kernel(**inp):
    global _PROG
    f32 = np.float32
    g = {k: np.asarray(v) for k, v in inp.items()}
    if _PROG is None:
        _PROG = Prog().build()
    nc = _PROG
    in_maps = make_in_maps(g)
    res = run_bass_kernel_spmd(nc, in_maps, core_ids=list(range(8)))
    R = res.results
    y_sample = np.stack([R[b]["yT"].T for b in range(2)]).astype(f32)
    y_prompt = np.concatenate([R[2 + pc]["yT"].T.reshape(4, 256, D) for pc in range(4)]).astype(f32)

    def gather(fn):
        return np.concatenate([fn(R[2 + pc]) for pc in range(4)]).astype(f32)

    new_ckv = gather(lambda r: r["o_ckvT"].T.reshape(4, 1, 256, 512))
    new_krope = gather(lambda r: r["o_kropeT"].T.reshape(4, 1, 256, 64))
    new_gk = gather(lambda r: r["o_gkT"].reshape(2, 128, 4, 256).transpose(2, 3, 0, 1).reshape(4, 1, 256, 2, 128))
    new_gv = gather(lambda r: r["o_gv"].reshape(4, 1, 256, 2, 128))
    new_dk = gather(lambda r: r["o_dkT"].reshape(8, 2, 128, 4, 256).transpose(3, 4, 0, 1, 2).reshape(4, 1, 256, 8, 2, 128))
    new_dv = gather(lambda r: r["o_dv"].reshape(4, 1, 256, 8, 256))
    return (np.ascontiguousarray(y_prompt), np.ascontiguousarray(y_sample), np.ascontiguousarray(new_ckv),
            np.ascontiguousarray(new_krope), np.ascontiguousarray(new_gk), np.ascontiguousarray(new_gv),
            np.ascontiguousarray(new_dk), np.ascontiguousarray(new_dv))
```
